# Optimizing a Trainium2 kernel written in Bass

```python
import jax, jax.numpy as jnp
from jax import lax
import numpy as np

D_MODEL = 1024
BATCH = 16
SEQ = 256
DEPTH = 4
DEC_BATCH = 4
DEC_SEQ = 4096
PAST_LEN = 256

GRID_W = 64
D_FF = ((8 * D_MODEL // 3 + 255) // 256) * 256
FFN_RESIDUAL = 0.5
N_MOD = 9
A_WIDTH = D_MODEL // 2
POOL_WINDOWS = (2, 4, 8, 16)
POOL_GROUP = A_WIDTH // len(POOL_WINDOWS)
NB_HEAD_DIM = 64
NB_HEADS = (D_MODEL // 2) // NB_HEAD_DIM
NB_ROWS = 8
NB_COLS = 16
C_HEAD_DIM = 64
C_Q_HEADS = D_MODEL // C_HEAD_DIM
C_KV_HEADS = C_Q_HEADS // 4
Q_BLOCK = 128
ROPE_THETA = 10000.0
EPS = 1e-6
NEG_INF = -1e30

kernel_name = "hybrid_dit_pool_natten_gqa_step"


def rms_norm(x, g):
    x32 = x.astype(jnp.float32)
    y = x32 * lax.rsqrt(jnp.mean(x32 * x32, axis=-1, keepdims=True) + EPS)
    return (y * g.astype(jnp.float32)).astype(x.dtype)


def modulate(h, shift, scale):
    return h * (1 + scale) + shift


def ada(cvec, w_mod, b_mod):
    m = jax.nn.silu(cvec) @ w_mod + b_mod
    if m.ndim == 2:
        m = m[:, None, :]
    return jnp.split(m, N_MOD, axis=-1)


def ffn_sub(x, g, shift, scale, gate, w_in, w_out):
    h = modulate(rms_norm(x, g), shift, scale)
    a, b = jnp.split(h @ w_in, 2, axis=-1)
    return x + gate * (FFN_RESIDUAL * ((jax.nn.silu(a) * b) @ w_out))


def head_rms(x, g):
    x32 = x.astype(jnp.float32)
    y = x32 * lax.rsqrt(jnp.mean(x32 * x32, axis=-1, keepdims=True) + EPS)
    return (y * g.astype(jnp.float32)).astype(x.dtype)


def _rotate(xa, pos):
    n = xa.shape[-1] // 2
    inv = ROPE_THETA ** (-jnp.arange(n, dtype=jnp.float32) / n)
    ang = pos[:, None] * inv[None, :]
    cos = jnp.cos(ang)[None, :, None, :]
    sin = jnp.sin(ang)[None, :, None, :]
    x1, x2 = xa[..., :n], xa[..., n:]
    return jnp.concatenate([x1 * cos - x2 * sin, x2 * cos + x1 * sin], axis=-1)


def rope_2d(x):
    L, D = x.shape[1], x.shape[-1]
    t = jnp.arange(L)
    x32 = x.astype(jnp.float32)
    half = D // 2
    xr = _rotate(x32[..., :half], (t // GRID_W).astype(jnp.float32))
    xc = _rotate(x32[..., half:], (t % GRID_W).astype(jnp.float32))
    return jnp.concatenate([xr, xc], axis=-1).astype(x.dtype)


def blocked_attention(q, k, v):
    B, Sq, Hq, D = q.shape
    Hkv = k.shape[2]
    G = Hq // Hkv
    nblk = Sq // Q_BLOCK
    qb = (q * (D ** -0.5)).reshape(B, nblk, Q_BLOCK, Hkv, G, D)
    qb = jnp.moveaxis(qb, 1, 0)

    def blk(qi):
        s = jnp.einsum('bqhgd,bkhd->bhgqk', qi, k).astype(jnp.float32)
        p = jax.nn.softmax(s, axis=-1).astype(v.dtype)
        return jnp.einsum('bhgqk,bkhd->bqhgd', p, v)

    o = lax.map(blk, qb)
    return jnp.moveaxis(o, 0, 1).reshape(B, Sq, Hq, D)


def pool_mix(u, w_pool, pool_scale):
    B, L, _ = u.shape
    u32 = u.astype(jnp.float32)
    t = jnp.arange(L)
    outs = []
    for gi, w in enumerate(POOL_WINDOWS):
        ug = u32[..., gi * POOL_GROUP:(gi + 1) * POOL_GROUP]
        csum = jnp.concatenate([jnp.zeros((B, 1, POOL_GROUP), jnp.float32), jnp.cumsum(ug, axis=1)], axis=1)
        lo = jnp.clip(t - w // 2, 0, L - 1)
        hi = jnp.clip(t + (w - 1 - w // 2), 0, L - 1)
        cnt = (hi - lo + 1).astype(jnp.float32)
        s = jnp.take(csum, hi + 1, axis=1) - jnp.take(csum, lo, axis=1)
        outs.append(s / cnt[None, :, None] - ug)
    pooled = jnp.stack(outs, axis=2).astype(u.dtype)
    mixed = jnp.einsum('blgc,gcd->blgd', pooled, w_pool).reshape(B, L, A_WIDTH)
    return mixed * pool_scale


def neighbourhood_attention(q, k, v, ctx_k, ctx_v, rpb):
    B, L, H, Dh = q.shape
    rows = L // GRID_W
    kh = min(NB_ROWS, rows)
    nb = kh * GRID_W
    qg = (q * (Dh ** -0.5)).reshape(B, rows, GRID_W, H, Dh)
    kg = k.reshape(B, rows, GRID_W, H, Dh)
    vg = v.reshape(B, rows, GRID_W, H, Dh)
    cq = jnp.arange(GRID_W)
    cstart = jnp.clip(cq - NB_COLS // 2, 0, GRID_W - NB_COLS)
    col_valid = (cq[None, :] >= cstart[:, None]) & (cq[None, :] < cstart[:, None] + NB_COLS)
    col_idx = jnp.clip(cq[None, :] - cq[:, None] + NB_COLS - 1, 0, 2 * NB_COLS - 2)

    def row_block(r):
        rs = jnp.clip(r - kh // 2, 0, rows - kh)
        kb = lax.dynamic_slice_in_dim(kg, rs, kh, axis=1)
        vb = lax.dynamic_slice_in_dim(vg, rs, kh, axis=1)
        qr = lax.dynamic_index_in_dim(qg, r, axis=1, keepdims=False)
        s_nb = jnp.einsum('bqhd,bkwhd->bhqkw', qr, kb).astype(jnp.float32)
        dr = rs + jnp.arange(kh) - r + NB_ROWS - 1
        bias = rpb[:, dr[:, None, None], col_idx[None, :, :]]
        bias = jnp.transpose(bias, (0, 2, 1, 3)).astype(jnp.float32)
        s_nb = jnp.where(col_valid[:, None, :], s_nb + bias, NEG_INF)
        s_ctx = jnp.einsum('bqhd,bchd->bhqc', qr, ctx_k).astype(jnp.float32)
        s = jnp.concatenate([s_nb.reshape(B, H, GRID_W, nb), s_ctx], axis=-1)
        p = jax.nn.softmax(s, axis=-1).astype(v.dtype)
        p_nb = p[..., :nb].reshape(B, H, GRID_W, kh, GRID_W)
        return (jnp.einsum('bhqkw,bkwhd->bqhd', p_nb, vb)
                + jnp.einsum('bhqc,bchd->bqhd', p[..., nb:], ctx_v))

    out = lax.map(row_block, jnp.arange(rows))
    return jnp.moveaxis(out, 0, 1).reshape(B, L, H, Dh)


def mixer_ab(h, w_in, w_pool, pool_scale, rpb, w_out, ctx_k=None, ctx_v=None):
    B, L, _ = h.shape
    proj = h @ w_in
    u = proj[..., :A_WIDTH]
    q, k, v = [t.reshape(B, L, NB_HEADS, NB_HEAD_DIM) for t in jnp.split(proj[..., A_WIDTH:], 3, axis=-1)]
    a_out = pool_mix(u, w_pool, pool_scale)
    if ctx_k is None:
        b_out = blocked_attention(q, k, v)
    else:
        b_out = neighbourhood_attention(q, k, v, ctx_k, ctx_v, rpb)
    out = jnp.concatenate([a_out, b_out.reshape(B, L, NB_HEADS * NB_HEAD_DIM)], axis=-1) @ w_out
    return out, k, v


def mixer_c(h, w_qkv, g_q, g_k, w_out, ctx_k=None, ctx_v=None):
    B, L, _ = h.shape
    proj = h @ w_qkv
    nq = C_Q_HEADS * C_HEAD_DIM
    nk = C_KV_HEADS * C_HEAD_DIM
    q = head_rms(proj[..., :nq].reshape(B, L, C_Q_HEADS, C_HEAD_DIM), g_q)
    k = head_rms(proj[..., nq:nq + nk].reshape(B, L, C_KV_HEADS, C_HEAD_DIM), g_k)
    v = proj[..., nq + nk:].reshape(B, L, C_KV_HEADS, C_HEAD_DIM)
    if ctx_k is None:
        o = blocked_attention(q, k, v)
    else:
        o = blocked_attention(rope_2d(q), jnp.concatenate([ctx_k, rope_2d(k)], axis=1),
                              jnp.concatenate([ctx_v, v], axis=1))
    return o.reshape(B, L, nq) @ w_out, k, v


def layer(p, l, x, cvec, ctx_k=None, ctx_v=None):
    sh1, sc1, g1, sh2, sc2, g2, sh3, sc3, g3 = ada(cvec, p['w_mod'][l], p['b_mod'][l])
    x = ffn_sub(x, p['g_norm'][l, 0], sh1, sc1, g1, p['w_ffn_in'][l, 0], p['w_ffn_out'][l, 0])
    h = modulate(rms_norm(x, p['g_norm'][l, 1]), sh2, sc2)
    if l % 2 == 0:
        e = l // 2
        mix, k, v = mixer_ab(h, p['w_in_ab'][e], p['w_pool'][e], p['pool_scale'][e], p['nb_rpb'][e],
                             p['w_out_ab'][e], ctx_k, ctx_v)
    else:
        o = l // 2
        mix, k, v = mixer_c(h, p['w_qkv_c'][o], p['g_qnorm'][o], p['g_knorm'][o], p['w_out_c'][o],
                            ctx_k, ctx_v)
    x = x + g2 * mix
    x = ffn_sub(x, p['g_norm'][l, 2], sh3, sc3, g3, p['w_ffn_in'][l, 1], p['w_ffn_out'][l, 1])
    return x, k, v


def setup_inputs(seed: int = 0) -> dict:
    key = jax.random.key(seed)
    ks = jax.random.split(key, 24)
    f32 = jnp.float32
    n_even = (DEPTH + 1) // 2
    n_odd = DEPTH // 2

    def nrm(k, shape, scale):
        return jax.random.normal(k, shape, f32) * scale

    mix_ab = A_WIDTH + NB_HEADS * NB_HEAD_DIM
    qkv_c = (C_Q_HEADS + 2 * C_KV_HEADS) * C_HEAD_DIM
    return {
        'x_prompt': nrm(ks[0], (BATCH, SEQ, D_MODEL), 1.0),
        'x_sample': nrm(ks[1], (DEC_BATCH, DEC_SEQ, D_MODEL), 1.0),
        'cache_nb_k': nrm(ks[2], (DEC_BATCH, n_even, PAST_LEN, NB_HEADS, NB_HEAD_DIM), 1.0),
        'cache_nb_v': nrm(ks[3], (DEC_BATCH, n_even, PAST_LEN, NB_HEADS, NB_HEAD_DIM), 1.0),
        'cache_attn_k': nrm(ks[4], (DEC_BATCH, n_odd, PAST_LEN, C_KV_HEADS, C_HEAD_DIM), 1.0),
        'cache_attn_v': nrm(ks[5], (DEC_BATCH, n_odd, PAST_LEN, C_KV_HEADS, C_HEAD_DIM), 1.0),
        'c': nrm(ks[6], (DEC_BATCH, D_MODEL), 1.0),
        'c_ctx': nrm(ks[7], (D_MODEL,), 1.0),
        'w_mod': nrm(ks[8], (DEPTH, D_MODEL, N_MOD * D_MODEL), 0.5 * D_MODEL ** -0.5),
        'b_mod': nrm(ks[9], (DEPTH, N_MOD * D_MODEL), 0.01),
        'g_norm': 1.0 + nrm(ks[10], (DEPTH, 3, D_MODEL), 0.1),
        'w_ffn_in': nrm(ks[11], (DEPTH, 2, D_MODEL, 2 * D_FF), D_MODEL ** -0.5),
        'w_ffn_out': nrm(ks[12], (DEPTH, 2, D_FF, D_MODEL), D_FF ** -0.5),
        'w_in_ab': nrm(ks[13], (n_even, D_MODEL, A_WIDTH + 3 * NB_HEADS * NB_HEAD_DIM), D_MODEL ** -0.5),
        'w_pool': nrm(ks[14], (n_even, len(POOL_WINDOWS), POOL_GROUP, POOL_GROUP), POOL_GROUP ** -0.5),
        'pool_scale': 1.0 + nrm(ks[15], (n_even, A_WIDTH), 0.1),
        'nb_rpb': nrm(ks[16], (n_even, NB_HEADS, 2 * NB_ROWS - 1, 2 * NB_COLS - 1), 0.1),
        'w_out_ab': nrm(ks[17], (n_even, mix_ab, D_MODEL), mix_ab ** -0.5),
        'w_qkv_c': nrm(ks[18], (n_odd, D_MODEL, qkv_c), D_MODEL ** -0.5),
        'g_qnorm': 1.0 + nrm(ks[19], (n_odd, C_HEAD_DIM), 0.1),
        'g_knorm': 1.0 + nrm(ks[20], (n_odd, C_HEAD_DIM), 0.1),
        'w_out_c': nrm(ks[21], (n_odd, C_Q_HEADS * C_HEAD_DIM, D_MODEL), (C_Q_HEADS * C_HEAD_DIM) ** -0.5),
        'g_final': 1.0 + nrm(ks[22], (D_MODEL,), 0.1),
    }


def reference(x_prompt, x_sample, cache_nb_k, cache_nb_v, cache_attn_k, cache_attn_v, c, c_ctx,
              w_mod, b_mod, g_norm, w_ffn_in, w_ffn_out, w_in_ab, w_pool, pool_scale, nb_rpb, w_out_ab,
              w_qkv_c, g_qnorm, g_knorm, w_out_c, g_final):
    p = {'w_mod': w_mod, 'b_mod': b_mod, 'g_norm': g_norm, 'w_ffn_in': w_ffn_in, 'w_ffn_out': w_ffn_out,
         'w_in_ab': w_in_ab, 'w_pool': w_pool, 'pool_scale': pool_scale, 'nb_rpb': nb_rpb,
         'w_out_ab': w_out_ab, 'w_qkv_c': w_qkv_c, 'g_qnorm': g_qnorm, 'g_knorm': g_knorm,
         'w_out_c': w_out_c}

    xp = x_prompt
    nb_k, nb_v, at_k, at_v = [], [], [], []
    for l in range(DEPTH):
        xp, k, v = layer(p, l, xp, c_ctx)
        if l % 2 == 0:
            nb_k.append(k)
            nb_v.append(v)
        else:
            at_k.append(k)
            at_v.append(v)
    y_prompt = rms_norm(xp, g_final)

    xs = x_sample
    for l in range(DEPTH):
        if l % 2 == 0:
            ck, cv = cache_nb_k[:, l // 2], cache_nb_v[:, l // 2]
        else:
            ck, cv = cache_attn_k[:, l // 2], cache_attn_v[:, l // 2]
        xs, _, _ = layer(p, l, xs, c, ck, cv)
    y_sample = rms_norm(xs, g_final)

    new_nb_k = jnp.stack(nb_k, axis=1)
    new_nb_v = jnp.stack(nb_v, axis=1)
    new_attn_k = jnp.stack(at_k, axis=1)
    new_attn_v = jnp.stack(at_v, axis=1)
    return (y_prompt, y_sample, new_nb_k, new_nb_v, new_attn_k, new_attn_v)
```

```python
import os
import numpy as np
import ml_dtypes
from contextlib import ExitStack
import concourse.bass as bass
import concourse.mybir as mybir
from concourse.bass_utils import run_bass_kernel_spmd

F32 = mybir.dt.float32
BF16 = mybir.dt.bfloat16
ALU = mybir.AluOpType
AF = mybir.ActivationFunctionType
AX = mybir.AxisListType

D = 1024
NCH = 8
DFF = 2816
NJ = 22
TP = 512
TS = 2048
T = TP + TS
DEPTH = 4
EPS = 1e-6
NEG = -30000.0
PAIRS = [[0, 1], [2, 3], [4, 5], [6, 7]]
TILES = [(i * 512, 512) for i in range(5)]


class Res:
    __slots__ = ("w", "r", "name", "pend")

    def __init__(self, name="", pend=None):
        self.w = None
        self.r = []
        self.name = name
        self.pend = list(pend) if pend else None


class Op:
    __slots__ = ("eng", "fn", "deps", "need", "sem", "val", "is_dma", "idx", "pos")

    def __init__(self, eng, fn, is_dma):
        self.eng = eng
        self.fn = fn
        self.deps = []
        self.need = False
        self.sem = None
        self.val = 0
        self.is_dma = is_dma


class Sched:
    ENGS = ["pe", "act", "dve", "pool", "sp"]
    NDS = 24

    def __init__(self, nc):
        self.nc = nc
        self.ops = {e: [] for e in self.ENGS}
        self.all_dma = []
        self.barrier_op = None
        self.dma_since = []

    def _add(self, op, reads, writes):
        deps = []
        op.pos = len(self.ops[op.eng])
        for r in reads:
            if r.w is not None:
                deps.append(r.w)
            if r.pend:
                deps.extend(r.pend)
        for w in writes:
            if w.w is not None:
                deps.append(w.w)
            deps.extend(w.r)
            if w.pend:
                deps.extend(w.pend)
                w.pend = None
        if self.barrier_op is not None:
            deps.append(self.barrier_op)
        seen = set()
        for d in deps:
            if d is op or id(d) in seen:
                continue
            seen.add(id(d))
            if (not d.is_dma) and (not op.is_dma) and d.eng == op.eng and op.eng == "pe":
                continue
            op.deps.append(d)
            d.need = True
        for r in reads:
            r.r.append(op)
        for w in writes:
            w.w = op
            w.r = []
        self.ops[op.eng].append(op)

    def op(self, eng, fn, reads=(), writes=()):
        o = Op(eng, fn, False)
        self._add(o, reads, writes)
        return o

    def dma(self, q, fn, reads=(), writes=()):
        o = Op(q, fn, True)
        o.need = True
        self._add(o, reads, writes)
        self.all_dma.append(o)
        self.dma_since.append(o)
        return o

    def fence_ops(self, reslist):
        best = {}
        out = []
        seen = set()
        for r in reslist:
            for o in ([r.w] if r.w is not None else []) + list(r.r) + (list(r.pend) if r.pend else []):
                if id(o) in seen:
                    continue
                seen.add(id(o))
                if o.is_dma:
                    out.append(o)
                elif o.eng not in best or o.pos > best[o.eng].pos:
                    best[o.eng] = o
        return out + list(best.values())

    def barrier(self, fn):
        o = Op("dve", fn, False)
        deps = []
        for e in self.ENGS:
            for p in reversed(self.ops[e]):
                if not p.is_dma:
                    deps.append(p)
                    break
        deps.extend(self.dma_since)
        self.dma_since = []
        for d in deps:
            o.deps.append(d)
            d.need = True
        o.need = True
        self.ops["dve"].append(o)
        self.barrier_op = o
        return o

    def check(self):
        done = set()
        ptr = {e: 0 for e in self.ENGS}
        prog = True
        while prog:
            prog = False
            for e in self.ENGS:
                while ptr[e] < len(self.ops[e]):
                    o = self.ops[e][ptr[e]]
                    if all(id(d) in done for d in o.deps):
                        done.add(id(o))
                        ptr[e] += 1
                        prog = True
                    else:
                        break
        stuck = {e: (ptr[e], len(self.ops[e])) for e in self.ENGS if ptr[e] < len(self.ops[e])}
        if stuck:
            for e in stuck:
                o = self.ops[e][ptr[e]]
                print("STUCK", e, ptr[e], "deps not done:", [(d.eng, d.is_dma, self.ops[d.eng].index(d)) for d in o.deps if id(d) not in done])
            raise RuntimeError("deadlock in schedule: %s" % stuck)

    def emit(self, stack):
        nc = self.nc
        self.check()
        esem = {e: stack.enter_context(nc.semaphore("s_" + e)) for e in ["pe", "act", "dve", "pool"]}
        dsem = {q: [stack.enter_context(nc.semaphore("d_%s%d" % (q, i))) for i in range(self.NDS)]
                for q in ["pool", "sp"]}
        for e in self.ENGS:
            cnt = 0
            dcnt = 0
            for o in self.ops[e]:
                if o.is_dma:
                    o.sem = dsem[e][dcnt % self.NDS]
                    o.val = 16 * (dcnt // self.NDS + 1)
                    o.idx = dcnt
                    dcnt += 1
                elif o.need:
                    cnt += 1
                    o.sem = esem[e]
                    o.val = cnt
        engobj = {"pe": nc.tensor, "act": nc.scalar, "dve": nc.vector, "pool": nc.gpsimd, "sp": nc.sync}
        block = stack.enter_context(nc.Block())

        def make(e):
            def body(eng):
                waited = {}
                for o in self.ops[e]:
                    for d in o.deps:
                        k = id(d.sem)
                        if waited.get(k, 0) < d.val:
                            eng.wait_ge(d.sem, d.val)
                            waited[k] = d.val
                    if o.is_dma and o.val > 16:
                        k = id(o.sem)
                        if waited.get(k, 0) < o.val - 16:
                            eng.wait_ge(o.sem, o.val - 16)
                            waited[k] = o.val - 16
                    ins = o.fn(eng)
                    if o.is_dma:
                        ins.then_inc(o.sem, 16)
                    elif o.need:
                        ins.then_inc(o.sem, 1)
                last = {}
                for o in self.ops[e]:
                    if o.is_dma:
                        last[id(o.sem)] = (o.sem, o.val)
                for k, (s, v) in last.items():
                    if waited.get(k, 0) < v:
                        eng.wait_ge(s, v)
            return body

        block.tensor(make("pe"))
        block.scalar(make("act"))
        block.vector(make("dve"))
        block.gpsimd(make("pool"))
        block.sync(make("sp"))


def build_program(n_layers=DEPTH, mix=0, debug=False):
    nc = bass.Bass("TRN2", target_bir_lowering=False)
    stack = ExitStack()
    S = Sched(nc)

    def din(name, shape, dt=F32):
        return nc.dram_tensor(name, list(shape), dt, kind="ExternalInput").ap()

    def dout(name, shape, dt=F32):
        return nc.dram_tensor(name, list(shape), dt, kind="ExternalOutput").ap()

    xin = din("xin", [T, D])
    cvec = din("cvec", [2, D])
    w_mod = din("w_mod", [DEPTH, D, 9 * D])
    b_mod = din("b_mod", [DEPTH, 9 * D])
    g_norm = din("g_norm", [DEPTH, 3, D])
    w_ffn_in = din("w_ffn_in", [DEPTH, 2, D, 2 * DFF])
    w_ffn_out = din("w_ffn_out", [DEPTH, 2, DFF, D])
    w_in_ab = din("w_in_ab", [2, D, 2048])
    w_pool = din("w_pool", [2, 4, 128, 128])
    pool_scale = din("pool_scale", [2, 512])
    rpbP = din("rpbP", [2, 8, 25, 128])
    w_out_ab = din("w_out_ab", [2, D, D])
    w_qkv_c = din("w_qkv_c", [2, D, 1536])
    g_qnorm = din("g_qnorm", [2, 64])
    g_knorm = din("g_knorm", [2, 64])
    w_out_c = din("w_out_c", [2, D, D])
    g_final = din("g_final", [D])
    cnk = din("cnk", [2, 256, 512])
    cnv = din("cnv", [2, 256, 512])
    cak = din("cak", [2, 256, 256])
    cav = din("cav", [2, 256, 256])
    c_ident = din("c_ident", [128, 128])
    c_rot = din("c_rot", [128, 128])
    c_bd = din("c_bd", [128, 128])
    c_swp = din("c_swp", [128, 128])
    c_cos = din("c_cos", [128, TS])
    c_sin = din("c_sin", [128, TS])
    c_rm = din("c_rm", [16, 4 * 512])
    c_sel = din("c_sel", [16, 8 * 128])
    c_cmt = din("c_cmt", [128, 512])
    c_invb = din("c_invb", [128, 4 * 6 * 2 * 8])
    c_hv = din("c_hv", [128, 2])

    y = dout("y", [T, D])
    o_nbk = dout("o_nbk", [2, 2, 256, 512])
    o_nbv = dout("o_nbv", [2, 2, 256, 512])
    o_atk = dout("o_atk", [2, 2, 256, 256])
    o_atv = dout("o_atv", [2, 2, 256, 256])

    ccHi = nc.dram_tensor("ccHi", [2 * D, 256], BF16)
    ccHo = nc.dram_tensor("ccHo", [4 * D, 256], BF16)
    ccKi = nc.dram_tensor("ccKi", [256, TS], BF16)
    ccKo = nc.dram_tensor("ccKo", [512, TS], BF16)
    ccVi = nc.dram_tensor("ccVi", [TS, 256], BF16)
    ccVo = nc.dram_tensor("ccVo", [2 * TS, 256], BF16)
    rp_scr = nc.dram_tensor("rp_scr", [8 * 25 * 64, 128], F32)

    def sb(name, shape, dt):
        return stack.enter_context(nc.sbuf_tensor(name, list(shape), dt))

    def ps(name, shape, dt=F32):
        return stack.enter_context(nc.psum_tensor(name, list(shape), dt))

    X = sb("X", [128, NCH, T], F32)
    rX = [[Res("X%d_%d" % (t, c)) for c in range(NCH)] for t in range(5)]
    ARB = sb("ARB", [128, 51200], BF16)
    ARF = sb("ARF", [128, 3072], F32)
    ident = sb("ident", [128, 128], F32)
    identb = sb("identb", [128, 128], BF16)
    onesb = sb("onesb", [128, 128], BF16)
    rotb = sb("rotb", [128, 128], BF16)
    bdb = sb("bdb", [128, 128], BF16)
    swpf = sb("swpf", [128, 128], F32)
    epsT = sb("epsT", [128, 1], F32)
    MOD = sb("MOD", [128, 72, 2], F32)
    GS = sb("GS", [128, 3, NCH, 2], F32)
    HG = sb("HG", [128, 3, NCH, 2], F32)
    GN = sb("GN", [128, DEPTH * 3 * NCH], F32)
    BM = sb("BM", [128, 72], F32)
    GF = sb("GF", [128, NCH], F32)
    SC = sb("SC", [128, NCH, 2], F32)
    SCb = sb("SCb", [128, NCH, 2], BF16)
    RSTD = sb("RSTD", [128, 512], F32)
    TMPF = sb("TMPF", [128, 2, 512], F32)
    TMPB = sb("TMPB", [128, 2, 512], BF16)
    rConst = Res("const")
    rMOD = Res("MOD")
    rGS = Res("GS")
    rRSTD = Res("RSTD")
    rTMPF = [Res("TMPF0"), Res("TMPF1")]
    rTMPB = [Res("TMPB0"), Res("TMPB1")]

    PS = [ps("ps%d" % i, [128, 512]) for i in range(8)]
    rPS = [Res("ps%d" % i) for i in range(8)]

    class Rot:
        def __init__(self, idxs):
            self.idxs = idxs
            self.i = 0

        def next(self):
            k = self.idxs[self.i % len(self.idxs)]
            self.i += 1
            return k

    S.dma("sp", lambda e: e.dma_start(out=ident[:], in_=c_ident), [], [rConst])
    S.dma("pool", lambda e: e.dma_start(out=identb[:], in_=c_ident), [], [rConst])
    S.dma("pool", lambda e: e.dma_start(out=rotb[:], in_=c_rot), [], [rConst])
    S.dma("pool", lambda e: e.dma_start(out=bdb[:], in_=c_bd), [], [rConst])
    S.dma("sp", lambda e: e.dma_start(out=swpf[:], in_=c_swp), [], [rConst])
    S.op("dve", lambda e: e.memset(onesb[:], 1.0), [], [rConst])
    S.op("dve", lambda e: e.memset(epsT[:], EPS), [], [rConst])
    with nc.allow_non_contiguous_dma(reason="small param layouts"):
        S.dma("sp", lambda e: e.dma_start(out=GN[:], in_=g_norm.rearrange("l s (c p) -> p (l s c)", p=128), allow_slow_non_contiguous=True), [], [rConst])
        S.dma("sp", lambda e: e.dma_start(out=GF[:], in_=g_final.rearrange("(c p) -> p c", p=128), allow_slow_non_contiguous=True), [], [rConst])
        for vv in range(2):
            S.dma("sp", lambda e, vv=vv: e.dma_start(out=SC[:, :, vv], in_=cvec[vv].rearrange("(c p) -> p c", p=128), allow_slow_non_contiguous=True), [], [rConst])
    S.op("act", lambda e: e.activation(out=SCb[:], in_=SC[:], func=AF.Silu), [rConst], [rConst])

    XL = ARF[:, 0:2048].rearrange("p (a f) -> p a f", a=2)
    rXL = [Res("XL%d" % i) for i in range(2)]
    psr = Rot([0, 1, 2, 3])
    for t in range(5):
        for s4 in range(4):
            tk = t * 512 + s4 * 128
            sl = s4 % 2
            S.dma("sp", lambda e, sl=sl, tk=tk: e.dma_start(out=XL[:, sl, :], in_=xin[tk:tk + 128, :]), [], [rXL[sl]])
            for hf in range(2):
                b = psr.next()
                for cc in range(4):
                    c = hf * 4 + cc
                    S.op("pe", lambda e, b=b, sl=sl, c=c, cc=cc: e.transpose(PS[b][:, cc * 128:(cc + 1) * 128],
                                                                             XL[:, sl, c * 128:(c + 1) * 128], ident[:]),
                         [rXL[sl], rConst], [rPS[b]])
                if hf == 0:
                    S.op("dve", lambda e, b=b, hf=hf, tk=tk: e.tensor_copy(
                        out=X[:, hf * 4:(hf + 1) * 4, tk:tk + 128], in_=PS[b][:].rearrange("p (c n) -> p c n", c=4)),
                        [rPS[b]], [rX[t][hf * 4 + i] for i in range(4)])
                else:
                    S.op("act", lambda e, b=b, hf=hf, tk=tk: e.activation(
                        out=X[:, hf * 4:(hf + 1) * 4, tk:tk + 128], in_=PS[b][:].rearrange("p (c n) -> p c n", c=4), func=AF.Copy),
                        [rPS[b]], [rX[t][hf * 4 + i] for i in range(4)])

    def vsel(t):
        return 0 if t == 0 else 1

    BARS = sb("BARS", [128, 1], F32)

    def barrier():
        S.barrier(lambda e: e.memset(BARS[:], 0.0))

    Wbuf = ARB[:, 30720:43008].rearrange("p (b x) -> p b x", b=2)
    g_rW = [Res("W0"), Res("W1")]
    g_rH = [Res("H%d" % t) for t in range(5)]
    g_rG = [[Res("G%d_%d" % (b, t)) for t in range(5)] for b in range(2)]

    def ada(l):
        WM = Wbuf[:, :, 0:4096].rearrange("p b (k n) -> p b k n", k=8)
        rWM = g_rW
        with nc.allow_non_contiguous_dma(reason="bias layout"):
            S.dma("sp", lambda e: e.dma_start(out=BM[:], in_=b_mod[l].rearrange("(m p) -> p m", p=128), allow_slow_non_contiguous=True), [rMOD], [rMOD])
        pb = 7
        for blk in range(18):
            bi = blk % 2
            S.dma("pool", lambda e, bi=bi, blk=blk: e.dma_start(
                out=WM[:, bi], in_=w_mod[l][:, blk * 512:(blk + 1) * 512].rearrange("(k p) n -> p k n", p=128)),
                [], [rWM[bi]])
            for q in range(4):
                ch = blk * 4 + q
                for kc in range(8):
                    S.op("pe", lambda e, bi=bi, q=q, kc=kc, ch=ch: e.matmul(
                        PS[pb][:, ch * 2:ch * 2 + 2], WM[:, bi, kc, q * 128:(q + 1) * 128], SCb[:, kc, :],
                        start=(kc == 0), stop=(kc == 7)), [rWM[bi], rConst], [rPS[pb]])
        S.op("dve", lambda e: e.tensor_tensor(
            out=MOD[:], in0=PS[pb][:, 0:144].rearrange("p (m v) -> p m v", v=2),
            in1=BM[:].unsqueeze(2).to_broadcast([128, 72, 2]), op=ALU.add), [rPS[pb], rMOD], [rMOD])
        for s in range(3):
            gsl = GN[:, (l * 3 + s) * 8:(l * 3 + s + 1) * 8]
            S.op("dve", lambda e, s=s, gsl=gsl: e.scalar_tensor_tensor(
                out=GS[:, s], in0=MOD[:, (3 * s + 1) * 8:(3 * s + 2) * 8, :], scalar=1.0,
                in1=gsl.unsqueeze(2).to_broadcast([128, 8, 2]), op0=ALU.add, op1=ALU.mult), [rMOD, rConst], [rGS])
            S.op("dve", lambda e, s=s: e.tensor_scalar(
                out=HG[:, s], in0=MOD[:, (3 * s + 2) * 8:(3 * s + 3) * 8, :], scalar1=(1.0 if s == 1 else 0.5),
                scalar2=None, op0=ALU.mult), [rMOD], [rGS])

    def rstd_from_ps(pb, n, inv_d, reads):
        S.op("act", lambda e: e.activation(out=RSTD[:, 0:n], in_=PS[pb][:, 0:n], func=AF.Sqrt, bias=epsT[:], scale=inv_d),
             [rPS[pb], rConst] + reads, [rRSTD])
        S.op("dve", lambda e: e.reciprocal(out=RSTD[:, 0:n], in_=RSTD[:, 0:n]), [rRSTD], [rRSTD])

    SQ = ARB[:, 43008:47104].rearrange("p (c n) -> p c n", c=8)
    rSQ = Res("SQ")

    def norm_mod(s, t, Hdst, rH):
        v = vsel(t)
        t0 = t * 512
        S.op("act", lambda e: e.activation(out=SQ[:], in_=X[:, :, t0:t0 + 512], func=AF.Square), rX[t], [rSQ])
        pb = 6
        for c in range(8):
            S.op("pe", lambda e, c=c: e.matmul(PS[pb][:], onesb[:], SQ[:, c, :], start=(c == 0), stop=(c == 7)),
                 [rSQ, rConst], [rPS[pb]])
        rstd_from_ps(pb, 512, 1.0 / D, [])
        for c in range(8):
            k = c % 2
            S.op("dve", lambda e, c=c, k=k: e.tensor_tensor(out=TMPF[:, k], in0=X[:, c, t0:t0 + 512], in1=RSTD[:], op=ALU.mult),
                 [rX[t][c], rRSTD], [rTMPF[k]])
            S.op("act", lambda e, c=c, k=k: e.activation(out=Hdst[:, c, :], in_=TMPF[:, k], func=AF.Identity,
                                                          bias=MOD[:, (3 * s) * 8 + c, v:v + 1], scale=GS[:, s, c, v:v + 1]),
                 [rTMPF[k], rGS, rMOD], [rH])

    def ffn(l, s, wi):
        H = ARB[:, 0:20480].rearrange("p (c n) -> p c n", c=8)
        G = ARB[:, 20480:30720].rearrange("p (b j n) -> p b j n", b=2, j=2)
        W = Wbuf
        rH, rG, rW = g_rH, g_rG, g_rW
        for t in range(2):
            norm_mod(s, t, H[:, :, t * 512:(t + 1) * 512], rH[t])
        win = w_ffn_in[l, wi]
        wout = w_ffn_out[l, wi]
        pin = Rot([0, 1, 2, 3])
        pout = Rot([4, 5, 6, 7])
        for jp in range(11):
            b = jp % 2
            WA = W[:, b, 0:2048].rearrange("p (k n) -> p k n", k=8)
            WB = W[:, b, 2048:4096].rearrange("p (k n) -> p k n", k=8)
            WO = W[:, b, 4096:6144].rearrange("p (j n) -> p j n", j=2)
            S.dma("pool", lambda e, WA=WA, jp=jp: e.dma_start(
                out=WA, in_=win[:, jp * 256:(jp + 1) * 256].rearrange("(k p) n -> p k n", p=128)), [], [rW[b]])
            S.dma("pool", lambda e, WB=WB, jp=jp: e.dma_start(
                out=WB, in_=win[:, DFF + jp * 256:DFF + (jp + 1) * 256].rearrange("(k p) n -> p k n", p=128)), [], [rW[b]])
            S.dma("pool", lambda e, WO=WO, jp=jp: e.dma_start(
                out=WO, in_=wout[jp * 256:(jp + 1) * 256, :].rearrange("(j p) n -> p j n", p=128)), [], [rW[b]])
            for t in range(5):
                t0 = t * 512
                for jj in range(2):
                    pa = pin.next()
                    pbk = pin.next()
                    for kc in range(8):
                        S.op("pe", lambda e, pa=pa, kc=kc, jj=jj, WA=WA, t0=t0: e.matmul(
                            PS[pa][:], WA[:, kc, jj * 128:(jj + 1) * 128], H[:, kc, t0:t0 + 512],
                            start=(kc == 0), stop=(kc == 7)), [rW[b], rH[t]], [rPS[pa]])
                    for kc in range(8):
                        S.op("pe", lambda e, pbk=pbk, kc=kc, jj=jj, WB=WB, t0=t0: e.matmul(
                            PS[pbk][:], WB[:, kc, jj * 128:(jj + 1) * 128], H[:, kc, t0:t0 + 512],
                            start=(kc == 0), stop=(kc == 7)), [rW[b], rH[t]], [rPS[pbk]])
                    k = jj
                    S.op("act", lambda e, pa=pa, k=k: e.activation(out=TMPB[:, k], in_=PS[pa][:], func=AF.Silu),
                         [rPS[pa]], [rTMPB[k]])
                    S.op("dve", lambda e, pbk=pbk, k=k, jj=jj, t0=t0, b=b: e.tensor_tensor(
                        out=G[:, b, jj, t0:t0 + 512], in0=TMPB[:, k], in1=PS[pbk][:], op=ALU.mult),
                        [rTMPB[k], rPS[pbk]], [rG[b][t]])
                if jp == 0 and t + 2 < 5:
                    norm_mod(s, t + 2, H[:, :, (t + 2) * 512:(t + 3) * 512], rH[t + 2])
            for t in range(5):
                t0 = t * 512
                v = vsel(t)
                for c in range(8):
                    po = pout.next()
                    for jj in range(2):
                        S.op("pe", lambda e, po=po, jj=jj, c=c, WO=WO, t0=t0, b=b: e.matmul(
                            PS[po][:], WO[:, jj, c * 128:(c + 1) * 128], G[:, b, jj, t0:t0 + 512],
                            start=(jj == 0), stop=(jj == 1)), [rW[b], rG[b][t]], [rPS[po]])
                    S.op("dve", lambda e, po=po, c=c, t0=t0, v=v: e.scalar_tensor_tensor(
                        out=X[:, c, t0:t0 + 512], in0=PS[po][:], scalar=HG[:, s, c, v:v + 1], in1=X[:, c, t0:t0 + 512],
                        op0=ALU.mult, op1=ALU.add), [rPS[po], rGS, rX[t][c]], [rX[t][c]])

    def final_out():
        YT = ARF[:, 0:2048].rearrange("p (a f) -> p a f", a=2)
        rYT = rXL
        pr = Rot([0, 1, 2, 3, 4, 5])
        XGt = ARF[:, 2048:2560]
        rXG = Res("XG")
        RS1 = sb("RS1", [128, 1], F32)
        rRS1 = Res("RS1")
        for t in range(5):
            for s4 in range(4):
                tk = t * 512 + s4 * 128
                S.op("act", lambda e, tk=tk: e.activation(out=SQ[:, :, 0:128], in_=X[:, :, tk:tk + 128], func=AF.Square),
                     rX[t], [rSQ])
                pb = 6
                for c in range(8):
                    S.op("pe", lambda e, c=c: e.matmul(PS[pb][:, 0:1], SQ[:, c, 0:128], onesb[:, 0:1],
                                                       start=(c == 0), stop=(c == 7)), [rSQ, rConst], [rPS[pb]])
                S.op("act", lambda e: e.activation(out=RS1[:], in_=PS[pb][:, 0:1], func=AF.Sqrt, bias=epsT[:], scale=1.0 / D),
                     [rPS[pb], rConst], [rRS1])
                S.op("dve", lambda e: e.reciprocal(out=RS1[:], in_=RS1[:]), [rRS1], [rRS1])
                for half in range(2):
                    b = pr.next()
                    for cc in range(4):
                        c = half * 4 + cc
                        S.op("dve", lambda e, c=c, tk=tk: e.tensor_scalar(
                            out=XGt[:, 0:128], in0=X[:, c, tk:tk + 128], scalar1=GF[:, c:c + 1], scalar2=None, op0=ALU.mult),
                            [rX[t][c], rConst], [rXG])
                        S.op("pe", lambda e, b=b, cc=cc: e.transpose(PS[b][:, cc * 128:(cc + 1) * 128], XGt[:, 0:128], ident[:]),
                             [rXG, rConst], [rPS[b]])
                    S.op("dve", lambda e, b=b, half=half, s4=s4: e.tensor_scalar(
                        out=YT[:, s4 % 2, half * 512:(half + 1) * 512], in0=PS[b][:], scalar1=RS1[:, 0:1], scalar2=None, op0=ALU.mult),
                        [rPS[b], rRS1], [rYT[s4 % 2]])
                S.dma("sp", lambda e, s4=s4, tk=tk: e.dma_start(out=y[tk:tk + 128, :], in_=YT[:, s4 % 2, :]), [rYT[s4 % 2]], [])


    M0 = 20480
    region = []
    flatG = [r for bb in g_rG for r in bb]

    def newres(name, pend):
        r = Res(name, pend)
        region.append(r)
        return r

    rCCi = Res("cci")
    rCCo = Res("cco")
    rRP = Res("rp_scr")
    CMT = ARB[:, 47104:47616]
    SEL = ARB[0:16, 47616:48640]
    RMB = ARB[0:16, 48640:50688]
    S.dma("pool", lambda e: e.dma_start(out=CMT, in_=c_cmt), [], [rConst])
    S.dma("pool", lambda e: e.dma_start(out=SEL, in_=c_sel), [], [rConst])
    S.dma("pool", lambda e: e.dma_start(out=RMB, in_=c_rm), [], [rConst])
    INVB = sb("INVB", [128, 4 * 6 * 2 * 8], F32)
    HV = sb("HV", [128, 2], F32)
    PSCL = sb("PSCL", [128, 2 * 4], F32)
    S.dma("sp", lambda e: e.dma_start(out=INVB[:], in_=c_invb), [], [rConst])
    S.dma("sp", lambda e: e.dma_start(out=HV[:], in_=c_hv), [], [rConst])
    S.dma("sp", lambda e: e.dma_start(out=PSCL[:], in_=pool_scale.rearrange("e (g p) -> p (e g)", p=128),
                                      allow_slow_non_contiguous=True), [], [rConst])

    sbank = Rot([0, 1, 2])
    obank = Rot([3, 4])
    mbank = Rot([5, 6])
    PJ = 7

    def cc_allgather(cin, cout, nocc):
        if nocc:
            n = cin.shape[0]
            S.dma("sp", lambda e: e.dma_start(out=cout.ap()[0:n, :], in_=cin.ap()), [rCCi], [rCCo])
            S.dma("sp", lambda e: e.dma_start(out=cout.ap()[n:2 * n, :], in_=cin.ap()), [rCCi], [rCCo])
        else:
            S.op("pool", lambda e: e.collective_compute("AllGather", ALU.bypass, replica_groups=PAIRS,
                                                        ins=[cin.ap().opt()], outs=[cout.ap().opt()]),
                 [rCCi], [rCCo])

    def attention(qT, half, nq, ktiles, out_ap, rQ, rOut, PT, rPT, ptrot, hook=None, sb=None, pre_n=2):
        sb = sb or sbank
        po = obank.next()
        pm = mbank.next()
        hs = slice(half * 64, (half + 1) * 64)
        n = len(ktiles)
        banks = {}

        def s_mm(i):
            kt = ktiles[i]
            nk = kt["nk"]
            pss = sb.next()
            banks[i] = pss
            terms = [(kt["kT"], qT, [rQ] + kt["reads"])] + kt.get("bias", [])
            for j, (lt, rh, rd) in enumerate(terms):
                S.op("pe", lambda e, pss=pss, lt=lt, rh=rh, j=j, nt=len(terms), nk=nk: e.matmul(
                    PS[pss][0:nk, 0:nq], lt, rh, start=(j == 0), stop=(j == nt - 1)), rd, [rPS[pss]])

        pre = min(pre_n, n)
        for i in range(pre):
            s_mm(i)
        for i, kt in enumerate(ktiles):
            nk = kt["nk"]
            pss = banks[i]
            pt = ptrot.next()
            S.op("act", lambda e, pss=pss, pt=pt, nk=nk: e.activation(out=PT[0:nk, pt, 0:nq], in_=PS[pss][0:nk, 0:nq], func=AF.Exp),
                 [rPS[pss]], [rPT[pt]])
            if i + pre < n:
                s_mm(i + pre)
            S.op("pe", lambda e, pt=pt, kt=kt, i=i, nk=nk: e.matmul(PS[po][hs, 0:nq], kt["v"], PT[0:nk, pt, 0:nq],
                                                                 start=(i == 0), stop=(i == n - 1)),
                 [rPT[pt]] + kt["reads"], [rPS[po]])
            S.op("pe", lambda e, pt=pt, i=i, nk=nk: e.matmul(PS[pm][hs, 0:nq], onesb[0:nk, 0:64], PT[0:nk, pt, 0:nq],
                                                         start=(i == 0), stop=(i == n - 1)),
                 [rPT[pt], rConst], [rPS[pm]])
            if hook is not None:
                hook()
        S.op("dve", lambda e: e.reciprocal(out=TMPF[hs, 0, 0:nq], in_=PS[pm][hs, 0:nq]), [rPS[pm]], [rTMPF[0]])
        S.op("dve", lambda e: e.tensor_tensor(out=out_ap, in0=PS[po][hs, 0:nq], in1=TMPF[hs, 0, 0:nq], op=ALU.mult),
             [rPS[po], rTMPF[0]], [rOut])

    def proj_fm(Wt, rW_, rhs_fn, n, evac, bank=None):
        bk = PJ if bank is None else bank
        for kc in range(8):
            rh, rd = rhs_fn(kc)
            S.op("pe", lambda e, kc=kc, rh=rh: e.matmul(PS[bk][:, 0:n], Wt[:, kc, :], rh, start=(kc == 0), stop=(kc == 7)),
                 [rW_] + rd, [rPS[bk]])
        if bank is None:
            evac(PS[bk])
        else:
            evac(PS[bk], bk)

    def even_mixer(l, nocc):
        e_ = l // 2
        H2 = ARB[:, 0:20480].rearrange("p (c n) -> p c n", c=8)
        rH = g_rH
        for t in range(5):
            norm_mod(1, t, H2[:, :, t * 512:(t + 1) * 512], rH[t])
        pend0 = S.fence_ops(flatG + g_rW)
        del region[:]
        for q4 in range(4):
            S.dma("sp", lambda e, q4=q4: e.dma_start(
                out=rp_scr.ap()[q4 * 3200:(q4 + 1) * 3200, :].rearrange("(hs ck) m -> hs ck m", ck=64),
                in_=rpbP[e_].rearrange("h s m -> (h s) m")[q4 * 50:(q4 + 1) * 50, :].unsqueeze(1).to_broadcast([50, 64, 128])),
                [rRP], [rRP])
        HH = ARB[:, M0:M0 + 4096].rearrange("p (c n) -> p c n", c=8)
        rHH = newres("HH", pend0)
        S.dma("sp", lambda e: e.dma_start(out=ccHi.ap()[0:1024, :].rearrange("(c p) n -> p c n", p=128), in_=H2[:, :, 512:768]),
              [rH[1], rCCo], [rCCi])
        S.dma("sp", lambda e: e.dma_start(out=ccHi.ap()[1024:2048, :].rearrange("(c p) n -> p c n", p=128), in_=H2[:, :, 2304:2560]),
              [rH[4], rCCo], [rCCi])
        cc_allgather(ccHi, ccHo, nocc)
        for h2 in range(2):
            S.dma("sp", lambda e, h2=h2: e.dma_start(
                out=HH[:, :, h2 * 256:(h2 + 1) * 256],
                in_=ccHo.ap()[1024 + h2 * 1024:2048 + h2 * 1024, :].rearrange("(c p) n -> p c n", p=128)), [rCCo], [rHH])

        gate_s = 1
        base = M0 + 4096
        STG = float(os.environ.get("MIXSTAGE", "99"))
        if STG <= 1:
            return
        pb_ = base
        WU4 = ARB[:, pb_:pb_ + 4096].rearrange("p (g k n) -> p g k n", g=4, k=8); pb_ += 4096
        WP4 = ARB[:, pb_:pb_ + 512].rearrange("p (g n) -> p g n", g=4); pb_ += 512
        WO4 = ARB[:, pb_:pb_ + 4096].rearrange("p (g n) -> p g n", g=4); pb_ += 4096
        PL2 = ARB[:, pb_:pb_ + 1024].rearrange("p (b n) -> p b n", b=2); pb_ += 1024
        AO4 = ARB[:, pb_:pb_ + 2048].rearrange("p (g n) -> p g n", g=4); pb_ += 2048
        UE2 = ARF[:, 0:1056].rearrange("p (b n) -> p b n", b=2)
        T1 = ARF[:, 1056:1584]
        T2 = ARF[:, 1584:2112]
        E8 = ARF[:, 2112:2128]
        rWU = newres("WU", pend0)
        rPLb = [newres("PL%d" % i, pend0) for i in range(2)]
        rAOg = [newres("AO%d" % i, pend0) for i in range(4)]
        rUEb = [newres("UE%d" % i, pend0) for i in range(2)]
        rT1 = newres("T1", pend0)
        rT2 = newres("T2", pend0)
        rE8 = newres("E8", pend0)
        for g in range(4):
            S.dma("pool", lambda e, g=g: e.dma_start(out=WU4[:, g], in_=w_in_ab[e_][:, g * 128:(g + 1) * 128].rearrange("(k p) n -> p k n", p=128)),
                  [], [rWU])
            S.dma("pool", lambda e, g=g: e.dma_start(out=WP4[:, g, :], in_=w_pool[e_, g]), [], [rWU])
            S.dma("pool", lambda e, g=g: e.dma_start(out=WO4[:, g, :], in_=w_out_ab[e_][g * 128:(g + 1) * 128, :]), [], [rWU])
        segs = [(0, 256, 0), (256, 256, 0), (512, 512, 1), (1024, 512, 2), (1536, 512, 3), (2048, 512, 4)]
        hrot = Rot([0, 1])
        CHE = os.environ.get("CHAIN_ENG", "dve")
        def pool_seg(si, t0, L, t):
            v = vsel(t)
            def pool_g(g):
                w = 2 << g
                ub = g % 2
                UE = UE2[:, ub, :]
                rUE = rUEb[ub]
                PL = PL2[:, ub, :]
                rPL = rPLb[ub]
                WU = WU4[:, g]
                for kc in range(8):
                    S.op("pe", lambda e, kc=kc, WU=WU: e.matmul(PS[PJ][:, 0:L], WU[:, kc, :], H2[:, kc, t0:t0 + L],
                                                             start=(kc == 0), stop=(kc == 7)), [rWU, rH[t]], [rPS[PJ]])
                S.op("act", lambda e, UE=UE: e.activation(out=UE[:, 8:8 + L], in_=PS[PJ][:, 0:L], func=AF.Copy), [rPS[PJ]], [rUE])
                if t >= 1:
                    pb2 = hrot.next()
                    for kc in range(8):
                        lh = HH[:, kc, 248:256] if t == 1 else H2[:, kc, t0 - 8:t0]
                        S.op("pe", lambda e, kc=kc, lh=lh, pb2=pb2, WU=WU: e.matmul(PS[pb2][:, 0:8], WU[:, kc, :], lh, start=(kc == 0), stop=(kc == 7)),
                             [rWU, rHH, rH[t - 1]], [rPS[pb2]])
                    for kc in range(8):
                        rh_ = HH[:, kc, 256:264] if t == 4 else H2[:, kc, t0 + 512:t0 + 520]
                        S.op("pe", lambda e, kc=kc, rh_=rh_, pb2=pb2, WU=WU: e.matmul(PS[pb2][:, 8:16], WU[:, kc, :], rh_, start=(kc == 0), stop=(kc == 7)),
                             [rWU, rHH, rH[min(t + 1, 4)]], [rPS[pb2]])
                    if t == 1:
                        S.op("dve", lambda e, pb2=pb2, UE=UE: e.tensor_scalar(out=UE[:, 0:8], in0=PS[pb2][:, 0:8], scalar1=HV[:, 0:1], scalar2=None, op0=ALU.mult),
                             [rPS[pb2], rConst], [rUE])
                    else:
                        S.op("dve", lambda e, pb2=pb2, UE=UE: e.tensor_copy(out=UE[:, 0:8], in_=PS[pb2][:, 0:8]), [rPS[pb2]], [rUE])
                    if t == 4:
                        S.op("dve", lambda e, pb2=pb2, UE=UE: e.tensor_scalar(out=UE[:, 8 + L:16 + L], in0=PS[pb2][:, 8:16], scalar1=HV[:, 1:2], scalar2=None, op0=ALU.mult),
                             [rPS[pb2], rConst], [rUE])
                    else:
                        S.op("dve", lambda e, pb2=pb2, UE=UE: e.tensor_copy(out=UE[:, 8 + L:16 + L], in_=PS[pb2][:, 8:16]), [rPS[pb2]], [rUE])
                else:
                    S.op("pool", lambda e, UE=UE: e.memset(UE[:, 8 + L:16 + L], 0.0), [], [rUE])
                    S.op("pool", lambda e, UE=UE: e.memset(UE[:, 0:8], 0.0), [], [rUE])
                Lp = L + 16
                S.op(CHE, lambda e, Lp=Lp, UE=UE: e.tensor_tensor(out=T1[:, 0:Lp - 1], in0=UE[:, 0:Lp - 1], in1=UE[:, 1:Lp], op=ALU.add), [rUE], [rT1])
                cur, rcur, oth, roth = T1, rT1, T2, rT2
                ln = Lp - 1
                sh = 1
                for k in range(g):
                    sh *= 2
                    nl = ln - sh
                    S.op(CHE, lambda e, cur=cur, oth=oth, nl=nl, sh=sh: e.tensor_tensor(out=oth[:, 0:nl], in0=cur[:, 0:nl], in1=cur[:, sh:sh + nl], op=ALU.add),
                         [rcur], [roth])
                    cur, rcur, oth, roth = oth, roth, cur, rcur
                    ln = nl
                off = 8 - w // 2
                S.op("dve", lambda e, cur=cur, off=off, w=w, UE=UE, PL=PL: e.scalar_tensor_tensor(
                    out=PL[:, 0:L], in0=cur[:, off:off + L], scalar=1.0 / w, in1=UE[:, 8:8 + L], op0=ALU.mult, op1=ALU.subtract),
                    [rcur, rUE], [rPL])
                for ed in range(2):
                    a0 = 0 if ed == 0 else L - 8
                    ib = ((g * 6 + si) * 2 + ed) * 8
                    S.op("dve", lambda e, cur=cur, off=off, a0=a0, ib=ib: e.tensor_tensor(
                        out=E8[:, 0:8], in0=cur[:, off + a0:off + a0 + 8], in1=INVB[:, ib:ib + 8], op=ALU.mult), [rcur, rConst], [rE8])
                    S.op("dve", lambda e, a0=a0, UE=UE, PL=PL: e.tensor_tensor(out=PL[:, a0:a0 + 8], in0=E8[:, 0:8], in1=UE[:, 8 + a0:16 + a0], op=ALU.subtract),
                         [rE8, rUE], [rPL])
                pw = hrot.next()
                S.op("pe", lambda e, g=g, pw=pw, PL=PL: e.matmul(PS[pw][:, 0:L], WP4[:, g, :], PL[:, 0:L], start=True, stop=True), [rWU, rPL], [rPS[pw]])
                S.op("act", lambda e, g=g, pw=pw: e.activation(out=AO4[:, g, 0:L], in_=PS[pw][:, 0:L], func=AF.Copy,
                                                               scale=PSCL[:, e_ * 4 + g:e_ * 4 + g + 1]), [rPS[pw], rConst], [rAOg[g]])
            for g in range(4):
                pool_g(g)
            for c in range(8):
                pbo = 2 + (c % 2) * 0 if False else (2 if c % 2 == 0 else 6)
                for g in range(4):
                    S.op("pe", lambda e, c=c, g=g, pbo=pbo: e.matmul(PS[pbo][:, 0:L], WO4[:, g, c * 128:(c + 1) * 128], AO4[:, g, 0:L],
                                                                 start=(g == 0), stop=(g == 3)), [rWU, rAOg[g]], [rPS[pbo]])
                S.op("dve", lambda e, c=c, pbo=pbo: e.scalar_tensor_tensor(
                    out=X[:, c, t0:t0 + L], in0=PS[pbo][:, 0:L], scalar=HG[:, gate_s, c, v:v + 1], in1=X[:, c, t0:t0 + L],
                    op0=ALU.mult, op1=ALU.add), [rPS[pbo], rGS, rX[t][c]], [rX[t][c]])

        for si, (t0, L, t) in enumerate(segs):
            pool_seg(si, t0, L, t)
        if STG <= 2:
            return
        keep = [rHH]
        pend1 = S.fence_ops([r for r in region if r is not rHH])
        del region[:]
        region.append(rHH)
        o = base
        KT = ARB[:, o:o + 3072]; o += 3072
        VT = ARB[:, o:o + 3072].rearrange("p (t f) -> p t f", f=128); o += 3072
        QT = ARB[:, o:o + 512]; o += 512
        MB = ARB[:, o:o + 2560]; o += 2560
        TB = ARB[:, o:o + 3072].rearrange("p (h x) -> p h x", h=2); o += 3072
        PT = ARB[:, o:o + 1536].rearrange("p (b n) -> p b n", b=3); o += 1536
        KTC = ARB[:, o:o + 256]; o += 256
        VTC = ARB[:, o:o + 256].rearrange("p (t f) -> p t f", f=128); o += 256
        WQ = ARB[:, o:o + 1024].rearrange("p (k n) -> p k n", k=8); o += 1024
        WK = ARB[:, o:o + 1024].rearrange("p (k n) -> p k n", k=8); o += 1024
        WV = ARB[:, o:o + 1024].rearrange("p (k n) -> p k n", k=8); o += 1024
        WO2 = ARB[:, o:o + 1024]; o += 1024
        assert o <= 43008, o
        KS = ARF[:, 0:512].rearrange("p (t f) -> p t f", f=128)
        VS = ARF[:, 512:1024].rearrange("p (t f) -> p t f", f=128)
        CS = ARF[:, 1024:1280].rearrange("p (t f) -> p t f", f=128)
        rKT = newres("KT", pend1); rVT = newres("VT", pend1); rQT = newres("QT", pend1); rMB = newres("MB", pend1)
        rQTb = [rQT, newres("QT2", pend1)]
        rMBt = [newres("MB%d" % i, pend1) for i in range(5)]
        rTB = newres("TB", pend1); rPT = [newres("PT%d" % i, pend1) for i in range(3)]
        rKTC = newres("KTC", pend1); rVTC = newres("VTC", pend1); rWA = newres("WA", pend1)
        rKS = newres("KS", pend1); rVS = newres("VS", pend1); rCS = newres("CS", pend1)
        ptrot = Rot([0, 1, 2])
        wia = w_in_ab[e_]
        for hc in range(4):
            S.dma("pool", lambda e, hc=hc: e.dma_start(out=WQ, in_=wia[:, 512 + hc * 128:640 + hc * 128].rearrange("(k p) n -> p k n", p=128)), [], [rWA])
            S.dma("pool", lambda e, hc=hc: e.dma_start(out=WK, in_=wia[:, 1024 + hc * 128:1152 + hc * 128].rearrange("(k p) n -> p k n", p=128)), [], [rWA])
            S.dma("pool", lambda e, hc=hc: e.dma_start(out=WV, in_=wia[:, 1536 + hc * 128:1664 + hc * 128].rearrange("(k p) n -> p k n", p=128)), [], [rWA])
            S.dma("pool", lambda e, hc=hc: e.dma_start(out=WO2, in_=w_out_ab[e_][512 + hc * 128:640 + hc * 128, :]), [], [rWA])
            for hf in range(2):
                h = 2 * hc + hf
                for kr in range(2):
                    if os.environ.get("NOTB"):
                        continue
                    off = ((h * 25 + 1 - kr) * 64) * 128 + 63
                    src = bass.AP(tensor=rp_scr, offset=off, ap=[[127, 64], [8192, 24], [1, 64]])
                    S.dma("pool", lambda e, hf=hf, kr=kr, src=src: e.dma_start(
                        out=TB[kr * 64:(kr + 1) * 64, hf, :].rearrange("p (s c) -> p s c", c=64), in_=src), [rRP], [rTB])
                S.op("pool", lambda e, hf=hf: e.tensor_tensor(
                    out=TB[:, hf, :].rearrange("p (s c) -> p s c", c=64), in0=TB[:, hf, :].rearrange("p (s c) -> p s c", c=64),
                    in1=CMT[:, 0:64].unsqueeze(1).to_broadcast([128, 24, 64]), op=ALU.add), [rTB, rConst], [rTB])
            if STG <= 2.2:
                return
            S.dma("sp", lambda e, hc=hc: e.dma_start(out=CS, in_=cnk[e_][:, hc * 128:(hc + 1) * 128].rearrange("(t p) f -> p t f", p=128)), [], [rCS])
            S.dma("pool", lambda e, hc=hc: e.dma_start(out=VTC, in_=cnv[e_][:, hc * 128:(hc + 1) * 128].rearrange("(t p) f -> p t f", p=128)), [], [rVTC])
            for tt in range(2):
                S.op("pe", lambda e, tt=tt: e.transpose(PS[PJ][:, tt * 128:(tt + 1) * 128], CS[:, tt, :], ident[:]), [rCS, rConst], [rPS[PJ]])
            S.op("act", lambda e: e.activation(out=KTC, in_=PS[PJ][:, 0:256], func=AF.Copy), [rPS[PJ]], [rKTC])
            if STG <= 2.4:
                return
            for t in range(5):
                dst = KT[:, 0:512] if t == 0 else KT[:, 512 + 256 + 512 * (t - 1):512 + 256 + 512 * t]
                if t % 2 == 0:
                    proj_fm(WK, rWA, lambda kc, t=t: (H2[:, kc, t * 512:(t + 1) * 512], [rH[t]]), 512,
                            lambda P, bk, dst=dst: S.op("act", lambda e: e.activation(out=dst, in_=P[:, 0:512], func=AF.Copy), [rPS[bk]], [rKT]), bank=PJ)
                else:
                    proj_fm(WK, rWA, lambda kc, t=t: (H2[:, kc, t * 512:(t + 1) * 512], [rH[t]]), 512,
                            lambda P, bk, dst=dst: S.op("dve", lambda e: e.tensor_copy(out=dst, in_=P[:, 0:512]), [rPS[bk]], [rKT]), bank=2)
            proj_fm(WK, rWA, lambda kc: (HH[:, kc, :], [rHH]), 512,
                    lambda P: (S.op("act", lambda e: e.activation(out=KT[:, 512:768], in_=P[:, 0:256], func=AF.Copy), [rPS[PJ]], [rKT]),
                               S.op("act", lambda e: e.activation(out=KT[:, 512 + 2304:512 + 2560], in_=P[:, 256:512], func=AF.Copy), [rPS[PJ]], [rKT])))
            if STG <= 2.6:
                return
            def vsrc(ti):
                if ti < 4:
                    return lambda kc: H2[:, kc, ti * 128:(ti + 1) * 128], rH[0]
                j = ti - 4
                if j < 2:
                    return lambda kc: HH[:, kc, j * 128:(j + 1) * 128], rHH
                if j >= 18:
                    return lambda kc: HH[:, kc, 256 + (j - 18) * 128:256 + (j - 17) * 128], rHH
                tk = 512 + (j - 2) * 128
                return lambda kc: H2[:, kc, tk:tk + 128], rH[tk // 512]
            for grp in range(6):
                pb2 = sbank.next()
                for q in range(4):
                    ti = grp * 4 + q
                    fn, rr = vsrc(ti)
                    for kc in range(8):
                        S.op("pe", lambda e, kc=kc, q=q, fn=fn, pb2=pb2: e.matmul(PS[pb2][:, q * 128:(q + 1) * 128], fn(kc), WV[:, kc, :],
                                                                              start=(kc == 0), stop=(kc == 7)), [rWA, rr], [rPS[pb2]])
                if grp == 0:
                    S.op("act", lambda e, pb2=pb2: e.activation(out=VS, in_=PS[pb2][:].rearrange("p (t f) -> p t f", f=128), func=AF.Copy), [rPS[pb2]], [rVS])
                    S.op("dve", lambda e: e.tensor_copy(out=VT[:, 0:4, :], in_=VS), [rVS], [rVT])
                else:
                    S.op("dve", lambda e, grp=grp, pb2=pb2: e.tensor_copy(out=VT[:, grp * 4:(grp + 1) * 4, :], in_=PS[pb2][:].rearrange("p (t f) -> p t f", f=128)),
                         [rPS[pb2]], [rVT])
                if grp == 0:
                    for bb in range(2):
                        if os.environ.get("NOOUTDMA"):
                            continue
                        S.dma("sp", lambda e, bb=bb, hc=hc: e.dma_start(
                            out=o_nbv[bb, e_][:, hc * 128:(hc + 1) * 128].rearrange("(t p) f -> p t f", p=128), in_=VS[:, bb * 2:bb * 2 + 2, :]), [rVS], [])
            if STG <= 2.8:
                return
            pb2 = sbank.next()
            for q in range(4):
                for kc in range(8):
                    S.op("pe", lambda e, kc=kc, q=q, pb2=pb2: e.matmul(PS[pb2][:, q * 128:(q + 1) * 128], H2[:, kc, q * 128:(q + 1) * 128], WK[:, kc, :],
                                                                   start=(kc == 0), stop=(kc == 7)), [rWA, rH[0]], [rPS[pb2]])
            S.op("act", lambda e, pb2=pb2: e.activation(out=KS, in_=PS[pb2][:].rearrange("p (t f) -> p t f", f=128), func=AF.Copy), [rPS[pb2]], [rKS])
            for bb in range(2):
                S.dma("sp", lambda e, bb=bb, hc=hc: e.dma_start(
                    out=o_nbk[bb, e_][:, hc * 128:(hc + 1) * 128].rearrange("(t p) f -> p t f", p=128), in_=KS[:, bb * 2:bb * 2 + 2, :]), [rKS], [])
            if STG <= 3:
                return
            QTb = [QT, ARB[:, 50688:51200]]
            sb_e = Rot([0, 1])
            OPE = 2

            def qproj_e(t):
                qd = QTb[t % 2]
                proj_fm(WQ, rWA, lambda kc: (H2[:, kc, t * 512:(t + 1) * 512], [rH[t]]), 512,
                        lambda P: S.op("act", lambda e: e.activation(out=qd, in_=P[:, 0:512], func=AF.Identity, scale=0.125), [rPS[PJ]], [rQTb[t % 2]]))

            def outproj_e(t, c):
                v = vsel(t)
                S.op("pe", lambda e: e.matmul(PS[OPE][:], WO2[:, c * 128:(c + 1) * 128], MB[:, t * 512:(t + 1) * 512], start=True, stop=True),
                     [rWA, rMBt[t]], [rPS[OPE]])
                S.op("dve", lambda e: e.scalar_tensor_tensor(
                    out=X[:, c, t * 512:(t + 1) * 512], in0=PS[OPE][:], scalar=HG[:, gate_s, c, v:v + 1], in1=X[:, c, t * 512:(t + 1) * 512],
                    op0=ALU.mult, op1=ALU.add), [rPS[OPE], rGS, rX[t][c]], [rX[t][c]])

            qproj_e(0)
            for t in range(5):
                if STG <= 4 and t >= 1:
                    return
                nsteps = 8 if t == 0 else 20
                acts = {}
                if t + 1 < 5:
                    acts.setdefault(0, []).append(lambda t=t: qproj_e(t + 1))
                if t >= 1:
                    for c in range(8):
                        acts.setdefault(2 + 2 * c, []).append(lambda t=t, c=c: outproj_e(t - 1, c))
                cnt = [0]

                def hook():
                    k = cnt[0]
                    cnt[0] += 1
                    for fn_ in acts.get(k, []):
                        fn_()

                QTt = QTb[t % 2]
                rQt = rQTb[t % 2]
                for hf in range(2):
                    hs = slice(hf * 64, (hf + 1) * 64)
                    if t == 0:
                        for bb in range(2):
                            kts = [dict(kT=KT[hs, bb * 256 + i * 128:bb * 256 + (i + 1) * 128], v=VT[:, bb * 2 + i, hs], nk=128, reads=[rKT, rVT])
                                   for i in range(2)]
                            attention(QTt[hs, bb * 256:(bb + 1) * 256], hf, 256, kts, MB[hs, bb * 256:(bb + 1) * 256], rQt, rMBt[t], PT, rPT, ptrot,
                                      hook=hook, sb=sb_e, pre_n=1)
                    else:
                        b = t - 1
                        kts = []
                        for kt in range(8):
                            ko = 512 + (8 * b + 2 * kt) * 64
                            bias = [(identb[:], TB[:, hf, (15 - 2 * kt) * 64:(15 - 2 * kt) * 64 + 512], [rTB, rConst]),
                                    (SEL[:, kt * 128:(kt + 1) * 128], RMB[:, b * 512:(b + 1) * 512], [rConst])]
                            kts.append(dict(kT=KT[hs, ko:ko + 128], v=VT[:, 4 + 4 * b + kt, hs], nk=128, reads=[rKT, rVT], bias=bias))
                        for i in range(2):
                            kts.append(dict(kT=KTC[hs, i * 128:(i + 1) * 128], v=VTC[:, i, hs], nk=128, reads=[rKTC, rVTC]))
                        attention(QTt[hs, :], hf, 512, kts, MB[hs, t * 512:(t + 1) * 512], rQt, rMBt[t], PT, rPT, ptrot,
                                  hook=hook, sb=sb_e, pre_n=1)
                assert cnt[0] == nsteps, (cnt[0], nsteps)
            for c in range(8):
                outproj_e(4, c)
        pend2 = S.fence_ops(region)
        del region[:]
        for r in flatG + g_rW:
            r.pend = list(pend2)


    c_ropeA = din("c_ropeA", [128, 64])
    c_ropeB = din("c_ropeB", [128, 128])
    ROPA = sb("ROPA", [128, 64], F32)
    ROPB = sb("ROPB", [128, 128], F32)
    GQK = sb("GQK", [128, 4], F32)
    GQS = sb("GQS", [128, 2], F32)
    GKR = sb("GKR", [128, 2, 64], F32)
    S.dma("sp", lambda e: e.dma_start(out=ROPA[:], in_=c_ropeA), [], [rConst])
    S.dma("sp", lambda e: e.dma_start(out=ROPB[:], in_=c_ropeB), [], [rConst])
    for o_ in range(2):
        for hf_ in range(2):
            S.dma("sp", lambda e, o_=o_, hf_=hf_: e.dma_start(out=GQK[hf_ * 64:(hf_ + 1) * 64, 2 * o_:2 * o_ + 1],
                                                         in_=g_qnorm[o_].rearrange("(d one) -> d one", one=1),
                                                         allow_slow_non_contiguous=True), [], [rConst])
            S.dma("sp", lambda e, o_=o_, hf_=hf_: e.dma_start(out=GQK[hf_ * 64:(hf_ + 1) * 64, 2 * o_ + 1:2 * o_ + 2],
                                                         in_=g_knorm[o_].rearrange("(d one) -> d one", one=1),
                                                         allow_slow_non_contiguous=True), [], [rConst])
        S.dma("sp", lambda e, o_=o_: e.dma_start(out=GKR[:, o_, :], in_=g_knorm[o_:o_ + 1, :].to_broadcast([128, 64])), [], [rConst])
    for o_ in range(2):
        S.op("dve", lambda e, o_=o_: e.tensor_scalar(out=GQS[:, o_:o_ + 1], in0=GQK[:, 2 * o_:2 * o_ + 1], scalar1=0.125, scalar2=None, op0=ALU.mult),
             [rConst], [rConst])

    def odd_mixer(l, nocc):
        o_ = l // 2
        H2 = ARB[:, 0:20480].rearrange("p (c n) -> p c n", c=8)
        rH = g_rH
        for t in range(2):
            norm_mod(1, t, H2[:, :, t * 512:(t + 1) * 512], rH[t])
        pend0 = S.fence_ops(flatG + g_rW)
        del region[:]
        gate_s = 1
        wq = w_qkv_c[o_]
        a = M0
        KTP = ARB[:, a:a + 1024].rearrange("p (c n) -> p c n", c=2); a += 1024
        VTP = ARB[:, a:a + 1024].rearrange("p (t f) -> p t f", f=256); a += 1024
        SQb = ARB[:, a:a + 512]; a += 512
        KNb = ARB[:, a:a + 512]; a += 512
        a_common = a
        WK2 = ARB[:, a:a + 2048].rearrange("p (k n) -> p k n", k=8); a += 2048
        WV2 = ARB[:, a:a + 2048].rearrange("p (k n) -> p k n", k=8); a += 2048
        KTS = ARB[:, a:a + 1024].rearrange("p (b n) -> p b n", b=2); a += 1024
        VST = ARB[:, a:a + 1024].rearrange("p (b t f) -> p b t f", b=2, t=2); a += 1024
        RT1 = ARF[:, 0:512]
        RT2 = ARF[:, 512:1024]
        KS = ARF[:, 1024:1280]
        VS = ARF[:, 1280:1792].rearrange("p (t f) -> p t f", f=256)
        SS4 = ARF[:, 1792:1796]
        CS = ARF[:, 1800:2056].rearrange("p (t f) -> p t f", f=128)
        rKTP = newres("KTP", pend0); rVTP = newres("VTP", pend0); rSQb = newres("SQb", pend0); rKNb = newres("KNb", pend0)
        rWA = newres("WA", pend0); rKTS = [newres("KTS%d" % i, pend0) for i in range(2)]
        rVST = [newres("VST%d" % i, pend0) for i in range(2)]
        rRT1 = newres("RT1", pend0); rRT2 = newres("RT2", pend0); rKS = newres("KS", pend0); rVS = newres("VS", pend0)
        rSS4 = newres("SS4", pend0); rCS = newres("CS", pend0)

        def head_norm_rope(P, gcol, t, dst, rdst):
            S.op("act", lambda e: e.activation(out=SQb, in_=P, func=AF.Square), [rPS[PJ]], [rSQb])
            pss = sbank.next()
            S.op("pe", lambda e, pss=pss: e.matmul(PS[pss][:], bdb[:], SQb, start=True, stop=True), [rSQb, rConst], [rPS[pss]])
            rstd_from_ps(pss, 512, 1.0 / 64, [])
            if t == 0:
                S.op("dve", lambda e: e.scalar_tensor_tensor(out=dst, in0=P, scalar=gcol, in1=RSTD[:], op0=ALU.mult, op1=ALU.mult),
                     [rPS[PJ], rRSTD, rConst], [rdst])
                return
            S.op("dve", lambda e: e.scalar_tensor_tensor(out=KNb, in0=P, scalar=gcol, in1=RSTD[:], op0=ALU.mult, op1=ALU.mult),
                 [rPS[PJ], rRSTD, rConst], [rKNb])
            pr = sbank.next()
            S.op("pe", lambda e, pr=pr: e.matmul(PS[pr][:], rotb[:], KNb, start=True, stop=True), [rKNb, rConst], [rPS[pr]])
            r0 = 8 * (t - 1)
            v3 = lambda ap: ap.rearrange("p (r c) -> p r c", c=64)
            CAb = ROPA[:, r0:r0 + 8].unsqueeze(2).to_broadcast([128, 8, 64])
            SAb = ROPA[:, 32 + r0:32 + r0 + 8].unsqueeze(2).to_broadcast([128, 8, 64])
            CBb = ROPB[:, 0:64].unsqueeze(1).to_broadcast([128, 8, 64])
            SBb = ROPB[:, 64:128].unsqueeze(1).to_broadcast([128, 8, 64])
            S.op("dve", lambda e: e.tensor_tensor(out=v3(RT1), in0=v3(KNb), in1=CAb, op=ALU.mult), [rKNb, rConst], [rRT1])
            S.op("pool", lambda e: e.tensor_tensor(out=v3(RT1), in0=v3(RT1), in1=CBb, op=ALU.mult), [rRT1, rConst], [rRT1])
            S.op("dve", lambda e, pr=pr: e.tensor_tensor(out=v3(RT2), in0=v3(PS[pr][:]), in1=SAb, op=ALU.mult), [rPS[pr], rConst], [rRT2])
            S.op("pool", lambda e: e.tensor_tensor(out=v3(RT2), in0=v3(RT2), in1=SBb, op=ALU.mult), [rRT2, rConst], [rRT2])
            S.op("pool", lambda e: e.tensor_tensor(out=dst, in0=RT1, in1=RT2, op=ALU.add), [rRT1, rRT2], [rdst])

        S.dma("pool", lambda e: e.dma_start(out=WK2, in_=wq[:, 1024:1280].rearrange("(k p) n -> p k n", p=128)), [], [rWA])
        S.dma("pool", lambda e: e.dma_start(out=WV2, in_=wq[:, 1280:1536].rearrange("(k p) n -> p k n", p=128)), [], [rWA])
        gk = GQK[:, 2 * o_ + 1:2 * o_ + 2]
        nk_ = 0
        for t in range(5):
            for kc2 in range(2):
                for kc in range(8):
                    S.op("pe", lambda e, kc=kc, kc2=kc2, t=t: e.matmul(PS[PJ][:], WK2[:, kc, kc2 * 128:(kc2 + 1) * 128], H2[:, kc, t * 512:(t + 1) * 512],
                                                                    start=(kc == 0), stop=(kc == 7)), [rWA, rH[t]], [rPS[PJ]])
                if t == 0:
                    head_norm_rope(PS[PJ][:], gk, 0, KTP[:, kc2, :], rKTP)
                else:
                    bi = nk_ % 2
                    nk_ += 1
                    head_norm_rope(PS[PJ][:], gk, t, KTS[:, bi, :], rKTS[bi])
                    S.dma("sp", lambda e, bi=bi, kc2=kc2, t=t: e.dma_start(
                        out=ccKi.ap()[kc2 * 128:(kc2 + 1) * 128, (t - 1) * 512:t * 512], in_=KTS[:, bi, :]), [rKTS[bi], rCCo], [rCCi])
            if t + 2 < 5:
                norm_mod(1, t + 2, H2[:, :, (t + 2) * 512:(t + 3) * 512], rH[t + 2])
        nv_ = 0
        for pr2 in range(10):
            pb2 = sbank.next()
            for q in range(2):
                ti = pr2 * 2 + q
                for kc in range(8):
                    S.op("pe", lambda e, kc=kc, q=q, ti=ti, pb2=pb2: e.matmul(PS[pb2][:, q * 256:(q + 1) * 256], H2[:, kc, ti * 128:(ti + 1) * 128], WV2[:, kc, :],
                                                                          start=(kc == 0), stop=(kc == 7)), [rWA, rH[ti // 4]], [rPS[pb2]])
            if pr2 < 2:
                S.op("act", lambda e, pb2=pb2: e.activation(out=VS, in_=PS[pb2][:].rearrange("p (t f) -> p t f", f=256), func=AF.Copy), [rPS[pb2]], [rVS])
                S.op("dve", lambda e, pr2=pr2: e.tensor_copy(out=VTP[:, pr2 * 2:pr2 * 2 + 2, :], in_=VS), [rVS], [rVTP])
                S.dma("sp", lambda e, pr2=pr2: e.dma_start(out=o_atv[pr2, o_].rearrange("(t p) f -> p t f", p=128), in_=VS), [rVS], [])
            else:
                bi = nv_ % 2
                nv_ += 1
                S.op("dve", lambda e, pb2=pb2, bi=bi: e.tensor_copy(out=VST[:, bi], in_=PS[pb2][:].rearrange("p (t f) -> p t f", f=256)), [rPS[pb2]], [rVST[bi]])
                S.dma("sp", lambda e, bi=bi, pr2=pr2: e.dma_start(
                    out=ccVi.ap()[(pr2 - 2) * 256:(pr2 - 1) * 256, :].rearrange("(t p) f -> p t f", p=128), in_=VST[:, bi]), [rVST[bi], rCCo], [rCCi])
        for ti in range(4):
            pb2 = sbank.next()
            for kc in range(8):
                S.op("pe", lambda e, kc=kc, ti=ti, pb2=pb2: e.matmul(PS[pb2][:, 0:256], H2[:, kc, ti * 128:(ti + 1) * 128], WK2[:, kc, :],
                                                                 start=(kc == 0), stop=(kc == 7)), [rWA, rH[0]], [rPS[pb2]])
            S.op("act", lambda e, pb2=pb2: e.activation(out=KS, in_=PS[pb2][:, 0:256], func=AF.Square), [rPS[pb2]], [rKS])
            S.op("dve", lambda e: e.tensor_reduce(out=SS4, in_=KS.rearrange("p (h d) -> p h d", d=64), axis=AX.X, op=ALU.add), [rKS], [rSS4])
            S.op("act", lambda e: e.activation(out=SS4, in_=SS4, func=AF.Sqrt, bias=epsT[:], scale=1.0 / 64), [rSS4, rConst], [rSS4])
            S.op("dve", lambda e: e.reciprocal(out=SS4, in_=SS4), [rSS4], [rSS4])
            S.op("dve", lambda e, pb2=pb2: e.tensor_tensor(out=KS.rearrange("p (h d) -> p h d", d=64), in0=PS[pb2][:, 0:256].rearrange("p (h d) -> p h d", d=64),
                                                  in1=SS4.unsqueeze(2).to_broadcast([128, 4, 64]), op=ALU.mult), [rPS[pb2], rSS4, rKS], [rKS])
            S.op("dve", lambda e: e.tensor_tensor(out=KS.rearrange("p (h d) -> p h d", d=64), in0=KS.rearrange("p (h d) -> p h d", d=64),
                                                  in1=GKR[:, o_, :].unsqueeze(1).to_broadcast([128, 4, 64]), op=ALU.mult), [rKS, rConst], [rKS])
            S.dma("sp", lambda e, ti=ti: e.dma_start(out=o_atk[ti // 2, o_][(ti % 2) * 128:(ti % 2 + 1) * 128, :], in_=KS), [rKS], [])
        cc_allgather(ccKi, ccKo, nocc)
        cc_allgather(ccVi, ccVo, nocc)

        keepers = [rKTP, rVTP, rSQb, rKNb, rRT1, rRT2, rCS]
        pend1 = S.fence_ops([r for r in region if r not in keepers] + [rSQ])
        del region[:]
        region.extend(keepers)
        a = a_common
        KTF = ARB[:, a:a + 4352]; a += 4352
        VTF = ARB[:, a:a + 6528].rearrange("p (t f) -> p t f", f=192); a += 6528
        VPA = ARB[:, a:a + 1536].rearrange("p (t k f) -> p t k f", t=4, k=2); a += 1536
        WQj = ARB[:, a:a + 2048].rearrange("p (b k n) -> p b k n", b=2, k=8); a += 2048
        WOj = ARB[:, a:a + 2048].rearrange("p (b n) -> p b n", b=2); a += 2048
        QZ = ARB[:, a:a + 2048].rearrange("p (u h n) -> p u h n", u=2, h=2); a += 2048
        MJ = ARB[:, a:a + 1024].rearrange("p (u n) -> p u n", u=2); a += 1024
        PT = ARB[:, a:a + 1536].rearrange("p (b n) -> p b n", b=3); a += 1536
        SQ2 = SQb
        KN2 = KNb
        QRAW = ARF[:, 2560:3072]
        assert a <= 47104, a
        rKTF = newres("KTF", pend1); rVTF = newres("VTF", pend1); rVPA = newres("VPA", pend1)
        rWj = [newres("Wj%d" % i, pend1) for i in range(2)]
        rQZ = [newres("QZ%d" % i, pend1) for i in range(2)]
        rMJ = [newres("MJ%d" % i, pend1) for i in range(2)]
        rPT = [newres("PT%d" % i, pend1) for i in range(3)]
        rSQ2 = rSQb; rKN2 = rKNb
        rQRAW = newres("QRAW", pend1)
        ptrot = Rot([0, 1, 2])
        gqs = GQS[:, o_:o_ + 1]
        BD_B, ROT_B, OPJ, SWP_B, KW_B = 5, 6, 6, 5, 7
        KEEPWARM = True
        KWN = 128
        MSE = "dve" if os.environ.get("NOPOOLMS") else "pool"
        for u2 in range(2):
            S.op(MSE, lambda e, u2=u2: e.memset(QZ[64:128, u2, 0, :], 0.0), [], [rQZ[u2]])
            S.op(MSE, lambda e, u2=u2: e.memset(QZ[0:64, u2, 1, :], 0.0), [], [rQZ[u2]])
        S.op(MSE, lambda e: e.memset(VTF[:, :, 64:128], 1.0), [], [rVTF])
        S.op(MSE, lambda e: e.memset(VPA[:, :, :, 64:128], 1.0), [], [rVPA])
        for kc2 in range(2):
            S.op("dve", lambda e, kc2=kc2: e.tensor_copy(out=VPA[:, :, kc2, 0:64], in_=VTP[:, :, (2 * kc2) * 64:(2 * kc2 + 1) * 64]), [rVTP], [rVPA])
            S.op("dve", lambda e, kc2=kc2: e.tensor_copy(out=VPA[:, :, kc2, 128:192], in_=VTP[:, :, (2 * kc2 + 1) * 64:(2 * kc2 + 2) * 64]), [rVTP], [rVPA])
        S.op("dve", lambda e: e.memset(TMPF[:, 0, :], 0.0), [], [rTMPF[0]])

        units = [(kc2, j, t) for kc2 in range(2) for j in range(4) for t in range(5)]
        OSTG = float(os.environ.get("ODDSTG", "99"))
        if OSTG <= 0:
            return

        def load_kv(kc2):
            S.dma("sp", lambda e: e.dma_start(out=CS, in_=cak[o_][:, kc2 * 128:(kc2 + 1) * 128].rearrange("(t p) f -> p t f", p=128)), [], [rCS])
            for tt in range(2):
                S.op("pe", lambda e, tt=tt: e.transpose(PS[PJ][:, tt * 128:(tt + 1) * 128], CS[:, tt, :], ident[:]), [rCS, rConst], [rPS[PJ]])
            S.op("act", lambda e: e.activation(out=KTF[:, 0:256], in_=PS[PJ][:, 0:256], func=AF.Copy), [rPS[PJ]], [rKTF])
            for rk in range(2):
                S.dma("sp", lambda e, rk=rk: e.dma_start(out=KTF[:, 256 + rk * 2048:256 + (rk + 1) * 2048],
                                                     in_=ccKo.ap()[rk * 256 + kc2 * 128:rk * 256 + (kc2 + 1) * 128, :]), [rCCo], [rKTF])
            for g2 in range(2):
                c0 = kc2 * 128 + g2 * 64
                S.dma("pool", lambda e, g2=g2, c0=c0: e.dma_start(out=VTF[:, 0:2, g2 * 128:g2 * 128 + 64],
                                                              in_=cav[o_][:, c0:c0 + 64].rearrange("(t p) f -> p t f", p=128)), [], [rVTF])
                S.dma("sp", lambda e, g2=g2, c0=c0: e.dma_start(out=VTF[:, 2:34, g2 * 128:g2 * 128 + 64],
                                                            in_=ccVo.ap()[:, c0:c0 + 64].rearrange("(t p) f -> p t f", p=128)), [rCCo], [rVTF])

        def load_w(kc2, j, bj):
            for two in range(2):
                c0 = kc2 * 512 + two * 256 + j * 64
                S.dma("pool", lambda e, two=two, c0=c0: e.dma_start(
                    out=WQj[:, bj, :, two * 64:(two + 1) * 64], in_=wq[:, c0:c0 + 64].rearrange("(k p) n -> p k n", p=128)), [], [rWj[bj]])
                r0_ = (8 * kc2 + 4 * two + j) * 64
                S.dma("pool", lambda e, two=two, r0_=r0_: e.dma_start(
                    out=WOj[two * 64:(two + 1) * 64, bj, :], in_=w_out_c[o_][r0_:r0_ + 64, :]), [], [rWj[bj]])

        def wbuf(ui):
            kc2, j, t = units[ui]
            return (kc2 * 4 + j) % 2

        def stage0(ui):
            kc2, j, t = units[ui]
            bj = wbuf(ui)
            if t == 0:
                if j == 0:
                    load_kv(kc2)
                load_w(kc2, j, bj)
            for kc in range(8):
                S.op("pe", lambda e, kc=kc: e.matmul(PS[PJ][:], WQj[:, bj, kc, :], H2[:, kc, t * 512:(t + 1) * 512],
                                                 start=(kc == 0), stop=(kc == 7)), [rWj[bj], rH[t]], [rPS[PJ]])
            S.op("dve", lambda e: e.tensor_copy(out=QRAW, in_=PS[PJ][:]), [rPS[PJ]], [rQRAW])
            S.op("pool", lambda e: e.tensor_tensor(out=SQ2, in0=QRAW, in1=QRAW, op=ALU.mult), [rQRAW], [rSQ2])

        def stage1(ui):
            kc2, j, t = units[ui]
            u2 = ui % 2
            S.op("pe", lambda e: e.matmul(PS[BD_B][:], bdb[:], SQ2, start=True, stop=True), [rSQ2, rConst], [rPS[BD_B]])
            S.op("act", lambda e: e.activation(out=RSTD[:], in_=PS[BD_B][:], func=AF.Ln, bias=epsT[:], scale=1.0 / 64),
                 [rPS[BD_B], rConst], [rRSTD])
            S.op("act", lambda e: e.activation(out=RSTD[:], in_=RSTD[:], func=AF.Exp, scale=-0.5), [rRSTD], [rRSTD])
            if t == 0:
                for hf in range(2):
                    hs = slice(hf * 64, (hf + 1) * 64)
                    S.op("dve", lambda e, hs=hs, hf=hf: e.scalar_tensor_tensor(out=QZ[hs, u2, hf, :], in0=QRAW[hs, :], scalar=gqs[hs, :], in1=RSTD[hs, :],
                                                                        op0=ALU.mult, op1=ALU.mult), [rQRAW, rRSTD, rConst], [rQZ[u2]])
            else:
                S.op("dve", lambda e: e.scalar_tensor_tensor(out=KN2, in0=QRAW, scalar=gqs, in1=RSTD[:], op0=ALU.mult, op1=ALU.mult),
                     [rQRAW, rRSTD, rConst], [rKN2])

        def stage2(ui):
            kc2, j, t = units[ui]
            u2 = ui % 2
            if t == 0:
                return
            S.op("pe", lambda e: e.matmul(PS[ROT_B][:], rotb[:], KN2, start=True, stop=True), [rKN2, rConst], [rPS[ROT_B]])
            r0 = 8 * (t - 1)
            v3 = lambda ap: ap.rearrange("p (r c) -> p r c", c=64)
            CAb = ROPA[:, r0:r0 + 8].unsqueeze(2).to_broadcast([128, 8, 64])
            SAb = ROPA[:, 32 + r0:32 + r0 + 8].unsqueeze(2).to_broadcast([128, 8, 64])
            CBb = ROPB[:, 0:64].unsqueeze(1).to_broadcast([128, 8, 64])
            SBb = ROPB[:, 64:128].unsqueeze(1).to_broadcast([128, 8, 64])
            S.op("dve", lambda e: e.tensor_tensor(out=v3(RT1), in0=v3(KN2), in1=CAb, op=ALU.mult), [rKN2, rConst], [rRT1])
            S.op("pool", lambda e: e.tensor_tensor(out=v3(RT1), in0=v3(RT1), in1=CBb, op=ALU.mult), [rRT1, rConst], [rRT1])
            S.op("dve", lambda e: e.tensor_tensor(out=v3(RT2), in0=v3(PS[ROT_B][:]), in1=SAb, op=ALU.mult), [rPS[ROT_B], rConst], [rRT2])
            S.op("pool", lambda e: e.tensor_tensor(out=v3(RT2), in0=v3(RT2), in1=SBb, op=ALU.mult), [rRT2, rConst], [rRT2])
            for hf in range(2):
                hs = slice(hf * 64, (hf + 1) * 64)
                S.op("pool", lambda e, hs=hs, hf=hf: e.tensor_tensor(out=QZ[hs, u2, hf, :], in0=RT1[hs, :], in1=RT2[hs, :], op=ALU.add),
                     [rRT1, rRT2], [rQZ[u2]])

        def outproj_c(ui, c):
            kc2, j, t = units[ui]
            u2 = ui % 2
            bj = wbuf(ui)
            v = vsel(t)
            S.op("pe", lambda e: e.matmul(PS[OPJ][:], WOj[:, bj, c * 128:(c + 1) * 128], MJ[:, u2, :], start=True, stop=True),
                 [rWj[bj], rMJ[u2]], [rPS[OPJ]])
            S.op("dve", lambda e: e.scalar_tensor_tensor(
                out=X[:, c, t * 512:(t + 1) * 512], in0=PS[OPJ][:], scalar=HG[:, gate_s, c, v:v + 1], in1=X[:, c, t * 512:(t + 1) * 512],
                op0=ALU.mult, op1=ALU.add), [rPS[OPJ], rGS, rX[t][c]], [rX[t][c]])

        def attention_aug(qz, hf, nq, ktiles, out_ap, rQ_, rOut, hook):
            po = obank.next()
            hs = slice(hf * 64, (hf + 1) * 64)
            ss = slice((1 - hf) * 64, (2 - hf) * 64)
            n = len(ktiles)
            banks = {}

            def s_mm(i):
                kt = ktiles[i]
                pss = sbank.next()
                banks[i] = pss
                S.op("pe", lambda e, pss=pss, kt=kt: e.matmul(PS[pss][:, 0:nq], kt["kT"], qz, start=True, stop=True),
                     [rQ_] + kt["reads"], [rPS[pss]])

            pre = min(2, n)
            for i in range(pre):
                s_mm(i)
            for i, kt in enumerate(ktiles):
                pss = banks[i]
                pt = ptrot.next()
                S.op("act", lambda e, pss=pss, pt=pt: e.activation(out=PT[:, pt, 0:nq], in_=PS[pss][:, 0:nq], func=AF.Exp), [rPS[pss]], [rPT[pt]])
                if i + pre < n:
                    s_mm(i + pre)
                S.op("pe", lambda e, pt=pt, kt=kt, i=i: e.matmul(PS[po][:, 0:nq], kt["v"], PT[:, pt, 0:nq], start=(i == 0), stop=(i == n - 1)),
                     [rPT[pt]] + kt["reads"], [rPS[po]])
                if nq == 512 and KEEPWARM:
                    S.op("pe", lambda e, pt=pt: e.matmul(PS[KW_B][:, 0:KWN], identb[:], PT[:, pt, 0:KWN], start=True, stop=True),
                         [rPT[pt], rConst], [rPS[KW_B]])
                hook()
            S.op("dve", lambda e: e.reciprocal(out=TMPF[ss, 0, 0:nq], in_=PS[po][ss, 0:nq]), [rPS[po]], [rTMPF[0]])

            def fin():
                psw = SWP_B
                S.op("pe", lambda e: e.matmul(PS[psw][:, 0:nq], swpf[:], TMPF[:, 0, 0:nq], start=True, stop=True), [rTMPF[0], rConst], [rPS[psw]])
                S.op("act", lambda e: e.activation(out=TMPF[hs, 1, 0:nq], in_=PS[psw][hs, 0:nq], func=AF.Copy), [rPS[psw]], [rTMPF[1]])
                S.op("dve", lambda e: e.tensor_tensor(out=out_ap, in0=PS[po][hs, 0:nq], in1=TMPF[hs, 1, 0:nq], op=ALU.mult),
                     [rPS[po], rTMPF[1]], [rOut])
            return fin

        pending_fin = []

        def needs_kv(ui):
            return ui < len(units) and units[ui][1] == 0 and units[ui][2] == 0

        for ui, (kc2, j, t) in enumerate(units):
            u2 = ui % 2
            if needs_kv(ui):
                stage0(ui)
                if OSTG <= 0.3:
                    return
                stage1(ui)
                if OSTG <= 0.6:
                    return
                stage2(ui)
            if OSTG <= 1:
                return
            if OSTG < 50 and ui >= OSTG - 1:
                return
            nsteps = 8 if t == 0 else 68
            acts = {}
            nxt = ui + 1 < len(units) and not needs_kv(ui + 1)
            if nxt:
                acts.setdefault(0, []).append(lambda: stage0(ui + 1))
                acts.setdefault(16 if t > 0 else nsteps // 3, []).append(lambda: stage1(ui + 1))
                acts.setdefault(28 if t > 0 else (2 * nsteps) // 3, []).append(lambda: stage2(ui + 1))
            if ui >= 1:
                for c in range(8):
                    st_ = min(3 + c, 7) if t == 0 else 36 + 2 * c
                    acts.setdefault(st_, []).append(lambda c=c: outproj_c(ui - 1, c))
            cnt = [0]
            since = [99]

            def hook():
                k = cnt[0]
                cnt[0] += 1
                since[0] += 1
                if pending_fin and since[0] >= 9:
                    pending_fin.pop(0)()
                for fn_ in acts.get(k, []):
                    fn_()

            for hf in range(2):
                hs = slice(hf * 64, (hf + 1) * 64)
                if t == 0:
                    for bb in range(2):
                        kts = [dict(kT=KTP[:, kc2, bb * 256 + i * 128:bb * 256 + (i + 1) * 128],
                                    v=VPA[:, bb * 2 + i, kc2, hf * 64:hf * 64 + 128], nk=128, reads=[rKTP, rVPA]) for i in range(2)]
                        while pending_fin:
                            pending_fin.pop(0)()
                        pending_fin.append(attention_aug(QZ[:, u2, hf, bb * 256:(bb + 1) * 256], hf, 256, kts, MJ[hs, u2, bb * 256:(bb + 1) * 256], rQZ[u2], rMJ[u2], hook))
                        since[0] = 0
                else:
                    kts = [dict(kT=KTF[:, i * 128:(i + 1) * 128], v=VTF[:, i, hf * 64:hf * 64 + 128], nk=128, reads=[rKTF, rVTF]) for i in range(34)]
                    while len(pending_fin) > 1:
                        pending_fin.pop(0)()
                    pending_fin.append(attention_aug(QZ[:, u2, hf, :], hf, 512, kts, MJ[hs, u2, :], rQZ[u2], rMJ[u2], hook))
                    since[0] = 0
            assert cnt[0] == nsteps, (cnt[0], nsteps)
        while pending_fin:
            pending_fin.pop(0)()
        for c in range(8):
            outproj_c(len(units) - 1, c)
        pend2 = S.fence_ops(region)
        del region[:]
        for r in flatG + g_rW + [rSQ]:
            r.pend = list(pend2)

    dbg_n = [0]

    def dumpX(tag):
        if not debug:
            return
        o = dout("dbgX_%s" % tag, [128, NCH * T])
        allr = [r for t in range(5) for r in rX[t]]
        S.dma("sp", lambda e: e.dma_start(out=o, in_=X[:].rearrange("p c n -> p (c n)")), allr, [])

    def dumpS(tag, ap, n, reads):
        if not debug:
            return
        o = dout("dbgS_%s" % tag, [128, n])
        S.dma("sp", lambda e: e.dma_start(out=o, in_=ap), reads, [])

    dumpX("load")
    for l in range(n_layers):
        ada(l)
        if l == 0:
            dumpS("mod", MOD[:].rearrange("p m v -> p (m v)"), 144, [rMOD])
            dumpS("gs", GS[:].rearrange("p s c v -> p (s c v)"), 48, [rGS])
            dumpS("hg", HG[:].rearrange("p s c v -> p (s c v)"), 48, [rGS])
        ffn(l, 0, 0)
        if l == 0:
            dumpX("ffn0")
        if mix:
            if l % 2 == 0:
                even_mixer(l, mix == 2)
                if l == 0:
                    dumpX("mix0")
            else:
                odd_mixer(l, mix == 2)
                if l == 1:
                    dumpX("mix1")
        ffn(l, 2, 1)
    final_out()

    S.emit(stack)
    stack.close()
    return nc


def _consts(core):
    half = core % 2
    ident = np.eye(128, dtype=np.float32)
    rot = np.zeros((128, 128), np.float32)
    for m in range(128):
        if m % 32 < 16:
            rot[m + 16, m] = -1.0
        else:
            rot[m - 16, m] = 1.0
    bd = np.zeros((128, 128), np.float32)
    bd[:64, :64] = 1.0
    bd[64:, 64:] = 1.0
    t = np.arange(TS)
    row = (32 * half + t // 64).astype(np.float32)
    col = (t % 64).astype(np.float32)
    inv = (10000.0 ** (-np.arange(16, dtype=np.float32) / 16)).astype(np.float32)
    cos = np.zeros((128, TS), np.float32)
    sin = np.zeros((128, TS), np.float32)
    for p in range(128):
        d = p % 64
        pos = row if d < 32 else col
        ang = pos * inv[d % 16]
        cos[p] = np.cos(ang)
        sin[p] = np.sin(ang)
    cq = np.arange(64)
    cstart = np.clip(cq - 8, 0, 48)
    ck = np.arange(64)
    valid = (ck[None, :] >= cstart[:, None]) & (ck[None, :] < cstart[:, None] + 16)
    cm = np.where(valid.T, 0.0, NEG).astype(np.float32)
    cmt = np.tile(cm, (2, 8))
    sel = np.zeros((16, 8 * 128), np.float32)
    for kt in range(8):
        for kr in range(2):
            sel[2 * kt + kr, kt * 128 + kr * 64:kt * 128 + (kr + 1) * 64] = 1.0
    rm = np.full((16, 4 * 512), NEG, np.float32)
    for b in range(4):
        for qr in range(8):
            i = 8 * b + qr
            r = 32 * half + i
            rs = min(max(r - 4, 0), 56)
            for k in range(16):
                keyrow = 32 * half - 4 + 8 * b + k
                if rs <= keyrow <= rs + 7:
                    rm[k, b * 512 + qr * 64:b * 512 + (qr + 1) * 64] = 0.0
    invb = np.zeros((4, 6, 2, 8), np.float32)
    for g in range(4):
        w = 2 << g
        for si in range(6):
            for ed in range(2):
                for j in range(8):
                    if si < 2:
                        L = 256
                        tt = j if ed == 0 else 248 + j
                    else:
                        L = 4096
                        tt = half * 2048 + (si - 2) * 512 + (j if ed == 0 else 504 + j)
                    lo = min(max(tt - w // 2, 0), L - 1)
                    hi = min(max(tt + (w - 1 - w // 2), 0), L - 1)
                    invb[g, si, ed, j] = 1.0 / (hi - lo + 1)
    invb = np.tile(invb.reshape(1, -1), (128, 1)).astype(np.float32)
    hv = np.zeros((128, 2), np.float32)
    hv[:, 0] = 1.0 if half == 1 else 0.0
    hv[:, 1] = 1.0 if half == 0 else 0.0
    ropeA = np.ones((128, 64), np.float32)
    ropeB = np.ones((128, 128), np.float32)
    for p in range(128):
        d = p % 64
        f = inv[d % 16]
        if d < 32:
            ang = (32 * half + np.arange(32, dtype=np.float32)) * f
            ropeA[p, 0:32] = np.cos(ang)
            ropeA[p, 32:64] = np.sin(ang)
        else:
            ang = np.arange(64, dtype=np.float32) * f
            ropeB[p, 0:64] = np.cos(ang)
            ropeB[p, 64:128] = np.sin(ang)
    swp = np.zeros((128, 128), np.float32)
    for m in range(128):
        swp[(m + 64) % 128, m] = 1.0
    return dict(c_ident=ident, c_rot=rot, c_bd=bd, c_swp=swp, c_cos=cos, c_sin=sin,
                c_rm=rm, c_sel=sel, c_cmt=cmt, c_invb=invb, c_hv=hv, c_ropeA=ropeA, c_ropeB=ropeB)


_NC_CACHE = {}


def kernel(x_prompt, x_sample, cache_nb_k, cache_nb_v, cache_attn_k, cache_attn_v, c, c_ctx,
           w_mod, b_mod, g_norm, w_ffn_in, w_ffn_out, w_in_ab, w_pool, pool_scale, nb_rpb, w_out_ab,
           w_qkv_c, g_qnorm, g_knorm, w_out_c, g_final):
    f = lambda a: np.ascontiguousarray(np.asarray(a, dtype=np.float32))
    x_prompt, x_sample = f(x_prompt), f(x_sample)
    if "nc" not in _NC_CACHE:
        _NC_CACHE["nc"] = build_program(DEPTH, 1)
    nc = _NC_CACHE["nc"]
    rpbP = np.zeros((2, 8, 25, 128), np.float32)
    rpbP[:, :, 5:20, 48:79] = f(nb_rpb)[:, :, ::-1, ::-1]
    shared = dict(w_mod=f(w_mod), b_mod=f(b_mod), g_norm=f(g_norm), w_ffn_in=f(w_ffn_in), w_ffn_out=f(w_ffn_out),
                  w_in_ab=f(w_in_ab), w_pool=f(w_pool), pool_scale=f(pool_scale), rpbP=rpbP, w_out_ab=f(w_out_ab),
                  w_qkv_c=f(w_qkv_c), g_qnorm=f(g_qnorm), g_knorm=f(g_knorm), w_out_c=f(w_out_c), g_final=f(g_final))
    in_maps = []
    for core in range(8):
        seq, half = core // 2, core % 2
        xin = np.concatenate([x_prompt[2 * core].reshape(256, D), x_prompt[2 * core + 1].reshape(256, D),
                              x_sample[seq, half * TS:(half + 1) * TS]], axis=0)
        m = dict(shared)
        m.update(xin=np.ascontiguousarray(xin),
                 cvec=np.ascontiguousarray(np.stack([f(c_ctx), f(c)[seq]], 0)),
                 cnk=f(cache_nb_k)[seq].reshape(2, 256, 512), cnv=f(cache_nb_v)[seq].reshape(2, 256, 512),
                 cak=f(cache_attn_k)[seq].reshape(2, 256, 256), cav=f(cache_attn_v)[seq].reshape(2, 256, 256))
        m.update(_consts(core))
        in_maps.append(m)
    res = run_bass_kernel_spmd(nc, in_maps, core_ids=list(range(8)))
    R = res.results
    y_prompt = np.zeros((16, 256, D), np.float32)
    y_sample = np.zeros((4, 4096, D), np.float32)
    nbk = np.zeros((16, 2, 256, 8, 64), np.float32)
    nbv = np.zeros((16, 2, 256, 8, 64), np.float32)
    atk = np.zeros((16, 2, 256, 4, 64), np.float32)
    atv = np.zeros((16, 2, 256, 4, 64), np.float32)
    for core in range(8):
        seq, half = core // 2, core % 2
        yy = R[core]["y"]
        y_prompt[2 * core] = yy[0:256]
        y_prompt[2 * core + 1] = yy[256:512]
        y_sample[seq, half * TS:(half + 1) * TS] = yy[512:]
        nbk[2 * core:2 * core + 2] = R[core]["o_nbk"].reshape(2, 2, 256, 8, 64)
        nbv[2 * core:2 * core + 2] = R[core]["o_nbv"].reshape(2, 2, 256, 8, 64)
        atk[2 * core:2 * core + 2] = R[core]["o_atk"].reshape(2, 2, 256, 4, 64)
        atv[2 * core:2 * core + 2] = R[core]["o_atv"].reshape(2, 2, 256, 4, 64)
    return (y_prompt, y_sample, nbk, nbv, atk, atv)
```

```python
import os
import numpy as np
import ml_dtypes
from contextlib import ExitStack
import concourse.bass as bass
import concourse.mybir as mybir
from concourse.bass_utils import run_bass_kernel_spmd

F32 = mybir.dt.float32
BF16 = mybir.dt.bfloat16
ALU = mybir.AluOpType
AF = mybir.ActivationFunctionType
AX = mybir.AxisListType

D = 1024
NCH = 8
DFF = 2816
NJ = 22
TP = 512
TS = 2048
T = TP + TS
DEPTH = 4
EPS = 1e-6
NEG = -30000.0
PAIRS = [[0, 1], [2, 3], [4, 5], [6, 7]]
TILES = [(i * 512, 512) for i in range(5)]


class Res:
    __slots__ = ("w", "r", "name", "pend")

    def __init__(self, name="", pend=None):
        self.w = None
        self.r = []
        self.name = name
        self.pend = list(pend) if pend else None


class Op:
    __slots__ = ("eng", "fn", "deps", "need", "sem", "val", "is_dma", "idx", "pos")

    def __init__(self, eng, fn, is_dma):
        self.eng = eng
        self.fn = fn
        self.deps = []
        self.need = False
        self.sem = None
        self.val = 0
        self.is_dma = is_dma


class Sched:
    ENGS = ["pe", "act", "dve", "pool", "sp"]
    NDS = 24

    def __init__(self, nc):
        self.nc = nc
        self.ops = {e: [] for e in self.ENGS}
        self.all_dma = []
        self.barrier_op = None
        self.dma_since = []

    def _add(self, op, reads, writes):
        deps = []
        op.pos = len(self.ops[op.eng])
        for r in reads:
            if r.w is not None:
                deps.append(r.w)
            if r.pend:
                deps.extend(r.pend)
        for w in writes:
            if w.w is not None:
                deps.append(w.w)
            deps.extend(w.r)
            if w.pend:
                deps.extend(w.pend)
                w.pend = None
        if self.barrier_op is not None:
            deps.append(self.barrier_op)
        seen = set()
        for d in deps:
            if d is op or id(d) in seen:
                continue
            seen.add(id(d))
            if (not d.is_dma) and (not op.is_dma) and d.eng == op.eng and op.eng == "pe":
                continue
            op.deps.append(d)
            d.need = True
        for r in reads:
            r.r.append(op)
        for w in writes:
            w.w = op
            w.r = []
        self.ops[op.eng].append(op)

    def op(self, eng, fn, reads=(), writes=()):
        o = Op(eng, fn, False)
        self._add(o, reads, writes)
        return o

    def dma(self, q, fn, reads=(), writes=()):
        o = Op(q, fn, True)
        o.need = True
        self._add(o, reads, writes)
        self.all_dma.append(o)
        self.dma_since.append(o)
        return o

    def fence_ops(self, reslist):
        best = {}
        out = []
        seen = set()
        for r in reslist:
            for o in ([r.w] if r.w is not None else []) + list(r.r) + (list(r.pend) if r.pend else []):
                if id(o) in seen:
                    continue
                seen.add(id(o))
                if o.is_dma:
                    out.append(o)
                elif o.eng not in best or o.pos > best[o.eng].pos:
                    best[o.eng] = o
        return out + list(best.values())

    def barrier(self, fn):
        o = Op("dve", fn, False)
        deps = []
        for e in self.ENGS:
            for p in reversed(self.ops[e]):
                if not p.is_dma:
                    deps.append(p)
                    break
        deps.extend(self.dma_since)
        self.dma_since = []
        for d in deps:
            o.deps.append(d)
            d.need = True
        o.need = True
        self.ops["dve"].append(o)
        self.barrier_op = o
        return o

    def check(self):
        done = set()
        ptr = {e: 0 for e in self.ENGS}
        prog = True
        while prog:
            prog = False
            for e in self.ENGS:
                while ptr[e] < len(self.ops[e]):
                    o = self.ops[e][ptr[e]]
                    if all(id(d) in done for d in o.deps):
                        done.add(id(o))
                        ptr[e] += 1
                        prog = True
                    else:
                        break
        stuck = {e: (ptr[e], len(self.ops[e])) for e in self.ENGS if ptr[e] < len(self.ops[e])}
        if stuck:
            for e in stuck:
                o = self.ops[e][ptr[e]]
                print("STUCK", e, ptr[e], "deps not done:", [(d.eng, d.is_dma, self.ops[d.eng].index(d)) for d in o.deps if id(d) not in done])
            raise RuntimeError("deadlock in schedule: %s" % stuck)

    def emit(self, stack):
        nc = self.nc
        self.check()
        esem = {e: stack.enter_context(nc.semaphore("s_" + e)) for e in ["pe", "act", "dve", "pool"]}
        dsem = {q: [stack.enter_context(nc.semaphore("d_%s%d" % (q, i))) for i in range(self.NDS)]
                for q in ["pool", "sp"]}
        for e in self.ENGS:
            cnt = 0
            dcnt = 0
            for o in self.ops[e]:
                if o.is_dma:
                    o.sem = dsem[e][dcnt % self.NDS]
                    o.val = 16 * (dcnt // self.NDS + 1)
                    o.idx = dcnt
                    dcnt += 1
                elif o.need:
                    cnt += 1
                    o.sem = esem[e]
                    o.val = cnt
        engobj = {"pe": nc.tensor, "act": nc.scalar, "dve": nc.vector, "pool": nc.gpsimd, "sp": nc.sync}
        block = stack.enter_context(nc.Block())

        def make(e):
            def body(eng):
                waited = {}
                for o in self.ops[e]:
                    for d in o.deps:
                        k = id(d.sem)
                        if waited.get(k, 0) < d.val:
                            eng.wait_ge(d.sem, d.val)
                            waited[k] = d.val
                    if o.is_dma and o.val > 16:
                        k = id(o.sem)
                        if waited.get(k, 0) < o.val - 16:
                            eng.wait_ge(o.sem, o.val - 16)
                            waited[k] = o.val - 16
                    ins = o.fn(eng)
                    if o.is_dma:
                        ins.then_inc(o.sem, 16)
                    elif o.need:
                        ins.then_inc(o.sem, 1)
                last = {}
                for o in self.ops[e]:
                    if o.is_dma:
                        last[id(o.sem)] = (o.sem, o.val)
                for k, (s, v) in last.items():
                    if waited.get(k, 0) < v:
                        eng.wait_ge(s, v)
            return body

        block.tensor(make("pe"))
        block.scalar(make("act"))
        block.vector(make("dve"))
        block.gpsimd(make("pool"))
        block.sync(make("sp"))


def build_program(n_layers=DEPTH, mix=0, debug=False):
    nc = bass.Bass("TRN2", target_bir_lowering=False)
    stack = ExitStack()
    S = Sched(nc)

    def din(name, shape, dt=F32):
        return nc.dram_tensor(name, list(shape), dt, kind="ExternalInput").ap()

    def dout(name, shape, dt=F32):
        return nc.dram_tensor(name, list(shape), dt, kind="ExternalOutput").ap()

    xin = din("xin", [T, D])
    cvec = din("cvec", [2, D])
    w_mod = din("w_mod", [DEPTH, D, 9 * D])
    b_mod = din("b_mod", [DEPTH, 9 * D])
    g_norm = din("g_norm", [DEPTH, 3, D])
    w_ffn_in = din("w_ffn_in", [DEPTH, 2, D, 2 * DFF])
    w_ffn_out = din("w_ffn_out", [DEPTH, 2, DFF, D])
    w_in_ab = din("w_in_ab", [2, D, 2048])
    w_pool = din("w_pool", [2, 4, 128, 128])
    pool_scale = din("pool_scale", [2, 512])
    rpbP = din("rpbP", [2, 8, 25, 128])
    w_out_ab = din("w_out_ab", [2, D, D])
    w_qkv_c = din("w_qkv_c", [2, D, 1536])
    g_qnorm = din("g_qnorm", [2, 64])
    g_knorm = din("g_knorm", [2, 64])
    w_out_c = din("w_out_c", [2, D, D])
    g_final = din("g_final", [D])
    cnk = din("cnk", [2, 256, 512])
    cnv = din("cnv", [2, 256, 512])
    cak = din("cak", [2, 256, 256])
    cav = din("cav", [2, 256, 256])
    c_ident = din("c_ident", [128, 128])
    c_rot = din("c_rot", [128, 128])
    c_bd = din("c_bd", [128, 128])
    c_swp = din("c_swp", [128, 128])
    c_cos = din("c_cos", [128, TS])
    c_sin = din("c_sin", [128, TS])
    c_rm = din("c_rm", [16, 4 * 512])
    c_sel = din("c_sel", [16, 8 * 128])
    c_cmt = din("c_cmt", [128, 512])
    c_invb = din("c_invb", [128, 4 * 6 * 2 * 8])
    c_hv = din("c_hv", [128, 2])

    y = dout("y", [T, D])
    o_nbk = dout("o_nbk", [2, 2, 256, 512])
    o_nbv = dout("o_nbv", [2, 2, 256, 512])
    o_atk = dout("o_atk", [2, 2, 256, 256])
    o_atv = dout("o_atv", [2, 2, 256, 256])

    ccHi = nc.dram_tensor("ccHi", [2 * D, 256], BF16)
    ccHo = nc.dram_tensor("ccHo", [4 * D, 256], BF16)
    ccKi = nc.dram_tensor("ccKi", [256, TS], BF16)
    ccKo = nc.dram_tensor("ccKo", [512, TS], BF16)
    ccVi = nc.dram_tensor("ccVi", [TS, 256], BF16)
    ccVo = nc.dram_tensor("ccVo", [2 * TS, 256], BF16)
    rp_scr = nc.dram_tensor("rp_scr", [8 * 25 * 64, 128], F32)

    def sb(name, shape, dt):
        return stack.enter_context(nc.sbuf_tensor(name, list(shape), dt))

    def ps(name, shape, dt=F32):
        return stack.enter_context(nc.psum_tensor(name, list(shape), dt))

    X = sb("X", [128, NCH, T], F32)
    rX = [[Res("X%d_%d" % (t, c)) for c in range(NCH)] for t in range(5)]
    ARB = sb("ARB", [128, 51200], BF16)
    ARF = sb("ARF", [128, 3072], F32)
    ident = sb("ident", [128, 128], F32)
    identb = sb("identb", [128, 128], BF16)
    onesb = sb("onesb", [128, 128], BF16)
    rotb = sb("rotb", [128, 128], BF16)
    bdb = sb("bdb", [128, 128], BF16)
    swpf = sb("swpf", [128, 128], F32)
    epsT = sb("epsT", [128, 1], F32)
    MOD = sb("MOD", [128, 72, 2], F32)
    GS = sb("GS", [128, 3, NCH, 2], F32)
    HG = sb("HG", [128, 3, NCH, 2], F32)
    GN = sb("GN", [128, DEPTH * 3 * NCH], F32)
    BM = sb("BM", [128, 72], F32)
    GF = sb("GF", [128, NCH], F32)
    SC = sb("SC", [128, NCH, 2], F32)
    SCb = sb("SCb", [128, NCH, 2], BF16)
    RSTD = sb("RSTD", [128, 512], F32)
    TMPF = sb("TMPF", [128, 2, 512], F32)
    TMPB = sb("TMPB", [128, 2, 512], BF16)
    rConst = Res("const")
    rMOD = Res("MOD")
    rGS = Res("GS")
    rRSTD = Res("RSTD")
    rTMPF = [Res("TMPF0"), Res("TMPF1")]
    rTMPB = [Res("TMPB0"), Res("TMPB1")]

    PS = [ps("ps%d" % i, [128, 512]) for i in range(8)]
    rPS = [Res("ps%d" % i) for i in range(8)]

    class Rot:
        def __init__(self, idxs):
            self.idxs = idxs
            self.i = 0

        def next(self):
            k = self.idxs[self.i % len(self.idxs)]
            self.i += 1
            return k

    S.dma("sp", lambda e: e.dma_start(out=ident[:], in_=c_ident), [], [rConst])
    S.dma("pool", lambda e: e.dma_start(out=identb[:], in_=c_ident), [], [rConst])
    S.dma("pool", lambda e: e.dma_start(out=rotb[:], in_=c_rot), [], [rConst])
    S.dma("pool", lambda e: e.dma_start(out=bdb[:], in_=c_bd), [], [rConst])
    S.dma("sp", lambda e: e.dma_start(out=swpf[:], in_=c_swp), [], [rConst])
    S.op("dve", lambda e: e.memset(onesb[:], 1.0), [], [rConst])
    S.op("dve", lambda e: e.memset(epsT[:], EPS), [], [rConst])
    with nc.allow_non_contiguous_dma(reason="small param layouts"):
        S.dma("sp", lambda e: e.dma_start(out=GN[:], in_=g_norm.rearrange("l s (c p) -> p (l s c)", p=128), allow_slow_non_contiguous=True), [], [rConst])
        S.dma("sp", lambda e: e.dma_start(out=GF[:], in_=g_final.rearrange("(c p) -> p c", p=128), allow_slow_non_contiguous=True), [], [rConst])
        for vv in range(2):
            S.dma("sp", lambda e, vv=vv: e.dma_start(out=SC[:, :, vv], in_=cvec[vv].rearrange("(c p) -> p c", p=128), allow_slow_non_contiguous=True), [], [rConst])
    S.op("act", lambda e: e.activation(out=SCb[:], in_=SC[:], func=AF.Silu), [rConst], [rConst])

    XL = ARF[:, 0:2048].rearrange("p (a f) -> p a f", a=2)
    rXL = [Res("XL%d" % i) for i in range(2)]
    psr = Rot([0, 1, 2, 3])
    for t in range(5):
        for s4 in range(4):
            tk = t * 512 + s4 * 128
            sl = s4 % 2
            S.dma("sp", lambda e, sl=sl, tk=tk: e.dma_start(out=XL[:, sl, :], in_=xin[tk:tk + 128, :]), [], [rXL[sl]])
            for hf in range(2):
                b = psr.next()
                for cc in range(4):
                    c = hf * 4 + cc
                    S.op("pe", lambda e, b=b, sl=sl, c=c, cc=cc: e.transpose(PS[b][:, cc * 128:(cc + 1) * 128],
                                                                             XL[:, sl, c * 128:(c + 1) * 128], ident[:]),
                         [rXL[sl], rConst], [rPS[b]])
                if hf == 0:
                    S.op("dve", lambda e, b=b, hf=hf, tk=tk: e.tensor_copy(
                        out=X[:, hf * 4:(hf + 1) * 4, tk:tk + 128], in_=PS[b][:].rearrange("p (c n) -> p c n", c=4)),
                        [rPS[b]], [rX[t][hf * 4 + i] for i in range(4)])
                else:
                    S.op("act", lambda e, b=b, hf=hf, tk=tk: e.activation(
                        out=X[:, hf * 4:(hf + 1) * 4, tk:tk + 128], in_=PS[b][:].rearrange("p (c n) -> p c n", c=4), func=AF.Copy),
                        [rPS[b]], [rX[t][hf * 4 + i] for i in range(4)])

    def vsel(t):
        return 0 if t == 0 else 1

    BARS = sb("BARS", [128, 1], F32)

    def barrier():
        S.barrier(lambda e: e.memset(BARS[:], 0.0))

    Wbuf = ARB[:, 30720:43008].rearrange("p (b x) -> p b x", b=2)
    g_rW = [Res("W0"), Res("W1")]
    g_rH = [Res("H%d" % t) for t in range(5)]
    g_rG = [[Res("G%d_%d" % (b, t)) for t in range(5)] for b in range(2)]

    def ada(l):
        WM = Wbuf[:, :, 0:4096].rearrange("p b (k n) -> p b k n", k=8)
        rWM = g_rW
        with nc.allow_non_contiguous_dma(reason="bias layout"):
            S.dma("sp", lambda e: e.dma_start(out=BM[:], in_=b_mod[l].rearrange("(m p) -> p m", p=128), allow_slow_non_contiguous=True), [rMOD], [rMOD])
        pb = 7
        for blk in range(18):
            bi = blk % 2
            S.dma("pool", lambda e, bi=bi, blk=blk: e.dma_start(
                out=WM[:, bi], in_=w_mod[l][:, blk * 512:(blk + 1) * 512].rearrange("(k p) n -> p k n", p=128)),
                [], [rWM[bi]])
            for q in range(4):
                ch = blk * 4 + q
                for kc in range(8):
                    S.op("pe", lambda e, bi=bi, q=q, kc=kc, ch=ch: e.matmul(
                        PS[pb][:, ch * 2:ch * 2 + 2], WM[:, bi, kc, q * 128:(q + 1) * 128], SCb[:, kc, :],
                        start=(kc == 0), stop=(kc == 7)), [rWM[bi], rConst], [rPS[pb]])
        S.op("dve", lambda e: e.tensor_tensor(
            out=MOD[:], in0=PS[pb][:, 0:144].rearrange("p (m v) -> p m v", v=2),
            in1=BM[:].unsqueeze(2).to_broadcast([128, 72, 2]), op=ALU.add), [rPS[pb], rMOD], [rMOD])
        for s in range(3):
            gsl = GN[:, (l * 3 + s) * 8:(l * 3 + s + 1) * 8]
            S.op("dve", lambda e, s=s, gsl=gsl: e.scalar_tensor_tensor(
                out=GS[:, s], in0=MOD[:, (3 * s + 1) * 8:(3 * s + 2) * 8, :], scalar=1.0,
                in1=gsl.unsqueeze(2).to_broadcast([128, 8, 2]), op0=ALU.add, op1=ALU.mult), [rMOD, rConst], [rGS])
            S.op("dve", lambda e, s=s: e.tensor_scalar(
                out=HG[:, s], in0=MOD[:, (3 * s + 2) * 8:(3 * s + 3) * 8, :], scalar1=(1.0 if s == 1 else 0.5),
                scalar2=None, op0=ALU.mult), [rMOD], [rGS])

    def rstd_from_ps(pb, n, inv_d, reads):
        S.op("act", lambda e: e.activation(out=RSTD[:, 0:n], in_=PS[pb][:, 0:n], func=AF.Sqrt, bias=epsT[:], scale=inv_d),
             [rPS[pb], rConst] + reads, [rRSTD])
        S.op("dve", lambda e: e.reciprocal(out=RSTD[:, 0:n], in_=RSTD[:, 0:n]), [rRSTD], [rRSTD])

    SQ = ARB[:, 43008:47104].rearrange("p (c n) -> p c n", c=8)
    rSQ = Res("SQ")

    def norm_mod(s, t, Hdst, rH):
        v = vsel(t)
        t0 = t * 512
        S.op("act", lambda e: e.activation(out=SQ[:], in_=X[:, :, t0:t0 + 512], func=AF.Square), rX[t], [rSQ])
        pb = 6
        for c in range(8):
            S.op("pe", lambda e, c=c: e.matmul(PS[pb][:], onesb[:], SQ[:, c, :], start=(c == 0), stop=(c == 7)),
                 [rSQ, rConst], [rPS[pb]])
        rstd_from_ps(pb, 512, 1.0 / D, [])
        for c in range(8):
            k = c % 2
            S.op("dve", lambda e, c=c, k=k: e.tensor_tensor(out=TMPF[:, k], in0=X[:, c, t0:t0 + 512], in1=RSTD[:], op=ALU.mult),
                 [rX[t][c], rRSTD], [rTMPF[k]])
            S.op("act", lambda e, c=c, k=k: e.activation(out=Hdst[:, c, :], in_=TMPF[:, k], func=AF.Identity,
                                                          bias=MOD[:, (3 * s) * 8 + c, v:v + 1], scale=GS[:, s, c, v:v + 1]),
                 [rTMPF[k], rGS, rMOD], [rH])

    def ffn(l, s, wi):
        H = ARB[:, 0:20480].rearrange("p (c n) -> p c n", c=8)
        G = ARB[:, 20480:30720].rearrange("p (b j n) -> p b j n", b=2, j=2)
        W = Wbuf
        rH, rG, rW = g_rH, g_rG, g_rW
        for t in range(2):
            norm_mod(s, t, H[:, :, t * 512:(t + 1) * 512], rH[t])
        win = w_ffn_in[l, wi]
        wout = w_ffn_out[l, wi]
        pin = Rot([0, 1, 2, 3])
        pout = Rot([4, 5, 6, 7])
        for jp in range(11):
            b = jp % 2
            WA = W[:, b, 0:2048].rearrange("p (k n) -> p k n", k=8)
            WB = W[:, b, 2048:4096].rearrange("p (k n) -> p k n", k=8)
            WO = W[:, b, 4096:6144].rearrange("p (j n) -> p j n", j=2)
            S.dma("pool", lambda e, WA=WA, jp=jp: e.dma_start(
                out=WA, in_=win[:, jp * 256:(jp + 1) * 256].rearrange("(k p) n -> p k n", p=128)), [], [rW[b]])
            S.dma("pool", lambda e, WB=WB, jp=jp: e.dma_start(
                out=WB, in_=win[:, DFF + jp * 256:DFF + (jp + 1) * 256].rearrange("(k p) n -> p k n", p=128)), [], [rW[b]])
            S.dma("pool", lambda e, WO=WO, jp=jp: e.dma_start(
                out=WO, in_=wout[jp * 256:(jp + 1) * 256, :].rearrange("(j p) n -> p j n", p=128)), [], [rW[b]])
            for t in range(5):
                t0 = t * 512
                for jj in range(2):
                    pa = pin.next()
                    pbk = pin.next()
                    for kc in range(8):
                        S.op("pe", lambda e, pa=pa, kc=kc, jj=jj, WA=WA, t0=t0: e.matmul(
                            PS[pa][:], WA[:, kc, jj * 128:(jj + 1) * 128], H[:, kc, t0:t0 + 512],
                            start=(kc == 0), stop=(kc == 7)), [rW[b], rH[t]], [rPS[pa]])
                    for kc in range(8):
                        S.op("pe", lambda e, pbk=pbk, kc=kc, jj=jj, WB=WB, t0=t0: e.matmul(
                            PS[pbk][:], WB[:, kc, jj * 128:(jj + 1) * 128], H[:, kc, t0:t0 + 512],
                            start=(kc == 0), stop=(kc == 7)), [rW[b], rH[t]], [rPS[pbk]])
                    k = jj
                    S.op("act", lambda e, pa=pa, k=k: e.activation(out=TMPB[:, k], in_=PS[pa][:], func=AF.Silu),
                         [rPS[pa]], [rTMPB[k]])
                    S.op("dve", lambda e, pbk=pbk, k=k, jj=jj, t0=t0, b=b: e.tensor_tensor(
                        out=G[:, b, jj, t0:t0 + 512], in0=TMPB[:, k], in1=PS[pbk][:], op=ALU.mult),
                        [rTMPB[k], rPS[pbk]], [rG[b][t]])
                if jp == 0 and t + 2 < 5:
                    norm_mod(s, t + 2, H[:, :, (t + 2) * 512:(t + 3) * 512], rH[t + 2])
            for t in range(5):
                t0 = t * 512
                v = vsel(t)
                for c in range(8):
                    po = pout.next()
                    for jj in range(2):
                        S.op("pe", lambda e, po=po, jj=jj, c=c, WO=WO, t0=t0, b=b: e.matmul(
                            PS[po][:], WO[:, jj, c * 128:(c + 1) * 128], G[:, b, jj, t0:t0 + 512],
                            start=(jj == 0), stop=(jj == 1)), [rW[b], rG[b][t]], [rPS[po]])
                    S.op("dve", lambda e, po=po, c=c, t0=t0, v=v: e.scalar_tensor_tensor(
                        out=X[:, c, t0:t0 + 512], in0=PS[po][:], scalar=HG[:, s, c, v:v + 1], in1=X[:, c, t0:t0 + 512],
                        op0=ALU.mult, op1=ALU.add), [rPS[po], rGS, rX[t][c]], [rX[t][c]])

    def final_out():
        YT = ARF[:, 0:2048].rearrange("p (a f) -> p a f", a=2)
        rYT = rXL
        pr = Rot([0, 1, 2, 3, 4, 5])
        XGt = ARF[:, 2048:2560]
        XG2 = [ARF[:, 2048:2560].rearrange("p (c n) -> p c n", c=4), ARF[:, 2560:3072].rearrange("p (c n) -> p c n", c=4)]
        rXG2 = [Res("XGa"), Res("XGb")]
        rXG = Res("XG")
        RS1 = sb("RS1", [128, 1], F32)
        rRS1 = Res("RS1")
        for t in range(5):
            for s4 in range(4):
                tk = t * 512 + s4 * 128
                S.op("act", lambda e, tk=tk: e.activation(out=SQ[:, :, 0:128], in_=X[:, :, tk:tk + 128], func=AF.Square),
                     rX[t], [rSQ])
                pb = 6
                for c in range(8):
                    S.op("pe", lambda e, c=c: e.matmul(PS[pb][:, 0:1], SQ[:, c, 0:128], onesb[:, 0:1],
                                                       start=(c == 0), stop=(c == 7)), [rSQ, rConst], [rPS[pb]])
                S.op("act", lambda e: e.activation(out=RS1[:], in_=PS[pb][:, 0:1], func=AF.Sqrt, bias=epsT[:], scale=1.0 / D),
                     [rPS[pb], rConst], [rRS1])
                S.op("dve", lambda e: e.reciprocal(out=RS1[:], in_=RS1[:]), [rRS1], [rRS1])
                for half in range(2):
                    b = pr.next()
                    xg = XG2[half]
                    rxg = rXG2[half]
                    S.op("dve", lambda e, half=half, tk=tk, xg=xg: e.tensor_tensor(
                        out=xg, in0=X[:, half * 4:(half + 1) * 4, tk:tk + 128],
                        in1=GF[:, half * 4:(half + 1) * 4].unsqueeze(2).to_broadcast([128, 4, 128]), op=ALU.mult),
                        [rX[t][half * 4 + i] for i in range(4)] + [rConst], [rxg])
                    for cc in range(4):
                        S.op("pe", lambda e, b=b, cc=cc, xg=xg: e.transpose(PS[b][:, cc * 128:(cc + 1) * 128], xg[:, cc, :], ident[:]),
                             [rxg, rConst], [rPS[b]])
                    S.op("dve", lambda e, b=b, half=half, s4=s4: e.tensor_scalar(
                        out=YT[:, s4 % 2, half * 512:(half + 1) * 512], in0=PS[b][:], scalar1=RS1[:, 0:1], scalar2=None, op0=ALU.mult),
                        [rPS[b], rRS1], [rYT[s4 % 2]])
                S.dma("sp", lambda e, s4=s4, tk=tk: e.dma_start(out=y[tk:tk + 128, :], in_=YT[:, s4 % 2, :]), [rYT[s4 % 2]], [])


    M0 = 20480
    region = []
    flatG = [r for bb in g_rG for r in bb]

    def newres(name, pend):
        r = Res(name, pend)
        region.append(r)
        return r

    rCCi = Res("cci")
    rCCo = Res("cco")
    rRP = Res("rp_scr")
    CMT = ARB[:, 47104:47616]
    SEL = ARB[0:16, 47616:48640]
    RMB = ARB[0:16, 48640:50688]
    S.dma("pool", lambda e: e.dma_start(out=CMT, in_=c_cmt), [], [rConst])
    S.dma("pool", lambda e: e.dma_start(out=SEL, in_=c_sel), [], [rConst])
    S.dma("pool", lambda e: e.dma_start(out=RMB, in_=c_rm), [], [rConst])
    INVB = sb("INVB", [128, 4 * 6 * 2 * 8], F32)
    HV = sb("HV", [128, 2], F32)
    PSCL = sb("PSCL", [128, 2 * 4], F32)
    S.dma("sp", lambda e: e.dma_start(out=INVB[:], in_=c_invb), [], [rConst])
    S.dma("sp", lambda e: e.dma_start(out=HV[:], in_=c_hv), [], [rConst])
    S.dma("sp", lambda e: e.dma_start(out=PSCL[:], in_=pool_scale.rearrange("e (g p) -> p (e g)", p=128),
                                      allow_slow_non_contiguous=True), [], [rConst])

    sbank = Rot([0, 1, 2])
    obank = Rot([3, 4])
    mbank = Rot([5, 6])
    PJ = 7

    def cc_allgather(cin, cout, nocc):
        if nocc:
            n = cin.shape[0]
            S.dma("sp", lambda e: e.dma_start(out=cout.ap()[0:n, :], in_=cin.ap()), [rCCi], [rCCo])
            S.dma("sp", lambda e: e.dma_start(out=cout.ap()[n:2 * n, :], in_=cin.ap()), [rCCi], [rCCo])
        else:
            S.op("pool", lambda e: e.collective_compute("AllGather", ALU.bypass, replica_groups=PAIRS,
                                                        ins=[cin.ap().opt()], outs=[cout.ap().opt()]),
                 [rCCi], [rCCo])

    def attention(qT, half, nq, ktiles, out_ap, rQ, rOut, PT, rPT, ptrot, hook=None, sb=None, pre_n=2):
        sb = sb or sbank
        po = obank.next()
        pm = mbank.next()
        hs = slice(half * 64, (half + 1) * 64)
        n = len(ktiles)
        banks = {}

        def s_mm(i):
            kt = ktiles[i]
            nk = kt["nk"]
            pss = sb.next()
            banks[i] = pss
            terms = [(kt["kT"], qT, [rQ] + kt["reads"])] + kt.get("bias", [])
            for j, (lt, rh, rd) in enumerate(terms):
                S.op("pe", lambda e, pss=pss, lt=lt, rh=rh, j=j, nt=len(terms), nk=nk: e.matmul(
                    PS[pss][0:nk, 0:nq], lt, rh, start=(j == 0), stop=(j == nt - 1)), rd, [rPS[pss]])

        pre = min(pre_n, n)
        for i in range(pre):
            s_mm(i)
        for i, kt in enumerate(ktiles):
            nk = kt["nk"]
            pss = banks[i]
            pt = ptrot.next()
            S.op("act", lambda e, pss=pss, pt=pt, nk=nk: e.activation(out=PT[0:nk, pt, 0:nq], in_=PS[pss][0:nk, 0:nq], func=AF.Exp),
                 [rPS[pss]], [rPT[pt]])
            if i + pre < n:
                s_mm(i + pre)
            S.op("pe", lambda e, pt=pt, kt=kt, i=i, nk=nk: e.matmul(PS[po][hs, 0:nq], kt["v"], PT[0:nk, pt, 0:nq],
                                                                 start=(i == 0), stop=(i == n - 1)),
                 [rPT[pt]] + kt["reads"], [rPS[po]])
            S.op("pe", lambda e, pt=pt, i=i, nk=nk: e.matmul(PS[pm][hs, 0:nq], onesb[0:nk, 0:64], PT[0:nk, pt, 0:nq],
                                                         start=(i == 0), stop=(i == n - 1)),
                 [rPT[pt], rConst], [rPS[pm]])
            if hook is not None:
                hook()
        S.op("dve", lambda e: e.reciprocal(out=TMPF[hs, 0, 0:nq], in_=PS[pm][hs, 0:nq]), [rPS[pm]], [rTMPF[0]])
        S.op("dve", lambda e: e.tensor_tensor(out=out_ap, in0=PS[po][hs, 0:nq], in1=TMPF[hs, 0, 0:nq], op=ALU.mult),
             [rPS[po], rTMPF[0]], [rOut])

    def proj_fm(Wt, rW_, rhs_fn, n, evac, bank=None):
        bk = PJ if bank is None else bank
        for kc in range(8):
            rh, rd = rhs_fn(kc)
            S.op("pe", lambda e, kc=kc, rh=rh: e.matmul(PS[bk][:, 0:n], Wt[:, kc, :], rh, start=(kc == 0), stop=(kc == 7)),
                 [rW_] + rd, [rPS[bk]])
        if bank is None:
            evac(PS[bk])
        else:
            evac(PS[bk], bk)

    def even_mixer(l, nocc):
        e_ = l // 2
        H2 = ARB[:, 0:20480].rearrange("p (c n) -> p c n", c=8)
        rH = g_rH
        for t in range(5):
            norm_mod(1, t, H2[:, :, t * 512:(t + 1) * 512], rH[t])
        pend0 = S.fence_ops(flatG + g_rW)
        del region[:]
        for q4 in range(4):
            S.dma("sp", lambda e, q4=q4: e.dma_start(
                out=rp_scr.ap()[q4 * 3200:(q4 + 1) * 3200, :].rearrange("(hs ck) m -> hs ck m", ck=64),
                in_=rpbP[e_].rearrange("h s m -> (h s) m")[q4 * 50:(q4 + 1) * 50, :].unsqueeze(1).to_broadcast([50, 64, 128])),
                [rRP], [rRP])
        HH = ARB[:, M0:M0 + 4096].rearrange("p (c n) -> p c n", c=8)
        rHH = newres("HH", pend0)
        S.dma("sp", lambda e: e.dma_start(out=ccHi.ap()[0:1024, :].rearrange("(c p) n -> p c n", p=128), in_=H2[:, :, 512:768]),
              [rH[1], rCCo], [rCCi])
        S.dma("sp", lambda e: e.dma_start(out=ccHi.ap()[1024:2048, :].rearrange("(c p) n -> p c n", p=128), in_=H2[:, :, 2304:2560]),
              [rH[4], rCCo], [rCCi])
        cc_allgather(ccHi, ccHo, nocc)
        for h2 in range(2):
            S.dma("sp", lambda e, h2=h2: e.dma_start(
                out=HH[:, :, h2 * 256:(h2 + 1) * 256],
                in_=ccHo.ap()[1024 + h2 * 1024:2048 + h2 * 1024, :].rearrange("(c p) n -> p c n", p=128)), [rCCo], [rHH])

        gate_s = 1
        base = M0 + 4096
        STG = float(os.environ.get("MIXSTAGE", "99"))
        if STG <= 1:
            return
        WU = ARB[:, base:base + 1024].rearrange("p (k n) -> p k n", k=8)
        WP = ARB[:, base + 1024:base + 1152]
        WOr = ARB[:, base + 1152:base + 2176]
        PL = ARB[:, base + 2176:base + 2688]
        AO = ARB[:, base + 2688:base + 3200]
        UE = ARF[:, 0:528]
        T1 = ARF[:, 528:1056]
        T2 = ARF[:, 1056:1584]
        E8 = ARF[:, 1584:1600]
        rWU = newres("WU", pend0)
        rPL = newres("PL", pend0)
        rAO = newres("AO", pend0)
        rUE = newres("UE", pend0)
        rT1 = newres("T1", pend0)
        rT2 = newres("T2", pend0)
        rE8 = newres("E8", pend0)
        S.op("dve", lambda e: e.memset(UE, 0.0), [], [rUE])
        segs = [(0, 256, 0), (256, 256, 0), (512, 512, 1), (1024, 512, 2), (1536, 512, 3), (2048, 512, 4)]
        for g in range(4):
            w = 2 << g
            S.dma("pool", lambda e, g=g: e.dma_start(out=WU, in_=w_in_ab[e_][:, g * 128:(g + 1) * 128].rearrange("(k p) n -> p k n", p=128)),
                  [], [rWU])
            S.dma("pool", lambda e, g=g: e.dma_start(out=WP, in_=w_pool[e_, g]), [], [rWU])
            S.dma("pool", lambda e, g=g: e.dma_start(out=WOr, in_=w_out_ab[e_][g * 128:(g + 1) * 128, :]), [], [rWU])
            for si, (t0, L, t) in enumerate(segs):
                v = vsel(t)
                for kc in range(8):
                    S.op("pe", lambda e, kc=kc, t0=t0, L=L: e.matmul(PS[PJ][:, 0:L], WU[:, kc, :], H2[:, kc, t0:t0 + L],
                                                                   start=(kc == 0), stop=(kc == 7)), [rWU, rH[t]], [rPS[PJ]])
                S.op("act", lambda e, L=L: e.activation(out=UE[:, 8:8 + L], in_=PS[PJ][:, 0:L], func=AF.Copy), [rPS[PJ]], [rUE])
                if t >= 1:
                    pb2 = sbank.next()
                    for kc in range(8):
                        lh = HH[:, kc, 248:256] if t == 1 else H2[:, kc, t0 - 8:t0]
                        S.op("pe", lambda e, kc=kc, lh=lh, pb2=pb2: e.matmul(PS[pb2][:, 0:8], WU[:, kc, :], lh, start=(kc == 0), stop=(kc == 7)),
                             [rWU, rHH, rH[t - 1]], [rPS[pb2]])
                    for kc in range(8):
                        rh_ = HH[:, kc, 256:264] if t == 4 else H2[:, kc, t0 + 512:t0 + 520]
                        S.op("pe", lambda e, kc=kc, rh_=rh_, pb2=pb2: e.matmul(PS[pb2][:, 8:16], WU[:, kc, :], rh_, start=(kc == 0), stop=(kc == 7)),
                             [rWU, rHH, rH[min(t + 1, 4)]], [rPS[pb2]])
                    if t == 1:
                        S.op("dve", lambda e, pb2=pb2: e.tensor_scalar(out=UE[:, 0:8], in0=PS[pb2][:, 0:8], scalar1=HV[:, 0:1], scalar2=None, op0=ALU.mult),
                             [rPS[pb2], rConst], [rUE])
                    else:
                        S.op("dve", lambda e, pb2=pb2: e.tensor_copy(out=UE[:, 0:8], in_=PS[pb2][:, 0:8]), [rPS[pb2]], [rUE])
                    if t == 4:
                        S.op("dve", lambda e, pb2=pb2, L=L: e.tensor_scalar(out=UE[:, 8 + L:16 + L], in0=PS[pb2][:, 8:16], scalar1=HV[:, 1:2], scalar2=None, op0=ALU.mult),
                             [rPS[pb2], rConst], [rUE])
                    else:
                        S.op("dve", lambda e, pb2=pb2, L=L: e.tensor_copy(out=UE[:, 8 + L:16 + L], in_=PS[pb2][:, 8:16]), [rPS[pb2]], [rUE])
                else:
                    S.op("dve", lambda e, L=L: e.memset(UE[:, 8 + L:16 + L], 0.0), [], [rUE])
                    S.op("dve", lambda e: e.memset(UE[:, 0:8], 0.0), [], [rUE])
                Lp = L + 16
                S.op("dve", lambda e, Lp=Lp: e.tensor_tensor(out=T1[:, 0:Lp - 1], in0=UE[:, 0:Lp - 1], in1=UE[:, 1:Lp], op=ALU.add), [rUE], [rT1])
                cur, rcur, oth, roth = T1, rT1, T2, rT2
                ln = Lp - 1
                sh = 1
                for k in range(g):
                    sh *= 2
                    nl = ln - sh
                    S.op("dve", lambda e, cur=cur, oth=oth, nl=nl, sh=sh: e.tensor_tensor(out=oth[:, 0:nl], in0=cur[:, 0:nl], in1=cur[:, sh:sh + nl], op=ALU.add),
                         [rcur], [roth])
                    cur, rcur, oth, roth = oth, roth, cur, rcur
                    ln = nl
                off = 8 - w // 2
                S.op("dve", lambda e, cur=cur, off=off, L=L, w=w: e.scalar_tensor_tensor(
                    out=PL[:, 0:L], in0=cur[:, off:off + L], scalar=1.0 / w, in1=UE[:, 8:8 + L], op0=ALU.mult, op1=ALU.subtract),
                    [rcur, rUE], [rPL])
                for ed in range(2):
                    a0 = 0 if ed == 0 else L - 8
                    ib = ((g * 6 + si) * 2 + ed) * 8
                    S.op("dve", lambda e, cur=cur, off=off, a0=a0, ib=ib: e.tensor_tensor(
                        out=E8[:, 0:8], in0=cur[:, off + a0:off + a0 + 8], in1=INVB[:, ib:ib + 8], op=ALU.mult), [rcur, rConst], [rE8])
                    S.op("dve", lambda e, a0=a0: e.tensor_tensor(out=PL[:, a0:a0 + 8], in0=E8[:, 0:8], in1=UE[:, 8 + a0:16 + a0], op=ALU.subtract),
                         [rE8, rUE], [rPL])
                S.op("pe", lambda e, L=L: e.matmul(PS[PJ][:, 0:L], WP, PL[:, 0:L], start=True, stop=True), [rWU, rPL], [rPS[PJ]])
                S.op("act", lambda e, L=L, g=g: e.activation(out=AO[:, 0:L], in_=PS[PJ][:, 0:L], func=AF.Copy,
                                                             scale=PSCL[:, e_ * 4 + g:e_ * 4 + g + 1]), [rPS[PJ], rConst], [rAO])
                for c in range(8):
                    pbo = sbank.next()
                    S.op("pe", lambda e, c=c, pbo=pbo, L=L: e.matmul(PS[pbo][:, 0:L], WOr[:, c * 128:(c + 1) * 128], AO[:, 0:L], start=True, stop=True),
                         [rWU, rAO], [rPS[pbo]])
                    S.op("dve", lambda e, c=c, pbo=pbo, L=L, t0=t0, v=v: e.scalar_tensor_tensor(
                        out=X[:, c, t0:t0 + L], in0=PS[pbo][:, 0:L], scalar=HG[:, gate_s, c, v:v + 1], in1=X[:, c, t0:t0 + L],
                        op0=ALU.mult, op1=ALU.add), [rPS[pbo], rGS, rX[t][c]], [rX[t][c]])

        if STG <= 2:
            return
        keep = [rHH]
        pend1 = S.fence_ops([r for r in region if r is not rHH])
        del region[:]
        region.append(rHH)
        o = base
        KT = ARB[:, o:o + 3072]; o += 3072
        VT = ARB[:, o:o + 3072].rearrange("p (t f) -> p t f", f=128); o += 3072
        QT = ARB[:, o:o + 512]; o += 512
        MB = ARB[:, o:o + 2560]; o += 2560
        TB = ARB[:, o:o + 3072].rearrange("p (h x) -> p h x", h=2); o += 3072
        PT = ARB[:, o:o + 1536].rearrange("p (b n) -> p b n", b=3); o += 1536
        KTC = ARB[:, o:o + 256]; o += 256
        VTC = ARB[:, o:o + 256].rearrange("p (t f) -> p t f", f=128); o += 256
        WQ = ARB[:, o:o + 1024].rearrange("p (k n) -> p k n", k=8); o += 1024
        WK = ARB[:, o:o + 1024].rearrange("p (k n) -> p k n", k=8); o += 1024
        WV = ARB[:, o:o + 1024].rearrange("p (k n) -> p k n", k=8); o += 1024
        WO2 = ARB[:, o:o + 1024]; o += 1024
        assert o <= 43008, o
        KS = ARF[:, 0:512].rearrange("p (t f) -> p t f", f=128)
        VS = ARF[:, 512:1024].rearrange("p (t f) -> p t f", f=128)
        CS = ARF[:, 1024:1280].rearrange("p (t f) -> p t f", f=128)
        rKT = newres("KT", pend1); rVT = newres("VT", pend1); rQT = newres("QT", pend1); rMB = newres("MB", pend1)
        rQTb = [rQT, newres("QT2", pend1)]
        rMBt = [newres("MB%d" % i, pend1) for i in range(5)]
        rTB = newres("TB", pend1); rPT = [newres("PT%d" % i, pend1) for i in range(3)]
        rKTC = newres("KTC", pend1); rVTC = newres("VTC", pend1); rWA = newres("WA", pend1)
        rKS = newres("KS", pend1); rVS = newres("VS", pend1); rCS = newres("CS", pend1)
        ptrot = Rot([0, 1, 2])
        wia = w_in_ab[e_]
        for hc in range(4):
            S.dma("pool", lambda e, hc=hc: e.dma_start(out=WQ, in_=wia[:, 512 + hc * 128:640 + hc * 128].rearrange("(k p) n -> p k n", p=128)), [], [rWA])
            S.dma("pool", lambda e, hc=hc: e.dma_start(out=WK, in_=wia[:, 1024 + hc * 128:1152 + hc * 128].rearrange("(k p) n -> p k n", p=128)), [], [rWA])
            S.dma("pool", lambda e, hc=hc: e.dma_start(out=WV, in_=wia[:, 1536 + hc * 128:1664 + hc * 128].rearrange("(k p) n -> p k n", p=128)), [], [rWA])
            S.dma("pool", lambda e, hc=hc: e.dma_start(out=WO2, in_=w_out_ab[e_][512 + hc * 128:640 + hc * 128, :]), [], [rWA])
            for hf in range(2):
                h = 2 * hc + hf
                for kr in range(2):
                    if os.environ.get("NOTB"):
                        continue
                    off = ((h * 25 + 1 - kr) * 64) * 128 + 63
                    src = bass.AP(tensor=rp_scr, offset=off, ap=[[127, 64], [8192, 24], [1, 64]])
                    S.dma("pool", lambda e, hf=hf, kr=kr, src=src: e.dma_start(
                        out=TB[kr * 64:(kr + 1) * 64, hf, :].rearrange("p (s c) -> p s c", c=64), in_=src), [rRP], [rTB])
                S.op("pool", lambda e, hf=hf: e.tensor_tensor(
                    out=TB[:, hf, :].rearrange("p (s c) -> p s c", c=64), in0=TB[:, hf, :].rearrange("p (s c) -> p s c", c=64),
                    in1=CMT[:, 0:64].unsqueeze(1).to_broadcast([128, 24, 64]), op=ALU.add), [rTB, rConst], [rTB])
            if STG <= 2.2:
                return
            S.dma("sp", lambda e, hc=hc: e.dma_start(out=CS, in_=cnk[e_][:, hc * 128:(hc + 1) * 128].rearrange("(t p) f -> p t f", p=128)), [], [rCS])
            S.dma("pool", lambda e, hc=hc: e.dma_start(out=VTC, in_=cnv[e_][:, hc * 128:(hc + 1) * 128].rearrange("(t p) f -> p t f", p=128)), [], [rVTC])
            for tt in range(2):
                S.op("pe", lambda e, tt=tt: e.transpose(PS[PJ][:, tt * 128:(tt + 1) * 128], CS[:, tt, :], ident[:]), [rCS, rConst], [rPS[PJ]])
            S.op("act", lambda e: e.activation(out=KTC, in_=PS[PJ][:, 0:256], func=AF.Copy), [rPS[PJ]], [rKTC])
            if STG <= 2.4:
                return
            for t in range(5):
                dst = KT[:, 0:512] if t == 0 else KT[:, 512 + 256 + 512 * (t - 1):512 + 256 + 512 * t]
                if t % 2 == 0:
                    proj_fm(WK, rWA, lambda kc, t=t: (H2[:, kc, t * 512:(t + 1) * 512], [rH[t]]), 512,
                            lambda P, bk, dst=dst: S.op("act", lambda e: e.activation(out=dst, in_=P[:, 0:512], func=AF.Copy), [rPS[bk]], [rKT]), bank=PJ)
                else:
                    proj_fm(WK, rWA, lambda kc, t=t: (H2[:, kc, t * 512:(t + 1) * 512], [rH[t]]), 512,
                            lambda P, bk, dst=dst: S.op("dve", lambda e: e.tensor_copy(out=dst, in_=P[:, 0:512]), [rPS[bk]], [rKT]), bank=2)
            proj_fm(WK, rWA, lambda kc: (HH[:, kc, :], [rHH]), 512,
                    lambda P: (S.op("act", lambda e: e.activation(out=KT[:, 512:768], in_=P[:, 0:256], func=AF.Copy), [rPS[PJ]], [rKT]),
                               S.op("act", lambda e: e.activation(out=KT[:, 512 + 2304:512 + 2560], in_=P[:, 256:512], func=AF.Copy), [rPS[PJ]], [rKT])))
            if STG <= 2.6:
                return
            def vsrc(ti):
                if ti < 4:
                    return lambda kc: H2[:, kc, ti * 128:(ti + 1) * 128], rH[0]
                j = ti - 4
                if j < 2:
                    return lambda kc: HH[:, kc, j * 128:(j + 1) * 128], rHH
                if j >= 18:
                    return lambda kc: HH[:, kc, 256 + (j - 18) * 128:256 + (j - 17) * 128], rHH
                tk = 512 + (j - 2) * 128
                return lambda kc: H2[:, kc, tk:tk + 128], rH[tk // 512]
            for grp in range(6):
                pb2 = sbank.next()
                for q in range(4):
                    ti = grp * 4 + q
                    fn, rr = vsrc(ti)
                    for kc in range(8):
                        S.op("pe", lambda e, kc=kc, q=q, fn=fn, pb2=pb2: e.matmul(PS[pb2][:, q * 128:(q + 1) * 128], fn(kc), WV[:, kc, :],
                                                                              start=(kc == 0), stop=(kc == 7)), [rWA, rr], [rPS[pb2]])
                if grp == 0:
                    S.op("act", lambda e, pb2=pb2: e.activation(out=VS, in_=PS[pb2][:].rearrange("p (t f) -> p t f", f=128), func=AF.Copy), [rPS[pb2]], [rVS])
                    S.op("dve", lambda e: e.tensor_copy(out=VT[:, 0:4, :], in_=VS), [rVS], [rVT])
                else:
                    S.op("dve", lambda e, grp=grp, pb2=pb2: e.tensor_copy(out=VT[:, grp * 4:(grp + 1) * 4, :], in_=PS[pb2][:].rearrange("p (t f) -> p t f", f=128)),
                         [rPS[pb2]], [rVT])
                if grp == 0:
                    for bb in range(2):
                        if os.environ.get("NOOUTDMA"):
                            continue
                        S.dma("sp", lambda e, bb=bb, hc=hc: e.dma_start(
                            out=o_nbv[bb, e_][:, hc * 128:(hc + 1) * 128].rearrange("(t p) f -> p t f", p=128), in_=VS[:, bb * 2:bb * 2 + 2, :]), [rVS], [])
            if STG <= 2.8:
                return
            pb2 = sbank.next()
            for q in range(4):
                for kc in range(8):
                    S.op("pe", lambda e, kc=kc, q=q, pb2=pb2: e.matmul(PS[pb2][:, q * 128:(q + 1) * 128], H2[:, kc, q * 128:(q + 1) * 128], WK[:, kc, :],
                                                                   start=(kc == 0), stop=(kc == 7)), [rWA, rH[0]], [rPS[pb2]])
            S.op("act", lambda e, pb2=pb2: e.activation(out=KS, in_=PS[pb2][:].rearrange("p (t f) -> p t f", f=128), func=AF.Copy), [rPS[pb2]], [rKS])
            for bb in range(2):
                S.dma("sp", lambda e, bb=bb, hc=hc: e.dma_start(
                    out=o_nbk[bb, e_][:, hc * 128:(hc + 1) * 128].rearrange("(t p) f -> p t f", p=128), in_=KS[:, bb * 2:bb * 2 + 2, :]), [rKS], [])
            if STG <= 3:
                return
            QTb = [QT, ARB[:, 50688:51200]]
            sb_e = Rot([0, 1])
            OPE = 2

            def qproj_e(t):
                qd = QTb[t % 2]
                proj_fm(WQ, rWA, lambda kc: (H2[:, kc, t * 512:(t + 1) * 512], [rH[t]]), 512,
                        lambda P: S.op("act", lambda e: e.activation(out=qd, in_=P[:, 0:512], func=AF.Identity, scale=0.125), [rPS[PJ]], [rQTb[t % 2]]))

            def outproj_e(t, c):
                v = vsel(t)
                S.op("pe", lambda e: e.matmul(PS[OPE][:], WO2[:, c * 128:(c + 1) * 128], MB[:, t * 512:(t + 1) * 512], start=True, stop=True),
                     [rWA, rMBt[t]], [rPS[OPE]])
                S.op("dve", lambda e: e.scalar_tensor_tensor(
                    out=X[:, c, t * 512:(t + 1) * 512], in0=PS[OPE][:], scalar=HG[:, gate_s, c, v:v + 1], in1=X[:, c, t * 512:(t + 1) * 512],
                    op0=ALU.mult, op1=ALU.add), [rPS[OPE], rGS, rX[t][c]], [rX[t][c]])

            qproj_e(0)
            for t in range(5):
                if STG <= 4 and t >= 1:
                    return
                nsteps = 8 if t == 0 else 20
                acts = {}
                if t + 1 < 5:
                    acts.setdefault(0, []).append(lambda t=t: qproj_e(t + 1))
                if t >= 1:
                    for c in range(8):
                        acts.setdefault(2 + 2 * c, []).append(lambda t=t, c=c: outproj_e(t - 1, c))
                cnt = [0]

                def hook():
                    k = cnt[0]
                    cnt[0] += 1
                    for fn_ in acts.get(k, []):
                        fn_()

                QTt = QTb[t % 2]
                rQt = rQTb[t % 2]
                for hf in range(2):
                    hs = slice(hf * 64, (hf + 1) * 64)
                    if t == 0:
                        for bb in range(2):
                            kts = [dict(kT=KT[hs, bb * 256 + i * 128:bb * 256 + (i + 1) * 128], v=VT[:, bb * 2 + i, hs], nk=128, reads=[rKT, rVT])
                                   for i in range(2)]
                            attention(QTt[hs, bb * 256:(bb + 1) * 256], hf, 256, kts, MB[hs, bb * 256:(bb + 1) * 256], rQt, rMBt[t], PT, rPT, ptrot,
                                      hook=hook, sb=sb_e, pre_n=1)
                    else:
                        b = t - 1
                        kts = []
                        for kt in range(8):
                            ko = 512 + (8 * b + 2 * kt) * 64
                            bias = [(identb[:], TB[:, hf, (15 - 2 * kt) * 64:(15 - 2 * kt) * 64 + 512], [rTB, rConst]),
                                    (SEL[:, kt * 128:(kt + 1) * 128], RMB[:, b * 512:(b + 1) * 512], [rConst])]
                            kts.append(dict(kT=KT[hs, ko:ko + 128], v=VT[:, 4 + 4 * b + kt, hs], nk=128, reads=[rKT, rVT], bias=bias))
                        for i in range(2):
                            kts.append(dict(kT=KTC[hs, i * 128:(i + 1) * 128], v=VTC[:, i, hs], nk=128, reads=[rKTC, rVTC]))
                        attention(QTt[hs, :], hf, 512, kts, MB[hs, t * 512:(t + 1) * 512], rQt, rMBt[t], PT, rPT, ptrot,
                                  hook=hook, sb=sb_e, pre_n=1)
                assert cnt[0] == nsteps, (cnt[0], nsteps)
            for c in range(8):
                outproj_e(4, c)
        pend2 = S.fence_ops(region)
        del region[:]
        for r in flatG + g_rW:
            r.pend = list(pend2)


    c_ropeA = din("c_ropeA", [128, 64])
    c_ropeB = din("c_ropeB", [128, 128])
    ROPA = sb("ROPA", [128, 64], F32)
    ROPB = sb("ROPB", [128, 128], F32)
    GQK = sb("GQK", [128, 4], F32)
    GQS = sb("GQS", [128, 2], F32)
    GKR = sb("GKR", [128, 2, 64], F32)
    S.dma("sp", lambda e: e.dma_start(out=ROPA[:], in_=c_ropeA), [], [rConst])
    S.dma("sp", lambda e: e.dma_start(out=ROPB[:], in_=c_ropeB), [], [rConst])
    for o_ in range(2):
        for hf_ in range(2):
            S.dma("sp", lambda e, o_=o_, hf_=hf_: e.dma_start(out=GQK[hf_ * 64:(hf_ + 1) * 64, 2 * o_:2 * o_ + 1],
                                                         in_=g_qnorm[o_].rearrange("(d one) -> d one", one=1),
                                                         allow_slow_non_contiguous=True), [], [rConst])
            S.dma("sp", lambda e, o_=o_, hf_=hf_: e.dma_start(out=GQK[hf_ * 64:(hf_ + 1) * 64, 2 * o_ + 1:2 * o_ + 2],
                                                         in_=g_knorm[o_].rearrange("(d one) -> d one", one=1),
                                                         allow_slow_non_contiguous=True), [], [rConst])
        S.dma("sp", lambda e, o_=o_: e.dma_start(out=GKR[:, o_, :], in_=g_knorm[o_:o_ + 1, :].to_broadcast([128, 64])), [], [rConst])
    for o_ in range(2):
        S.op("dve", lambda e, o_=o_: e.tensor_scalar(out=GQS[:, o_:o_ + 1], in0=GQK[:, 2 * o_:2 * o_ + 1], scalar1=0.125, scalar2=None, op0=ALU.mult),
             [rConst], [rConst])

    def odd_mixer(l, nocc):
        o_ = l // 2
        H2 = ARB[:, 0:20480].rearrange("p (c n) -> p c n", c=8)
        rH = g_rH
        for t in range(2):
            norm_mod(1, t, H2[:, :, t * 512:(t + 1) * 512], rH[t])
        pend0 = S.fence_ops(flatG + g_rW)
        del region[:]
        gate_s = 1
        wq = w_qkv_c[o_]
        a = M0
        KTP = ARB[:, a:a + 1024].rearrange("p (c n) -> p c n", c=2); a += 1024
        VTP = ARB[:, a:a + 1024].rearrange("p (t f) -> p t f", f=256); a += 1024
        SQb = ARB[:, a:a + 512]; a += 512
        KNb = ARB[:, a:a + 512]; a += 512
        a_common = a
        WK2 = ARB[:, a:a + 2048].rearrange("p (k n) -> p k n", k=8); a += 2048
        WV2 = ARB[:, a:a + 2048].rearrange("p (k n) -> p k n", k=8); a += 2048
        KTS = ARB[:, a:a + 1024].rearrange("p (b n) -> p b n", b=2); a += 1024
        VST = ARB[:, a:a + 1024].rearrange("p (b t f) -> p b t f", b=2, t=2); a += 1024
        RT1 = ARF[:, 0:512]
        RT2 = ARF[:, 512:1024]
        KS = ARF[:, 1024:1280]
        VS = ARF[:, 1280:1792].rearrange("p (t f) -> p t f", f=256)
        SS4 = ARF[:, 1792:1796]
        CS = ARF[:, 1800:2056].rearrange("p (t f) -> p t f", f=128)
        rKTP = newres("KTP", pend0); rVTP = newres("VTP", pend0); rSQb = newres("SQb", pend0); rKNb = newres("KNb", pend0)
        rWA = newres("WA", pend0); rKTS = [newres("KTS%d" % i, pend0) for i in range(2)]
        rVST = [newres("VST%d" % i, pend0) for i in range(2)]
        rRT1 = newres("RT1", pend0); rRT2 = newres("RT2", pend0); rKS = newres("KS", pend0); rVS = newres("VS", pend0)
        rSS4 = newres("SS4", pend0); rCS = newres("CS", pend0)

        def head_norm_rope(P, gcol, t, dst, rdst):
            S.op("act", lambda e: e.activation(out=SQb, in_=P, func=AF.Square), [rPS[PJ]], [rSQb])
            pss = sbank.next()
            S.op("pe", lambda e, pss=pss: e.matmul(PS[pss][:], bdb[:], SQb, start=True, stop=True), [rSQb, rConst], [rPS[pss]])
            rstd_from_ps(pss, 512, 1.0 / 64, [])
            if t == 0:
                S.op("dve", lambda e: e.scalar_tensor_tensor(out=dst, in0=P, scalar=gcol, in1=RSTD[:], op0=ALU.mult, op1=ALU.mult),
                     [rPS[PJ], rRSTD, rConst], [rdst])
                return
            S.op("dve", lambda e: e.scalar_tensor_tensor(out=KNb, in0=P, scalar=gcol, in1=RSTD[:], op0=ALU.mult, op1=ALU.mult),
                 [rPS[PJ], rRSTD, rConst], [rKNb])
            pr = sbank.next()
            S.op("pe", lambda e, pr=pr: e.matmul(PS[pr][:], rotb[:], KNb, start=True, stop=True), [rKNb, rConst], [rPS[pr]])
            r0 = 8 * (t - 1)
            v3 = lambda ap: ap.rearrange("p (r c) -> p r c", c=64)
            CAb = ROPA[:, r0:r0 + 8].unsqueeze(2).to_broadcast([128, 8, 64])
            SAb = ROPA[:, 32 + r0:32 + r0 + 8].unsqueeze(2).to_broadcast([128, 8, 64])
            CBb = ROPB[:, 0:64].unsqueeze(1).to_broadcast([128, 8, 64])
            SBb = ROPB[:, 64:128].unsqueeze(1).to_broadcast([128, 8, 64])
            S.op("dve", lambda e: e.tensor_tensor(out=v3(RT1), in0=v3(KNb), in1=CAb, op=ALU.mult), [rKNb, rConst], [rRT1])
            S.op("pool", lambda e: e.tensor_tensor(out=v3(RT1), in0=v3(RT1), in1=CBb, op=ALU.mult), [rRT1, rConst], [rRT1])
            S.op("dve", lambda e, pr=pr: e.tensor_tensor(out=v3(RT2), in0=v3(PS[pr][:]), in1=SAb, op=ALU.mult), [rPS[pr], rConst], [rRT2])
            S.op("pool", lambda e: e.tensor_tensor(out=v3(RT2), in0=v3(RT2), in1=SBb, op=ALU.mult), [rRT2, rConst], [rRT2])
            S.op("pool", lambda e: e.tensor_tensor(out=dst, in0=RT1, in1=RT2, op=ALU.add), [rRT1, rRT2], [rdst])

        S.dma("pool", lambda e: e.dma_start(out=WK2, in_=wq[:, 1024:1280].rearrange("(k p) n -> p k n", p=128)), [], [rWA])
        S.dma("pool", lambda e: e.dma_start(out=WV2, in_=wq[:, 1280:1536].rearrange("(k p) n -> p k n", p=128)), [], [rWA])
        gk = GQK[:, 2 * o_ + 1:2 * o_ + 2]
        nk_ = 0
        for t in range(5):
            for kc2 in range(2):
                for kc in range(8):
                    S.op("pe", lambda e, kc=kc, kc2=kc2, t=t: e.matmul(PS[PJ][:], WK2[:, kc, kc2 * 128:(kc2 + 1) * 128], H2[:, kc, t * 512:(t + 1) * 512],
                                                                    start=(kc == 0), stop=(kc == 7)), [rWA, rH[t]], [rPS[PJ]])
                if t == 0:
                    head_norm_rope(PS[PJ][:], gk, 0, KTP[:, kc2, :], rKTP)
                else:
                    bi = nk_ % 2
                    nk_ += 1
                    head_norm_rope(PS[PJ][:], gk, t, KTS[:, bi, :], rKTS[bi])
                    S.dma("sp", lambda e, bi=bi, kc2=kc2, t=t: e.dma_start(
                        out=ccKi.ap()[kc2 * 128:(kc2 + 1) * 128, (t - 1) * 512:t * 512], in_=KTS[:, bi, :]), [rKTS[bi], rCCo], [rCCi])
            if t + 2 < 5:
                norm_mod(1, t + 2, H2[:, :, (t + 2) * 512:(t + 3) * 512], rH[t + 2])
        nv_ = 0
        for pr2 in range(10):
            pb2 = sbank.next()
            for q in range(2):
                ti = pr2 * 2 + q
                for kc in range(8):
                    S.op("pe", lambda e, kc=kc, q=q, ti=ti, pb2=pb2: e.matmul(PS[pb2][:, q * 256:(q + 1) * 256], H2[:, kc, ti * 128:(ti + 1) * 128], WV2[:, kc, :],
                                                                          start=(kc == 0), stop=(kc == 7)), [rWA, rH[ti // 4]], [rPS[pb2]])
            if pr2 < 2:
                S.op("act", lambda e, pb2=pb2: e.activation(out=VS, in_=PS[pb2][:].rearrange("p (t f) -> p t f", f=256), func=AF.Copy), [rPS[pb2]], [rVS])
                S.op("dve", lambda e, pr2=pr2: e.tensor_copy(out=VTP[:, pr2 * 2:pr2 * 2 + 2, :], in_=VS), [rVS], [rVTP])
                S.dma("sp", lambda e, pr2=pr2: e.dma_start(out=o_atv[pr2, o_].rearrange("(t p) f -> p t f", p=128), in_=VS), [rVS], [])
            else:
                bi = nv_ % 2
                nv_ += 1
                S.op("dve", lambda e, pb2=pb2, bi=bi: e.tensor_copy(out=VST[:, bi], in_=PS[pb2][:].rearrange("p (t f) -> p t f", f=256)), [rPS[pb2]], [rVST[bi]])
                S.dma("sp", lambda e, bi=bi, pr2=pr2: e.dma_start(
                    out=ccVi.ap()[(pr2 - 2) * 256:(pr2 - 1) * 256, :].rearrange("(t p) f -> p t f", p=128), in_=VST[:, bi]), [rVST[bi], rCCo], [rCCi])
        for ti in range(4):
            pb2 = sbank.next()
            for kc in range(8):
                S.op("pe", lambda e, kc=kc, ti=ti, pb2=pb2: e.matmul(PS[pb2][:, 0:256], H2[:, kc, ti * 128:(ti + 1) * 128], WK2[:, kc, :],
                                                                 start=(kc == 0), stop=(kc == 7)), [rWA, rH[0]], [rPS[pb2]])
            S.op("act", lambda e, pb2=pb2: e.activation(out=KS, in_=PS[pb2][:, 0:256], func=AF.Square), [rPS[pb2]], [rKS])
            S.op("dve", lambda e: e.tensor_reduce(out=SS4, in_=KS.rearrange("p (h d) -> p h d", d=64), axis=AX.X, op=ALU.add), [rKS], [rSS4])
            S.op("act", lambda e: e.activation(out=SS4, in_=SS4, func=AF.Sqrt, bias=epsT[:], scale=1.0 / 64), [rSS4, rConst], [rSS4])
            S.op("dve", lambda e: e.reciprocal(out=SS4, in_=SS4), [rSS4], [rSS4])
            S.op("dve", lambda e, pb2=pb2: e.tensor_tensor(out=KS.rearrange("p (h d) -> p h d", d=64), in0=PS[pb2][:, 0:256].rearrange("p (h d) -> p h d", d=64),
                                                  in1=SS4.unsqueeze(2).to_broadcast([128, 4, 64]), op=ALU.mult), [rPS[pb2], rSS4, rKS], [rKS])
            S.op("dve", lambda e: e.tensor_tensor(out=KS.rearrange("p (h d) -> p h d", d=64), in0=KS.rearrange("p (h d) -> p h d", d=64),
                                                  in1=GKR[:, o_, :].unsqueeze(1).to_broadcast([128, 4, 64]), op=ALU.mult), [rKS, rConst], [rKS])
            S.dma("sp", lambda e, ti=ti: e.dma_start(out=o_atk[ti // 2, o_][(ti % 2) * 128:(ti % 2 + 1) * 128, :], in_=KS), [rKS], [])
        cc_allgather(ccKi, ccKo, nocc)
        cc_allgather(ccVi, ccVo, nocc)

        keepers = [rKTP, rVTP, rSQb, rKNb, rRT1, rRT2, rCS]
        pend1 = S.fence_ops([r for r in region if r not in keepers] + [rSQ])
        del region[:]
        region.extend(keepers)
        a = a_common
        KTF = ARB[:, a:a + 4352]; a += 4352
        VTF = ARB[:, a:a + 6528].rearrange("p (t f) -> p t f", f=192); a += 6528
        VPA = ARB[:, a:a + 1536].rearrange("p (t k f) -> p t k f", t=4, k=2); a += 1536
        WQj = ARB[:, a:a + 2048].rearrange("p (b k n) -> p b k n", b=2, k=8); a += 2048
        WOj = ARB[:, a:a + 2048].rearrange("p (b n) -> p b n", b=2); a += 2048
        QZ = ARB[:, a:a + 2048].rearrange("p (u h n) -> p u h n", u=2, h=2); a += 2048
        MJ = ARB[:, a:a + 1024].rearrange("p (u n) -> p u n", u=2); a += 1024
        PT = ARB[:, a:a + 1536].rearrange("p (b n) -> p b n", b=3); a += 1536
        SQ2 = SQb
        KN2 = KNb
        QRAW = ARF[:, 2560:3072]
        assert a <= 47104, a
        rKTF = newres("KTF", pend1); rVTF = newres("VTF", pend1); rVPA = newres("VPA", pend1)
        rWj = [newres("Wj%d" % i, pend1) for i in range(2)]
        rQZ = [newres("QZ%d" % i, pend1) for i in range(2)]
        rMJ = [newres("MJ%d" % i, pend1) for i in range(2)]
        rPT = [newres("PT%d" % i, pend1) for i in range(3)]
        rSQ2 = rSQb; rKN2 = rKNb
        rQRAW = newres("QRAW", pend1)
        ptrot = Rot([0, 1, 2])
        gqs = GQS[:, o_:o_ + 1]
        BD_B, ROT_B, OPJ, SWP_B, KW_B = 5, 6, 6, 5, 7
        KEEPWARM = True
        KWN = 128
        MSE = "dve" if os.environ.get("NOPOOLMS") else "pool"
        for u2 in range(2):
            S.op(MSE, lambda e, u2=u2: e.memset(QZ[64:128, u2, 0, :], 0.0), [], [rQZ[u2]])
            S.op(MSE, lambda e, u2=u2: e.memset(QZ[0:64, u2, 1, :], 0.0), [], [rQZ[u2]])
        S.op(MSE, lambda e: e.memset(VTF[:, :, 64:128], 1.0), [], [rVTF])
        S.op(MSE, lambda e: e.memset(VPA[:, :, :, 64:128], 1.0), [], [rVPA])
        for kc2 in range(2):
            S.op("dve", lambda e, kc2=kc2: e.tensor_copy(out=VPA[:, :, kc2, 0:64], in_=VTP[:, :, (2 * kc2) * 64:(2 * kc2 + 1) * 64]), [rVTP], [rVPA])
            S.op("dve", lambda e, kc2=kc2: e.tensor_copy(out=VPA[:, :, kc2, 128:192], in_=VTP[:, :, (2 * kc2 + 1) * 64:(2 * kc2 + 2) * 64]), [rVTP], [rVPA])
        S.op("dve", lambda e: e.memset(TMPF[:, 0, :], 0.0), [], [rTMPF[0]])

        units = [(kc2, j, t) for kc2 in range(2) for j in range(4) for t in range(5)]
        OSTG = float(os.environ.get("ODDSTG", "99"))
        if OSTG <= 0:
            return

        def load_kv(kc2):
            S.dma("sp", lambda e: e.dma_start(out=CS, in_=cak[o_][:, kc2 * 128:(kc2 + 1) * 128].rearrange("(t p) f -> p t f", p=128)), [], [rCS])
            for tt in range(2):
                S.op("pe", lambda e, tt=tt: e.transpose(PS[PJ][:, tt * 128:(tt + 1) * 128], CS[:, tt, :], ident[:]), [rCS, rConst], [rPS[PJ]])
            S.op("act", lambda e: e.activation(out=KTF[:, 0:256], in_=PS[PJ][:, 0:256], func=AF.Copy), [rPS[PJ]], [rKTF])
            for rk in range(2):
                S.dma("sp", lambda e, rk=rk: e.dma_start(out=KTF[:, 256 + rk * 2048:256 + (rk + 1) * 2048],
                                                     in_=ccKo.ap()[rk * 256 + kc2 * 128:rk * 256 + (kc2 + 1) * 128, :]), [rCCo], [rKTF])
            for g2 in range(2):
                c0 = kc2 * 128 + g2 * 64
                S.dma("pool", lambda e, g2=g2, c0=c0: e.dma_start(out=VTF[:, 0:2, g2 * 128:g2 * 128 + 64],
                                                              in_=cav[o_][:, c0:c0 + 64].rearrange("(t p) f -> p t f", p=128)), [], [rVTF])
                S.dma("sp", lambda e, g2=g2, c0=c0: e.dma_start(out=VTF[:, 2:34, g2 * 128:g2 * 128 + 64],
                                                            in_=ccVo.ap()[:, c0:c0 + 64].rearrange("(t p) f -> p t f", p=128)), [rCCo], [rVTF])

        def load_w(kc2, j, bj):
            for two in range(2):
                c0 = kc2 * 512 + two * 256 + j * 64
                S.dma("pool", lambda e, two=two, c0=c0: e.dma_start(
                    out=WQj[:, bj, :, two * 64:(two + 1) * 64], in_=wq[:, c0:c0 + 64].rearrange("(k p) n -> p k n", p=128)), [], [rWj[bj]])
                r0_ = (8 * kc2 + 4 * two + j) * 64
                S.dma("pool", lambda e, two=two, r0_=r0_: e.dma_start(
                    out=WOj[two * 64:(two + 1) * 64, bj, :], in_=w_out_c[o_][r0_:r0_ + 64, :]), [], [rWj[bj]])

        def wbuf(ui):
            kc2, j, t = units[ui]
            return (kc2 * 4 + j) % 2

        def stage0(ui):
            kc2, j, t = units[ui]
            bj = wbuf(ui)
            if t == 0:
                if j == 0:
                    load_kv(kc2)
                load_w(kc2, j, bj)
            for kc in range(8):
                S.op("pe", lambda e, kc=kc: e.matmul(PS[PJ][:], WQj[:, bj, kc, :], H2[:, kc, t * 512:(t + 1) * 512],
                                                 start=(kc == 0), stop=(kc == 7)), [rWj[bj], rH[t]], [rPS[PJ]])
            S.op("dve", lambda e: e.tensor_copy(out=QRAW, in_=PS[PJ][:]), [rPS[PJ]], [rQRAW])
            S.op("pool", lambda e: e.tensor_tensor(out=SQ2, in0=QRAW, in1=QRAW, op=ALU.mult), [rQRAW], [rSQ2])

        def stage1(ui):
            kc2, j, t = units[ui]
            u2 = ui % 2
            S.op("pe", lambda e: e.matmul(PS[BD_B][:], bdb[:], SQ2, start=True, stop=True), [rSQ2, rConst], [rPS[BD_B]])
            S.op("act", lambda e: e.activation(out=RSTD[:], in_=PS[BD_B][:], func=AF.Ln, bias=epsT[:], scale=1.0 / 64),
                 [rPS[BD_B], rConst], [rRSTD])
            S.op("act", lambda e: e.activation(out=RSTD[:], in_=RSTD[:], func=AF.Exp, scale=-0.5), [rRSTD], [rRSTD])
            if t == 0:
                for hf in range(2):
                    hs = slice(hf * 64, (hf + 1) * 64)
                    S.op("dve", lambda e, hs=hs, hf=hf: e.scalar_tensor_tensor(out=QZ[hs, u2, hf, :], in0=QRAW[hs, :], scalar=gqs[hs, :], in1=RSTD[hs, :],
                                                                        op0=ALU.mult, op1=ALU.mult), [rQRAW, rRSTD, rConst], [rQZ[u2]])
            else:
                S.op("dve", lambda e: e.scalar_tensor_tensor(out=KN2, in0=QRAW, scalar=gqs, in1=RSTD[:], op0=ALU.mult, op1=ALU.mult),
                     [rQRAW, rRSTD, rConst], [rKN2])

        def stage2(ui):
            kc2, j, t = units[ui]
            u2 = ui % 2
            if t == 0:
                return
            S.op("pe", lambda e: e.matmul(PS[ROT_B][:], rotb[:], KN2, start=True, stop=True), [rKN2, rConst], [rPS[ROT_B]])
            r0 = 8 * (t - 1)
            v3 = lambda ap: ap.rearrange("p (r c) -> p r c", c=64)
            CAb = ROPA[:, r0:r0 + 8].unsqueeze(2).to_broadcast([128, 8, 64])
            SAb = ROPA[:, 32 + r0:32 + r0 + 8].unsqueeze(2).to_broadcast([128, 8, 64])
            CBb = ROPB[:, 0:64].unsqueeze(1).to_broadcast([128, 8, 64])
            SBb = ROPB[:, 64:128].unsqueeze(1).to_broadcast([128, 8, 64])
            S.op("dve", lambda e: e.tensor_tensor(out=v3(RT1), in0=v3(KN2), in1=CAb, op=ALU.mult), [rKN2, rConst], [rRT1])
            S.op("pool", lambda e: e.tensor_tensor(out=v3(RT1), in0=v3(RT1), in1=CBb, op=ALU.mult), [rRT1, rConst], [rRT1])
            S.op("dve", lambda e: e.tensor_tensor(out=v3(RT2), in0=v3(PS[ROT_B][:]), in1=SAb, op=ALU.mult), [rPS[ROT_B], rConst], [rRT2])
            S.op("pool", lambda e: e.tensor_tensor(out=v3(RT2), in0=v3(RT2), in1=SBb, op=ALU.mult), [rRT2, rConst], [rRT2])
            for hf in range(2):
                hs = slice(hf * 64, (hf + 1) * 64)
                S.op("pool", lambda e, hs=hs, hf=hf: e.tensor_tensor(out=QZ[hs, u2, hf, :], in0=RT1[hs, :], in1=RT2[hs, :], op=ALU.add),
                     [rRT1, rRT2], [rQZ[u2]])

        def outproj_c(ui, c):
            kc2, j, t = units[ui]
            u2 = ui % 2
            bj = wbuf(ui)
            v = vsel(t)
            S.op("pe", lambda e: e.matmul(PS[OPJ][:], WOj[:, bj, c * 128:(c + 1) * 128], MJ[:, u2, :], start=True, stop=True),
                 [rWj[bj], rMJ[u2]], [rPS[OPJ]])
            S.op("dve", lambda e: e.scalar_tensor_tensor(
                out=X[:, c, t * 512:(t + 1) * 512], in0=PS[OPJ][:], scalar=HG[:, gate_s, c, v:v + 1], in1=X[:, c, t * 512:(t + 1) * 512],
                op0=ALU.mult, op1=ALU.add), [rPS[OPJ], rGS, rX[t][c]], [rX[t][c]])

        def attention_aug(qz, hf, nq, ktiles, out_ap, rQ_, rOut, hook):
            po = obank.next()
            hs = slice(hf * 64, (hf + 1) * 64)
            ss = slice((1 - hf) * 64, (2 - hf) * 64)
            n = len(ktiles)
            banks = {}

            def s_mm(i):
                kt = ktiles[i]
                pss = sbank.next()
                banks[i] = pss
                S.op("pe", lambda e, pss=pss, kt=kt: e.matmul(PS[pss][:, 0:nq], kt["kT"], qz, start=True, stop=True),
                     [rQ_] + kt["reads"], [rPS[pss]])

            pre = min(2, n)
            for i in range(pre):
                s_mm(i)
            for i, kt in enumerate(ktiles):
                pss = banks[i]
                pt = ptrot.next()
                S.op("act", lambda e, pss=pss, pt=pt: e.activation(out=PT[:, pt, 0:nq], in_=PS[pss][:, 0:nq], func=AF.Exp), [rPS[pss]], [rPT[pt]])
                if i + pre < n:
                    s_mm(i + pre)
                S.op("pe", lambda e, pt=pt, kt=kt, i=i: e.matmul(PS[po][:, 0:nq], kt["v"], PT[:, pt, 0:nq], start=(i == 0), stop=(i == n - 1)),
                     [rPT[pt]] + kt["reads"], [rPS[po]])
                if nq == 512 and KEEPWARM:
                    S.op("pe", lambda e, pt=pt: e.matmul(PS[KW_B][:, 0:KWN], identb[:], PT[:, pt, 0:KWN], start=True, stop=True),
                         [rPT[pt], rConst], [rPS[KW_B]])
                hook()
            S.op("dve", lambda e: e.reciprocal(out=TMPF[ss, 0, 0:nq], in_=PS[po][ss, 0:nq]), [rPS[po]], [rTMPF[0]])

            def fin():
                psw = SWP_B
                S.op("pe", lambda e: e.matmul(PS[psw][:, 0:nq], swpf[:], TMPF[:, 0, 0:nq], start=True, stop=True), [rTMPF[0], rConst], [rPS[psw]])
                S.op("act", lambda e: e.activation(out=TMPF[hs, 1, 0:nq], in_=PS[psw][hs, 0:nq], func=AF.Copy), [rPS[psw]], [rTMPF[1]])
                S.op("dve", lambda e: e.tensor_tensor(out=out_ap, in0=PS[po][hs, 0:nq], in1=TMPF[hs, 1, 0:nq], op=ALU.mult),
                     [rPS[po], rTMPF[1]], [rOut])
            return fin

        pending_fin = []

        def needs_kv(ui):
            return ui < len(units) and units[ui][1] == 0 and units[ui][2] == 0

        for ui, (kc2, j, t) in enumerate(units):
            u2 = ui % 2
            if needs_kv(ui):
                stage0(ui)
                if OSTG <= 0.3:
                    return
                stage1(ui)
                if OSTG <= 0.6:
                    return
                stage2(ui)
            if OSTG <= 1:
                return
            if OSTG < 50 and ui >= OSTG - 1:
                return
            nsteps = 8 if t == 0 else 68
            acts = {}
            nxt = ui + 1 < len(units) and not needs_kv(ui + 1)
            if nxt:
                acts.setdefault(0, []).append(lambda: stage0(ui + 1))
                acts.setdefault(16 if t > 0 else nsteps // 3, []).append(lambda: stage1(ui + 1))
                acts.setdefault(28 if t > 0 else (2 * nsteps) // 3, []).append(lambda: stage2(ui + 1))
            if ui >= 1:
                for c in range(8):
                    st_ = min(3 + c, 7) if t == 0 else 36 + 2 * c
                    acts.setdefault(st_, []).append(lambda c=c: outproj_c(ui - 1, c))
            cnt = [0]
            since = [99]

            def hook():
                k = cnt[0]
                cnt[0] += 1
                since[0] += 1
                if pending_fin and since[0] >= 9:
                    pending_fin.pop(0)()
                for fn_ in acts.get(k, []):
                    fn_()

            for hf in range(2):
                hs = slice(hf * 64, (hf + 1) * 64)
                if t == 0:
                    for bb in range(2):
                        kts = [dict(kT=KTP[:, kc2, bb * 256 + i * 128:bb * 256 + (i + 1) * 128],
                                    v=VPA[:, bb * 2 + i, kc2, hf * 64:hf * 64 + 128], nk=128, reads=[rKTP, rVPA]) for i in range(2)]
                        while pending_fin:
                            pending_fin.pop(0)()
                        pending_fin.append(attention_aug(QZ[:, u2, hf, bb * 256:(bb + 1) * 256], hf, 256, kts, MJ[hs, u2, bb * 256:(bb + 1) * 256], rQZ[u2], rMJ[u2], hook))
                        since[0] = 0
                else:
                    kts = [dict(kT=KTF[:, i * 128:(i + 1) * 128], v=VTF[:, i, hf * 64:hf * 64 + 128], nk=128, reads=[rKTF, rVTF]) for i in range(34)]
                    while len(pending_fin) > 1:
                        pending_fin.pop(0)()
                    pending_fin.append(attention_aug(QZ[:, u2, hf, :], hf, 512, kts, MJ[hs, u2, :], rQZ[u2], rMJ[u2], hook))
                    since[0] = 0
            assert cnt[0] == nsteps, (cnt[0], nsteps)
        while pending_fin:
            pending_fin.pop(0)()
        for c in range(8):
            outproj_c(len(units) - 1, c)
        pend2 = S.fence_ops(region)
        del region[:]
        for r in flatG + g_rW + [rSQ]:
            r.pend = list(pend2)

    dbg_n = [0]

    def dumpX(tag):
        if not debug:
            return
        o = dout("dbgX_%s" % tag, [128, NCH * T])
        allr = [r for t in range(5) for r in rX[t]]
        S.dma("sp", lambda e: e.dma_start(out=o, in_=X[:].rearrange("p c n -> p (c n)")), allr, [])

    def dumpS(tag, ap, n, reads):
        if not debug:
            return
        o = dout("dbgS_%s" % tag, [128, n])
        S.dma("sp", lambda e: e.dma_start(out=o, in_=ap), reads, [])

    dumpX("load")
    for l in range(n_layers):
        ada(l)
        if l == 0:
            dumpS("mod", MOD[:].rearrange("p m v -> p (m v)"), 144, [rMOD])
            dumpS("gs", GS[:].rearrange("p s c v -> p (s c v)"), 48, [rGS])
            dumpS("hg", HG[:].rearrange("p s c v -> p (s c v)"), 48, [rGS])
        ffn(l, 0, 0)
        if l == 0:
            dumpX("ffn0")
        if mix:
            if l % 2 == 0:
                even_mixer(l, mix == 2)
                if l == 0:
                    dumpX("mix0")
            else:
                odd_mixer(l, mix == 2)
                if l == 1:
                    dumpX("mix1")
        ffn(l, 2, 1)
    final_out()

    S.emit(stack)
    stack.close()
    return nc


def _consts(core):
    half = core % 2
    ident = np.eye(128, dtype=np.float32)
    rot = np.zeros((128, 128), np.float32)
    for m in range(128):
        if m % 32 < 16:
            rot[m + 16, m] = -1.0
        else:
            rot[m - 16, m] = 1.0
    bd = np.zeros((128, 128), np.float32)
    bd[:64, :64] = 1.0
    bd[64:, 64:] = 1.0
    t = np.arange(TS)
    row = (32 * half + t // 64).astype(np.float32)
    col = (t % 64).astype(np.float32)
    inv = (10000.0 ** (-np.arange(16, dtype=np.float32) / 16)).astype(np.float32)
    cos = np.zeros((128, TS), np.float32)
    sin = np.zeros((128, TS), np.float32)
    for p in range(128):
        d = p % 64
        pos = row if d < 32 else col
        ang = pos * inv[d % 16]
        cos[p] = np.cos(ang)
        sin[p] = np.sin(ang)
    cq = np.arange(64)
    cstart = np.clip(cq - 8, 0, 48)
    ck = np.arange(64)
    valid = (ck[None, :] >= cstart[:, None]) & (ck[None, :] < cstart[:, None] + 16)
    cm = np.where(valid.T, 0.0, NEG).astype(np.float32)
    cmt = np.tile(cm, (2, 8))
    sel = np.zeros((16, 8 * 128), np.float32)
    for kt in range(8):
        for kr in range(2):
            sel[2 * kt + kr, kt * 128 + kr * 64:kt * 128 + (kr + 1) * 64] = 1.0
    rm = np.full((16, 4 * 512), NEG, np.float32)
    for b in range(4):
        for qr in range(8):
            i = 8 * b + qr
            r = 32 * half + i
            rs = min(max(r - 4, 0), 56)
            for k in range(16):
                keyrow = 32 * half - 4 + 8 * b + k
                if rs <= keyrow <= rs + 7:
                    rm[k, b * 512 + qr * 64:b * 512 + (qr + 1) * 64] = 0.0
    invb = np.zeros((4, 6, 2, 8), np.float32)
    for g in range(4):
        w = 2 << g
        for si in range(6):
            for ed in range(2):
                for j in range(8):
                    if si < 2:
                        L = 256
                        tt = j if ed == 0 else 248 + j
                    else:
                        L = 4096
                        tt = half * 2048 + (si - 2) * 512 + (j if ed == 0 else 504 + j)
                    lo = min(max(tt - w // 2, 0), L - 1)
                    hi = min(max(tt + (w - 1 - w // 2), 0), L - 1)
                    invb[g, si, ed, j] = 1.0 / (hi - lo + 1)
    invb = np.tile(invb.reshape(1, -1), (128, 1)).astype(np.float32)
    hv = np.zeros((128, 2), np.float32)
    hv[:, 0] = 1.0 if half == 1 else 0.0
    hv[:, 1] = 1.0 if half == 0 else 0.0
    ropeA = np.ones((128, 64), np.float32)
    ropeB = np.ones((128, 128), np.float32)
    for p in range(128):
        d = p % 64
        f = inv[d % 16]
        if d < 32:
            ang = (32 * half + np.arange(32, dtype=np.float32)) * f
            ropeA[p, 0:32] = np.cos(ang)
            ropeA[p, 32:64] = np.sin(ang)
        else:
            ang = np.arange(64, dtype=np.float32) * f
            ropeB[p, 0:64] = np.cos(ang)
            ropeB[p, 64:128] = np.sin(ang)
    swp = np.zeros((128, 128), np.float32)
    for m in range(128):
        swp[(m + 64) % 128, m] = 1.0
    return dict(c_ident=ident, c_rot=rot, c_bd=bd, c_swp=swp, c_cos=cos, c_sin=sin,
                c_rm=rm, c_sel=sel, c_cmt=cmt, c_invb=invb, c_hv=hv, c_ropeA=ropeA, c_ropeB=ropeB)


_NC_CACHE = {}


def kernel(x_prompt, x_sample, cache_nb_k, cache_nb_v, cache_attn_k, cache_attn_v, c, c_ctx,
           w_mod, b_mod, g_norm, w_ffn_in, w_ffn_out, w_in_ab, w_pool, pool_scale, nb_rpb, w_out_ab,
           w_qkv_c, g_qnorm, g_knorm, w_out_c, g_final):
    f = lambda a: np.ascontiguousarray(np.asarray(a, dtype=np.float32))
    x_prompt, x_sample = f(x_prompt), f(x_sample)
    if "nc" not in _NC_CACHE:
        _NC_CACHE["nc"] = build_program(DEPTH, 1)
    nc = _NC_CACHE["nc"]
    rpbP = np.zeros((2, 8, 25, 128), np.float32)
    rpbP[:, :, 5:20, 48:79] = f(nb_rpb)[:, :, ::-1, ::-1]
    shared = dict(w_mod=f(w_mod), b_mod=f(b_mod), g_norm=f(g_norm), w_ffn_in=f(w_ffn_in), w_ffn_out=f(w_ffn_out),
                  w_in_ab=f(w_in_ab), w_pool=f(w_pool), pool_scale=f(pool_scale), rpbP=rpbP, w_out_ab=f(w_out_ab),
                  w_qkv_c=f(w_qkv_c), g_qnorm=f(g_qnorm), g_knorm=f(g_knorm), w_out_c=f(w_out_c), g_final=f(g_final))
    in_maps = []
    for core in range(8):
        seq, half = core // 2, core % 2
        xin = np.concatenate([x_prompt[2 * core].reshape(256, D), x_prompt[2 * core + 1].reshape(256, D),
                              x_sample[seq, half * TS:(half + 1) * TS]], axis=0)
        m = dict(shared)
        m.update(xin=np.ascontiguousarray(xin),
                 cvec=np.ascontiguousarray(np.stack([f(c_ctx), f(c)[seq]], 0)),
                 cnk=f(cache_nb_k)[seq].reshape(2, 256, 512), cnv=f(cache_nb_v)[seq].reshape(2, 256, 512),
                 cak=f(cache_attn_k)[seq].reshape(2, 256, 256), cav=f(cache_attn_v)[seq].reshape(2, 256, 256))
        m.update(_consts(core))
        in_maps.append(m)
    res = run_bass_kernel_spmd(nc, in_maps, core_ids=list(range(8)))
    R = res.results
    y_prompt = np.zeros((16, 256, D), np.float32)
    y_sample = np.zeros((4, 4096, D), np.float32)
    nbk = np.zeros((16, 2, 256, 8, 64), np.float32)
    nbv = np.zeros((16, 2, 256, 8, 64), np.float32)
    atk = np.zeros((16, 2, 256, 4, 64), np.float32)
    atv = np.zeros((16, 2, 256, 4, 64), np.float32)
    for core in range(8):
        seq, half = core // 2, core % 2
        yy = R[core]["y"]
        y_prompt[2 * core] = yy[0:256]
        y_prompt[2 * core + 1] = yy[256:512]
        y_sample[seq, half * TS:(half + 1) * TS] = yy[512:]
        nbk[2 * core:2 * core + 2] = R[core]["o_nbk"].reshape(2, 2, 256, 8, 64)
        nbv[2 * core:2 * core + 2] = R[core]["o_nbv"].reshape(2, 2, 256, 8, 64)
        atk[2 * core:2 * core + 2] = R[core]["o_atk"].reshape(2, 2, 256, 4, 64)
        atv[2 * core:2 * core + 2] = R[core]["o_atv"].reshape(2, 2, 256, 4, 64)
    return (y_prompt, y_sample, nbk, nbv, atk, atv)
```

```python
import os
import numpy as np
import ml_dtypes
from contextlib import ExitStack
import concourse.bass as bass
import concourse.mybir as mybir
from concourse.bass_utils import run_bass_kernel_spmd

F32 = mybir.dt.float32
BF16 = mybir.dt.bfloat16
ALU = mybir.AluOpType
AF = mybir.ActivationFunctionType
AX = mybir.AxisListType

D = 1024
NCH = 8
DFF = 2816
NJ = 22
TP = 512
TS = 2048
T = TP + TS
DEPTH = 4
EPS = 1e-6
NEG = -30000.0
PAIRS = [[0, 1], [2, 3], [4, 5], [6, 7]]
TILES = [(i * 512, 512) for i in range(5)]


class Res:
    __slots__ = ("w", "r", "name", "pend")

    def __init__(self, name="", pend=None):
        self.w = None
        self.r = []
        self.name = name
        self.pend = list(pend) if pend else None


class Op:
    __slots__ = ("eng", "fn", "deps", "need", "sem", "val", "is_dma", "idx", "pos")

    def __init__(self, eng, fn, is_dma):
        self.eng = eng
        self.fn = fn
        self.deps = []
        self.need = False
        self.sem = None
        self.val = 0
        self.is_dma = is_dma


class Sched:
    ENGS = ["pe", "act", "dve", "pool", "sp"]
    NDS = 24

    def __init__(self, nc):
        self.nc = nc
        self.ops = {e: [] for e in self.ENGS}
        self.all_dma = []
        self.barrier_op = None
        self.dma_since = []

    def _add(self, op, reads, writes):
        deps = []
        op.pos = len(self.ops[op.eng])
        for r in reads:
            if r.w is not None:
                deps.append(r.w)
            if r.pend:
                deps.extend(r.pend)
        for w in writes:
            if w.w is not None:
                deps.append(w.w)
            deps.extend(w.r)
            if w.pend:
                deps.extend(w.pend)
                w.pend = None
        if self.barrier_op is not None:
            deps.append(self.barrier_op)
        seen = set()
        for d in deps:
            if d is op or id(d) in seen:
                continue
            seen.add(id(d))
            if (not d.is_dma) and (not op.is_dma) and d.eng == op.eng and op.eng == "pe":
                continue
            op.deps.append(d)
            d.need = True
        for r in reads:
            r.r.append(op)
        for w in writes:
            w.w = op
            w.r = []
        self.ops[op.eng].append(op)

    def op(self, eng, fn, reads=(), writes=()):
        o = Op(eng, fn, False)
        self._add(o, reads, writes)
        return o

    def dma(self, q, fn, reads=(), writes=()):
        o = Op(q, fn, True)
        o.need = True
        self._add(o, reads, writes)
        self.all_dma.append(o)
        self.dma_since.append(o)
        return o

    def fence_ops(self, reslist):
        best = {}
        out = []
        seen = set()
        for r in reslist:
            for o in ([r.w] if r.w is not None else []) + list(r.r) + (list(r.pend) if r.pend else []):
                if id(o) in seen:
                    continue
                seen.add(id(o))
                if o.is_dma:
                    out.append(o)
                elif o.eng not in best or o.pos > best[o.eng].pos:
                    best[o.eng] = o
        return out + list(best.values())

    def barrier(self, fn):
        o = Op("dve", fn, False)
        deps = []
        for e in self.ENGS:
            for p in reversed(self.ops[e]):
                if not p.is_dma:
                    deps.append(p)
                    break
        deps.extend(self.dma_since)
        self.dma_since = []
        for d in deps:
            o.deps.append(d)
            d.need = True
        o.need = True
        self.ops["dve"].append(o)
        self.barrier_op = o
        return o

    def check(self):
        done = set()
        ptr = {e: 0 for e in self.ENGS}
        prog = True
        while prog:
            prog = False
            for e in self.ENGS:
                while ptr[e] < len(self.ops[e]):
                    o = self.ops[e][ptr[e]]
                    if all(id(d) in done for d in o.deps):
                        done.add(id(o))
                        ptr[e] += 1
                        prog = True
                    else:
                        break
        stuck = {e: (ptr[e], len(self.ops[e])) for e in self.ENGS if ptr[e] < len(self.ops[e])}
        if stuck:
            for e in stuck:
                o = self.ops[e][ptr[e]]
                print("STUCK", e, ptr[e], "deps not done:", [(d.eng, d.is_dma, self.ops[d.eng].index(d)) for d in o.deps if id(d) not in done])
            raise RuntimeError("deadlock in schedule: %s" % stuck)

    def emit(self, stack):
        nc = self.nc
        self.check()
        esem = {e: stack.enter_context(nc.semaphore("s_" + e)) for e in ["pe", "act", "dve", "pool"]}
        dsem = {q: [stack.enter_context(nc.semaphore("d_%s%d" % (q, i))) for i in range(self.NDS)]
                for q in ["pool", "sp"]}
        for e in self.ENGS:
            cnt = 0
            dcnt = 0
            for o in self.ops[e]:
                if o.is_dma:
                    o.sem = dsem[e][dcnt % self.NDS]
                    o.val = 16 * (dcnt // self.NDS + 1)
                    o.idx = dcnt
                    dcnt += 1
                elif o.need:
                    cnt += 1
                    o.sem = esem[e]
                    o.val = cnt
        engobj = {"pe": nc.tensor, "act": nc.scalar, "dve": nc.vector, "pool": nc.gpsimd, "sp": nc.sync}
        block = stack.enter_context(nc.Block())

        def make(e):
            def body(eng):
                waited = {}
                for o in self.ops[e]:
                    for d in o.deps:
                        k = id(d.sem)
                        if waited.get(k, 0) < d.val:
                            eng.wait_ge(d.sem, d.val)
                            waited[k] = d.val
                    if o.is_dma and o.val > 16:
                        k = id(o.sem)
                        if waited.get(k, 0) < o.val - 16:
                            eng.wait_ge(o.sem, o.val - 16)
                            waited[k] = o.val - 16
                    ins = o.fn(eng)
                    if o.is_dma:
                        ins.then_inc(o.sem, 16)
                    elif o.need:
                        ins.then_inc(o.sem, 1)
                last = {}
                for o in self.ops[e]:
                    if o.is_dma:
                        last[id(o.sem)] = (o.sem, o.val)
                for k, (s, v) in last.items():
                    if waited.get(k, 0) < v:
                        eng.wait_ge(s, v)
            return body

        block.tensor(make("pe"))
        block.scalar(make("act"))
        block.vector(make("dve"))
        block.gpsimd(make("pool"))
        block.sync(make("sp"))


def build_program(n_layers=DEPTH, mix=0, debug=False):
    nc = bass.Bass("TRN2", target_bir_lowering=False)
    stack = ExitStack()
    S = Sched(nc)

    def din(name, shape, dt=F32):
        return nc.dram_tensor(name, list(shape), dt, kind="ExternalInput").ap()

    def dout(name, shape, dt=F32):
        return nc.dram_tensor(name, list(shape), dt, kind="ExternalOutput").ap()

    xin = din("xin", [T, D])
    cvec = din("cvec", [2, D])
    w_mod = din("w_mod", [DEPTH, D, 9 * D])
    b_mod = din("b_mod", [DEPTH, 9 * D])
    g_norm = din("g_norm", [DEPTH, 3, D])
    w_ffn_in = din("w_ffn_in", [DEPTH, 2, D, 2 * DFF])
    w_ffn_out = din("w_ffn_out", [DEPTH, 2, DFF, D])
    w_in_ab = din("w_in_ab", [2, D, 2048])
    w_pool = din("w_pool", [2, 4, 128, 128])
    pool_scale = din("pool_scale", [2, 512])
    rpbP = din("rpbP", [2, 8, 25, 128])
    w_out_ab = din("w_out_ab", [2, D, D])
    w_qkv_c = din("w_qkv_c", [2, D, 1536])
    g_qnorm = din("g_qnorm", [2, 64])
    g_knorm = din("g_knorm", [2, 64])
    w_out_c = din("w_out_c", [2, D, D])
    g_final = din("g_final", [D])
    cnk = din("cnk", [2, 256, 512])
    cnv = din("cnv", [2, 256, 512])
    cak = din("cak", [2, 256, 256])
    cav = din("cav", [2, 256, 256])
    c_ident = din("c_ident", [128, 128])
    c_rot = din("c_rot", [128, 128])
    c_bd = din("c_bd", [128, 128])
    c_swp = din("c_swp", [128, 128])
    c_cos = din("c_cos", [128, TS])
    c_sin = din("c_sin", [128, TS])
    c_rm = din("c_rm", [16, 4 * 512])
    c_sel = din("c_sel", [16, 8 * 128])
    c_cmt = din("c_cmt", [128, 512])
    c_invb = din("c_invb", [128, 4 * 6 * 2 * 8])
    c_hv = din("c_hv", [128, 2])

    y = dout("y", [T, D])
    o_nbk = dout("o_nbk", [2, 2, 256, 512])
    o_nbv = dout("o_nbv", [2, 2, 256, 512])
    o_atk = dout("o_atk", [2, 2, 256, 256])
    o_atv = dout("o_atv", [2, 2, 256, 256])

    ccHi = nc.dram_tensor("ccHi", [2 * D, 256], BF16)
    ccHo = nc.dram_tensor("ccHo", [4 * D, 256], BF16)
    ccKi = nc.dram_tensor("ccKi", [256, TS], BF16)
    ccKo = nc.dram_tensor("ccKo", [512, TS], BF16)
    ccVi = nc.dram_tensor("ccVi", [TS, 256], BF16)
    ccVo = nc.dram_tensor("ccVo", [2 * TS, 256], BF16)
    rp_scr = nc.dram_tensor("rp_scr", [8 * 25 * 64, 128], F32)

    def sb(name, shape, dt):
        return stack.enter_context(nc.sbuf_tensor(name, list(shape), dt))

    def ps(name, shape, dt=F32):
        return stack.enter_context(nc.psum_tensor(name, list(shape), dt))

    X = sb("X", [128, NCH, T], F32)
    rX = [[Res("X%d_%d" % (t, c)) for c in range(NCH)] for t in range(5)]
    ARB = sb("ARB", [128, 51200], BF16)
    ARF = sb("ARF", [128, 3072], F32)
    ident = sb("ident", [128, 128], F32)
    identb = sb("identb", [128, 128], BF16)
    onesb = sb("onesb", [128, 128], BF16)
    rotb = sb("rotb", [128, 128], BF16)
    bdb = sb("bdb", [128, 128], BF16)
    swpf = sb("swpf", [128, 128], F32)
    epsT = sb("epsT", [128, 1], F32)
    MOD = sb("MOD", [128, 72, 2], F32)
    GS = sb("GS", [128, 3, NCH, 2], F32)
    HG = sb("HG", [128, 3, NCH, 2], F32)
    GN = sb("GN", [128, DEPTH * 3 * NCH], F32)
    BM = sb("BM", [128, 72], F32)
    GF = sb("GF", [128, NCH], F32)
    SC = sb("SC", [128, NCH, 2], F32)
    SCb = sb("SCb", [128, NCH, 2], BF16)
    RSTD = sb("RSTD", [128, 512], F32)
    TMPF = sb("TMPF", [128, 2, 512], F32)
    TMPB = sb("TMPB", [128, 2, 512], BF16)
    rConst = Res("const")
    rMOD = Res("MOD")
    rGS = Res("GS")
    rRSTD = Res("RSTD")
    rTMPF = [Res("TMPF0"), Res("TMPF1")]
    rTMPB = [Res("TMPB0"), Res("TMPB1")]

    PS = [ps("ps%d" % i, [128, 512]) for i in range(8)]
    rPS = [Res("ps%d" % i) for i in range(8)]

    class Rot:
        def __init__(self, idxs):
            self.idxs = idxs
            self.i = 0

        def next(self):
            k = self.idxs[self.i % len(self.idxs)]
            self.i += 1
            return k

    S.dma("sp", lambda e: e.dma_start(out=ident[:], in_=c_ident), [], [rConst])
    S.dma("pool", lambda e: e.dma_start(out=identb[:], in_=c_ident), [], [rConst])
    S.dma("pool", lambda e: e.dma_start(out=rotb[:], in_=c_rot), [], [rConst])
    S.dma("pool", lambda e: e.dma_start(out=bdb[:], in_=c_bd), [], [rConst])
    S.dma("sp", lambda e: e.dma_start(out=swpf[:], in_=c_swp), [], [rConst])
    S.op("dve", lambda e: e.memset(onesb[:], 1.0), [], [rConst])
    S.op("dve", lambda e: e.memset(epsT[:], EPS), [], [rConst])
    with nc.allow_non_contiguous_dma(reason="small param layouts"):
        S.dma("sp", lambda e: e.dma_start(out=GN[:], in_=g_norm.rearrange("l s (c p) -> p (l s c)", p=128), allow_slow_non_contiguous=True), [], [rConst])
        S.dma("sp", lambda e: e.dma_start(out=GF[:], in_=g_final.rearrange("(c p) -> p c", p=128), allow_slow_non_contiguous=True), [], [rConst])
        for vv in range(2):
            S.dma("sp", lambda e, vv=vv: e.dma_start(out=SC[:, :, vv], in_=cvec[vv].rearrange("(c p) -> p c", p=128), allow_slow_non_contiguous=True), [], [rConst])
    S.op("act", lambda e: e.activation(out=SCb[:], in_=SC[:], func=AF.Silu), [rConst], [rConst])

    XL = ARF[:, 0:2048].rearrange("p (a f) -> p a f", a=2)
    rXL = [Res("XL%d" % i) for i in range(2)]
    psr = Rot([0, 1, 2, 3])
    for t in range(5):
        for s4 in range(4):
            tk = t * 512 + s4 * 128
            sl = s4 % 2
            S.dma("sp", lambda e, sl=sl, tk=tk: e.dma_start(out=XL[:, sl, :], in_=xin[tk:tk + 128, :]), [], [rXL[sl]])
            for hf in range(2):
                b = psr.next()
                for cc in range(4):
                    c = hf * 4 + cc
                    S.op("pe", lambda e, b=b, sl=sl, c=c, cc=cc: e.transpose(PS[b][:, cc * 128:(cc + 1) * 128],
                                                                             XL[:, sl, c * 128:(c + 1) * 128], ident[:]),
                         [rXL[sl], rConst], [rPS[b]])
                if hf == 0:
                    S.op("dve", lambda e, b=b, hf=hf, tk=tk: e.tensor_copy(
                        out=X[:, hf * 4:(hf + 1) * 4, tk:tk + 128], in_=PS[b][:].rearrange("p (c n) -> p c n", c=4)),
                        [rPS[b]], [rX[t][hf * 4 + i] for i in range(4)])
                else:
                    S.op("act", lambda e, b=b, hf=hf, tk=tk: e.activation(
                        out=X[:, hf * 4:(hf + 1) * 4, tk:tk + 128], in_=PS[b][:].rearrange("p (c n) -> p c n", c=4), func=AF.Copy),
                        [rPS[b]], [rX[t][hf * 4 + i] for i in range(4)])

    def vsel(t):
        return 0 if t == 0 else 1

    BARS = sb("BARS", [128, 1], F32)

    def barrier():
        S.barrier(lambda e: e.memset(BARS[:], 0.0))

    Wbuf = ARB[:, 30720:43008].rearrange("p (b x) -> p b x", b=2)
    g_rW = [Res("W0"), Res("W1")]
    g_rH = [Res("H%d" % t) for t in range(5)]
    g_rG = [[Res("G%d_%d" % (b, t)) for t in range(5)] for b in range(2)]

    def ada(l):
        WM = Wbuf[:, :, 0:4096].rearrange("p b (k n) -> p b k n", k=8)
        rWM = g_rW
        with nc.allow_non_contiguous_dma(reason="bias layout"):
            S.dma("sp", lambda e: e.dma_start(out=BM[:], in_=b_mod[l].rearrange("(m p) -> p m", p=128), allow_slow_non_contiguous=True), [rMOD], [rMOD])
        pb = 7
        for blk in range(18):
            bi = blk % 2
            S.dma("pool", lambda e, bi=bi, blk=blk: e.dma_start(
                out=WM[:, bi], in_=w_mod[l][:, blk * 512:(blk + 1) * 512].rearrange("(k p) n -> p k n", p=128)),
                [], [rWM[bi]])
            for q in range(4):
                ch = blk * 4 + q
                for kc in range(8):
                    S.op("pe", lambda e, bi=bi, q=q, kc=kc, ch=ch: e.matmul(
                        PS[pb][:, ch * 2:ch * 2 + 2], WM[:, bi, kc, q * 128:(q + 1) * 128], SCb[:, kc, :],
                        start=(kc == 0), stop=(kc == 7)), [rWM[bi], rConst], [rPS[pb]])
        S.op("dve", lambda e: e.tensor_tensor(
            out=MOD[:], in0=PS[pb][:, 0:144].rearrange("p (m v) -> p m v", v=2),
            in1=BM[:].unsqueeze(2).to_broadcast([128, 72, 2]), op=ALU.add), [rPS[pb], rMOD], [rMOD])
        for s in range(3):
            gsl = GN[:, (l * 3 + s) * 8:(l * 3 + s + 1) * 8]
            S.op("dve", lambda e, s=s, gsl=gsl: e.scalar_tensor_tensor(
                out=GS[:, s], in0=MOD[:, (3 * s + 1) * 8:(3 * s + 2) * 8, :], scalar=1.0,
                in1=gsl.unsqueeze(2).to_broadcast([128, 8, 2]), op0=ALU.add, op1=ALU.mult), [rMOD, rConst], [rGS])
            S.op("dve", lambda e, s=s: e.tensor_scalar(
                out=HG[:, s], in0=MOD[:, (3 * s + 2) * 8:(3 * s + 3) * 8, :], scalar1=(1.0 if s == 1 else 0.5),
                scalar2=None, op0=ALU.mult), [rMOD], [rGS])

    def rstd_from_ps(pb, n, inv_d, reads):
        S.op("act", lambda e: e.activation(out=RSTD[:, 0:n], in_=PS[pb][:, 0:n], func=AF.Sqrt, bias=epsT[:], scale=inv_d),
             [rPS[pb], rConst] + reads, [rRSTD])
        S.op("dve", lambda e: e.reciprocal(out=RSTD[:, 0:n], in_=RSTD[:, 0:n]), [rRSTD], [rRSTD])

    SQ = ARB[:, 43008:47104].rearrange("p (c n) -> p c n", c=8)
    rSQ = Res("SQ")

    def norm_mod(s, t, Hdst, rH):
        v = vsel(t)
        t0 = t * 512
        S.op("act", lambda e: e.activation(out=SQ[:], in_=X[:, :, t0:t0 + 512], func=AF.Square), rX[t], [rSQ])
        pb = 6
        for c in range(8):
            S.op("pe", lambda e, c=c: e.matmul(PS[pb][:], onesb[:], SQ[:, c, :], start=(c == 0), stop=(c == 7)),
                 [rSQ, rConst], [rPS[pb]])
        rstd_from_ps(pb, 512, 1.0 / D, [])
        for c in range(8):
            k = c % 2
            S.op("dve", lambda e, c=c, k=k: e.tensor_tensor(out=TMPF[:, k], in0=X[:, c, t0:t0 + 512], in1=RSTD[:], op=ALU.mult),
                 [rX[t][c], rRSTD], [rTMPF[k]])
            S.op("act", lambda e, c=c, k=k: e.activation(out=Hdst[:, c, :], in_=TMPF[:, k], func=AF.Identity,
                                                          bias=MOD[:, (3 * s) * 8 + c, v:v + 1], scale=GS[:, s, c, v:v + 1]),
                 [rTMPF[k], rGS, rMOD], [rH])

    def ffn(l, s, wi):
        H = ARB[:, 0:20480].rearrange("p (c n) -> p c n", c=8)
        G = ARB[:, 20480:30720].rearrange("p (b j n) -> p b j n", b=2, j=2)
        W = Wbuf
        rH, rG, rW = g_rH, g_rG, g_rW
        for t in range(2):
            norm_mod(s, t, H[:, :, t * 512:(t + 1) * 512], rH[t])
        win = w_ffn_in[l, wi]
        wout = w_ffn_out[l, wi]
        pin = Rot([0, 1, 2, 3])
        pout = Rot([4, 5, 6, 7])
        for jp in range(11):
            b = jp % 2
            WA = W[:, b, 0:2048].rearrange("p (k n) -> p k n", k=8)
            WB = W[:, b, 2048:4096].rearrange("p (k n) -> p k n", k=8)
            WO = W[:, b, 4096:6144].rearrange("p (j n) -> p j n", j=2)
            S.dma("pool", lambda e, WA=WA, jp=jp: e.dma_start(
                out=WA, in_=win[:, jp * 256:(jp + 1) * 256].rearrange("(k p) n -> p k n", p=128)), [], [rW[b]])
            S.dma("pool", lambda e, WB=WB, jp=jp: e.dma_start(
                out=WB, in_=win[:, DFF + jp * 256:DFF + (jp + 1) * 256].rearrange("(k p) n -> p k n", p=128)), [], [rW[b]])
            S.dma("pool", lambda e, WO=WO, jp=jp: e.dma_start(
                out=WO, in_=wout[jp * 256:(jp + 1) * 256, :].rearrange("(j p) n -> p j n", p=128)), [], [rW[b]])
            for t in range(5):
                t0 = t * 512
                for jj in range(2):
                    pa = pin.next()
                    pbk = pin.next()
                    for kc in range(8):
                        S.op("pe", lambda e, pa=pa, kc=kc, jj=jj, WA=WA, t0=t0: e.matmul(
                            PS[pa][:], WA[:, kc, jj * 128:(jj + 1) * 128], H[:, kc, t0:t0 + 512],
                            start=(kc == 0), stop=(kc == 7)), [rW[b], rH[t]], [rPS[pa]])
                    for kc in range(8):
                        S.op("pe", lambda e, pbk=pbk, kc=kc, jj=jj, WB=WB, t0=t0: e.matmul(
                            PS[pbk][:], WB[:, kc, jj * 128:(jj + 1) * 128], H[:, kc, t0:t0 + 512],
                            start=(kc == 0), stop=(kc == 7)), [rW[b], rH[t]], [rPS[pbk]])
                    k = jj
                    S.op("act", lambda e, pa=pa, k=k: e.activation(out=TMPB[:, k], in_=PS[pa][:], func=AF.Silu),
                         [rPS[pa]], [rTMPB[k]])
                    S.op("dve", lambda e, pbk=pbk, k=k, jj=jj, t0=t0, b=b: e.tensor_tensor(
                        out=G[:, b, jj, t0:t0 + 512], in0=TMPB[:, k], in1=PS[pbk][:], op=ALU.mult),
                        [rTMPB[k], rPS[pbk]], [rG[b][t]])
                if jp == 0 and t + 2 < 5:
                    norm_mod(s, t + 2, H[:, :, (t + 2) * 512:(t + 3) * 512], rH[t + 2])
            for t in range(5):
                t0 = t * 512
                v = vsel(t)
                for c in range(8):
                    po = pout.next()
                    for jj in range(2):
                        S.op("pe", lambda e, po=po, jj=jj, c=c, WO=WO, t0=t0, b=b: e.matmul(
                            PS[po][:], WO[:, jj, c * 128:(c + 1) * 128], G[:, b, jj, t0:t0 + 512],
                            start=(jj == 0), stop=(jj == 1)), [rW[b], rG[b][t]], [rPS[po]])
                    S.op("dve", lambda e, po=po, c=c, t0=t0, v=v: e.scalar_tensor_tensor(
                        out=X[:, c, t0:t0 + 512], in0=PS[po][:], scalar=HG[:, s, c, v:v + 1], in1=X[:, c, t0:t0 + 512],
                        op0=ALU.mult, op1=ALU.add), [rPS[po], rGS, rX[t][c]], [rX[t][c]])

    def final_out():
        YT = ARF[:, 0:2048].rearrange("p (a f) -> p a f", a=2)
        rYT = rXL
        pr = Rot([0, 1, 2, 3, 4, 5])
        XGt = ARF[:, 2048:2560]
        XG2 = [ARF[:, 2048:2560].rearrange("p (c n) -> p c n", c=4), ARF[:, 2560:3072].rearrange("p (c n) -> p c n", c=4)]
        rXG2 = [Res("XGa"), Res("XGb")]
        rXG = Res("XG")
        RS1 = sb("RS1", [128, 1], F32)
        rRS1 = Res("RS1")
        for t in range(5):
            for s4 in range(4):
                tk = t * 512 + s4 * 128
                S.op("act", lambda e, tk=tk: e.activation(out=SQ[:, :, 0:128], in_=X[:, :, tk:tk + 128], func=AF.Square),
                     rX[t], [rSQ])
                pb = 6
                for c in range(8):
                    S.op("pe", lambda e, c=c: e.matmul(PS[pb][:, 0:1], SQ[:, c, 0:128], onesb[:, 0:1],
                                                       start=(c == 0), stop=(c == 7)), [rSQ, rConst], [rPS[pb]])
                S.op("act", lambda e: e.activation(out=RS1[:], in_=PS[pb][:, 0:1], func=AF.Sqrt, bias=epsT[:], scale=1.0 / D),
                     [rPS[pb], rConst], [rRS1])
                S.op("dve", lambda e: e.reciprocal(out=RS1[:], in_=RS1[:]), [rRS1], [rRS1])
                for half in range(2):
                    b = pr.next()
                    xg = XG2[half]
                    rxg = rXG2[half]
                    S.op("dve", lambda e, half=half, tk=tk, xg=xg: e.tensor_tensor(
                        out=xg, in0=X[:, half * 4:(half + 1) * 4, tk:tk + 128],
                        in1=GF[:, half * 4:(half + 1) * 4].unsqueeze(2).to_broadcast([128, 4, 128]), op=ALU.mult),
                        [rX[t][half * 4 + i] for i in range(4)] + [rConst], [rxg])
                    for cc in range(4):
                        S.op("pe", lambda e, b=b, cc=cc, xg=xg: e.transpose(PS[b][:, cc * 128:(cc + 1) * 128], xg[:, cc, :], ident[:]),
                             [rxg, rConst], [rPS[b]])
                    S.op("dve", lambda e, b=b, half=half, s4=s4: e.tensor_scalar(
                        out=YT[:, s4 % 2, half * 512:(half + 1) * 512], in0=PS[b][:], scalar1=RS1[:, 0:1], scalar2=None, op0=ALU.mult),
                        [rPS[b], rRS1], [rYT[s4 % 2]])
                S.dma("sp", lambda e, s4=s4, tk=tk: e.dma_start(out=y[tk:tk + 128, :], in_=YT[:, s4 % 2, :]), [rYT[s4 % 2]], [])


    M0 = 20480
    region = []
    flatG = [r for bb in g_rG for r in bb]

    def newres(name, pend):
        r = Res(name, pend)
        region.append(r)
        return r

    rCCi = Res("cci")
    rCCo = Res("cco")
    rRP = Res("rp_scr")
    CMT = ARB[:, 47104:47616]
    SEL = ARB[0:16, 47616:48640]
    RMB = ARB[0:16, 48640:50688]
    S.dma("pool", lambda e: e.dma_start(out=CMT, in_=c_cmt), [], [rConst])
    S.dma("pool", lambda e: e.dma_start(out=SEL, in_=c_sel), [], [rConst])
    S.dma("pool", lambda e: e.dma_start(out=RMB, in_=c_rm), [], [rConst])
    INVB = sb("INVB", [128, 4 * 6 * 2 * 8], F32)
    HV = sb("HV", [128, 2], F32)
    PSCL = sb("PSCL", [128, 2 * 4], F32)
    S.dma("sp", lambda e: e.dma_start(out=INVB[:], in_=c_invb), [], [rConst])
    S.dma("sp", lambda e: e.dma_start(out=HV[:], in_=c_hv), [], [rConst])
    S.dma("sp", lambda e: e.dma_start(out=PSCL[:], in_=pool_scale.rearrange("e (g p) -> p (e g)", p=128),
                                      allow_slow_non_contiguous=True), [], [rConst])

    sbank = Rot([0, 1, 2])
    obank = Rot([3, 4])
    mbank = Rot([5, 6])
    PJ = 7

    def cc_allgather(cin, cout, nocc):
        if nocc:
            n = cin.shape[0]
            S.dma("sp", lambda e: e.dma_start(out=cout.ap()[0:n, :], in_=cin.ap()), [rCCi], [rCCo])
            S.dma("sp", lambda e: e.dma_start(out=cout.ap()[n:2 * n, :], in_=cin.ap()), [rCCi], [rCCo])
        else:
            S.op("pool", lambda e: e.collective_compute("AllGather", ALU.bypass, replica_groups=PAIRS,
                                                        ins=[cin.ap().opt()], outs=[cout.ap().opt()]),
                 [rCCi], [rCCo])

    def attention(qT, half, nq, ktiles, out_ap, rQ, rOut, PT, rPT, ptrot, hook=None, sb=None, pre_n=2):
        sb = sb or sbank
        po = obank.next()
        pm = mbank.next()
        hs = slice(half * 64, (half + 1) * 64)
        n = len(ktiles)
        banks = {}

        def s_mm(i):
            kt = ktiles[i]
            nk = kt["nk"]
            pss = sb.next()
            banks[i] = pss
            terms = [(kt["kT"], qT, [rQ] + kt["reads"])] + kt.get("bias", [])
            for j, (lt, rh, rd) in enumerate(terms):
                S.op("pe", lambda e, pss=pss, lt=lt, rh=rh, j=j, nt=len(terms), nk=nk: e.matmul(
                    PS[pss][0:nk, 0:nq], lt, rh, start=(j == 0), stop=(j == nt - 1)), rd, [rPS[pss]])

        pre = min(pre_n, n)
        for i in range(pre):
            s_mm(i)
        for i, kt in enumerate(ktiles):
            nk = kt["nk"]
            pss = banks[i]
            pt = ptrot.next()
            S.op("act", lambda e, pss=pss, pt=pt, nk=nk: e.activation(out=PT[0:nk, pt, 0:nq], in_=PS[pss][0:nk, 0:nq], func=AF.Exp),
                 [rPS[pss]], [rPT[pt]])
            if i + pre < n:
                s_mm(i + pre)
            S.op("pe", lambda e, pt=pt, kt=kt, i=i, nk=nk: e.matmul(PS[po][hs, 0:nq], kt["v"], PT[0:nk, pt, 0:nq],
                                                                 start=(i == 0), stop=(i == n - 1)),
                 [rPT[pt]] + kt["reads"], [rPS[po]])
            S.op("pe", lambda e, pt=pt, i=i, nk=nk: e.matmul(PS[pm][hs, 0:nq], onesb[0:nk, 0:64], PT[0:nk, pt, 0:nq],
                                                         start=(i == 0), stop=(i == n - 1)),
                 [rPT[pt], rConst], [rPS[pm]])
            if hook is not None:
                hook()
        S.op("dve", lambda e: e.reciprocal(out=TMPF[hs, 0, 0:nq], in_=PS[pm][hs, 0:nq]), [rPS[pm]], [rTMPF[0]])
        S.op("dve", lambda e: e.tensor_tensor(out=out_ap, in0=PS[po][hs, 0:nq], in1=TMPF[hs, 0, 0:nq], op=ALU.mult),
             [rPS[po], rTMPF[0]], [rOut])

    def proj_fm(Wt, rW_, rhs_fn, n, evac, bank=None):
        bk = PJ if bank is None else bank
        for kc in range(8):
            rh, rd = rhs_fn(kc)
            S.op("pe", lambda e, kc=kc, rh=rh: e.matmul(PS[bk][:, 0:n], Wt[:, kc, :], rh, start=(kc == 0), stop=(kc == 7)),
                 [rW_] + rd, [rPS[bk]])
        if bank is None:
            evac(PS[bk])
        else:
            evac(PS[bk], bk)

    def even_mixer(l, nocc):
        e_ = l // 2
        H2 = ARB[:, 0:20480].rearrange("p (c n) -> p c n", c=8)
        rH = g_rH
        for t in range(5):
            norm_mod(1, t, H2[:, :, t * 512:(t + 1) * 512], rH[t])
        pend0 = S.fence_ops(flatG + g_rW)
        del region[:]
        for q4 in range(4):
            S.dma("sp", lambda e, q4=q4: e.dma_start(
                out=rp_scr.ap()[q4 * 3200:(q4 + 1) * 3200, :].rearrange("(hs ck) m -> hs ck m", ck=64),
                in_=rpbP[e_].rearrange("h s m -> (h s) m")[q4 * 50:(q4 + 1) * 50, :].unsqueeze(1).to_broadcast([50, 64, 128])),
                [rRP], [rRP])
        HH = ARB[:, M0:M0 + 4096].rearrange("p (c n) -> p c n", c=8)
        rHH = newres("HH", pend0)
        S.dma("sp", lambda e: e.dma_start(out=ccHi.ap()[0:1024, :].rearrange("(c p) n -> p c n", p=128), in_=H2[:, :, 512:768]),
              [rH[1], rCCo], [rCCi])
        S.dma("sp", lambda e: e.dma_start(out=ccHi.ap()[1024:2048, :].rearrange("(c p) n -> p c n", p=128), in_=H2[:, :, 2304:2560]),
              [rH[4], rCCo], [rCCi])
        cc_allgather(ccHi, ccHo, nocc)
        for h2 in range(2):
            S.dma("sp", lambda e, h2=h2: e.dma_start(
                out=HH[:, :, h2 * 256:(h2 + 1) * 256],
                in_=ccHo.ap()[1024 + h2 * 1024:2048 + h2 * 1024, :].rearrange("(c p) n -> p c n", p=128)), [rCCo], [rHH])

        gate_s = 1
        base = M0 + 4096
        STG = float(os.environ.get("MIXSTAGE", "99"))
        if STG <= 1:
            return
        WU = ARB[:, base:base + 1024].rearrange("p (k n) -> p k n", k=8)
        WP = ARB[:, base + 1024:base + 1152]
        WOr = ARB[:, base + 1152:base + 2176]
        PL = ARB[:, base + 2176:base + 2688]
        AO = ARB[:, base + 2688:base + 3200]
        UE = ARF[:, 0:528]
        T1 = ARF[:, 528:1056]
        T2 = ARF[:, 1056:1584]
        E8 = ARF[:, 1584:1600]
        rWU = newres("WU", pend0)
        rPL = newres("PL", pend0)
        rAO = newres("AO", pend0)
        rUE = newres("UE", pend0)
        rT1 = newres("T1", pend0)
        rT2 = newres("T2", pend0)
        rE8 = newres("E8", pend0)
        S.op("dve", lambda e: e.memset(UE, 0.0), [], [rUE])
        segs = [(0, 256, 0), (256, 256, 0), (512, 512, 1), (1024, 512, 2), (1536, 512, 3), (2048, 512, 4)]
        for g in range(4):
            w = 2 << g
            S.dma("pool", lambda e, g=g: e.dma_start(out=WU, in_=w_in_ab[e_][:, g * 128:(g + 1) * 128].rearrange("(k p) n -> p k n", p=128)),
                  [], [rWU])
            S.dma("pool", lambda e, g=g: e.dma_start(out=WP, in_=w_pool[e_, g]), [], [rWU])
            S.dma("pool", lambda e, g=g: e.dma_start(out=WOr, in_=w_out_ab[e_][g * 128:(g + 1) * 128, :]), [], [rWU])
            for si, (t0, L, t) in enumerate(segs):
                v = vsel(t)
                for kc in range(8):
                    S.op("pe", lambda e, kc=kc, t0=t0, L=L: e.matmul(PS[PJ][:, 0:L], WU[:, kc, :], H2[:, kc, t0:t0 + L],
                                                                   start=(kc == 0), stop=(kc == 7)), [rWU, rH[t]], [rPS[PJ]])
                S.op("act", lambda e, L=L: e.activation(out=UE[:, 8:8 + L], in_=PS[PJ][:, 0:L], func=AF.Copy), [rPS[PJ]], [rUE])
                if t >= 1:
                    pb2 = sbank.next()
                    for kc in range(8):
                        lh = HH[:, kc, 248:256] if t == 1 else H2[:, kc, t0 - 8:t0]
                        S.op("pe", lambda e, kc=kc, lh=lh, pb2=pb2: e.matmul(PS[pb2][:, 0:8], WU[:, kc, :], lh, start=(kc == 0), stop=(kc == 7)),
                             [rWU, rHH, rH[t - 1]], [rPS[pb2]])
                    for kc in range(8):
                        rh_ = HH[:, kc, 256:264] if t == 4 else H2[:, kc, t0 + 512:t0 + 520]
                        S.op("pe", lambda e, kc=kc, rh_=rh_, pb2=pb2: e.matmul(PS[pb2][:, 8:16], WU[:, kc, :], rh_, start=(kc == 0), stop=(kc == 7)),
                             [rWU, rHH, rH[min(t + 1, 4)]], [rPS[pb2]])
                    if t == 1:
                        S.op("dve", lambda e, pb2=pb2: e.tensor_scalar(out=UE[:, 0:8], in0=PS[pb2][:, 0:8], scalar1=HV[:, 0:1], scalar2=None, op0=ALU.mult),
                             [rPS[pb2], rConst], [rUE])
                    else:
                        S.op("dve", lambda e, pb2=pb2: e.tensor_copy(out=UE[:, 0:8], in_=PS[pb2][:, 0:8]), [rPS[pb2]], [rUE])
                    if t == 4:
                        S.op("dve", lambda e, pb2=pb2, L=L: e.tensor_scalar(out=UE[:, 8 + L:16 + L], in0=PS[pb2][:, 8:16], scalar1=HV[:, 1:2], scalar2=None, op0=ALU.mult),
                             [rPS[pb2], rConst], [rUE])
                    else:
                        S.op("dve", lambda e, pb2=pb2, L=L: e.tensor_copy(out=UE[:, 8 + L:16 + L], in_=PS[pb2][:, 8:16]), [rPS[pb2]], [rUE])
                else:
                    S.op("dve", lambda e, L=L: e.memset(UE[:, 8 + L:16 + L], 0.0), [], [rUE])
                    S.op("dve", lambda e: e.memset(UE[:, 0:8], 0.0), [], [rUE])
                Lp = L + 16
                S.op("dve", lambda e, Lp=Lp: e.tensor_tensor(out=T1[:, 0:Lp - 1], in0=UE[:, 0:Lp - 1], in1=UE[:, 1:Lp], op=ALU.add), [rUE], [rT1])
                cur, rcur, oth, roth = T1, rT1, T2, rT2
                ln = Lp - 1
                sh = 1
                for k in range(g):
                    sh *= 2
                    nl = ln - sh
                    S.op("dve", lambda e, cur=cur, oth=oth, nl=nl, sh=sh: e.tensor_tensor(out=oth[:, 0:nl], in0=cur[:, 0:nl], in1=cur[:, sh:sh + nl], op=ALU.add),
                         [rcur], [roth])
                    cur, rcur, oth, roth = oth, roth, cur, rcur
                    ln = nl
                off = 8 - w // 2
                S.op("dve", lambda e, cur=cur, off=off, L=L, w=w: e.scalar_tensor_tensor(
                    out=PL[:, 0:L], in0=cur[:, off:off + L], scalar=1.0 / w, in1=UE[:, 8:8 + L], op0=ALU.mult, op1=ALU.subtract),
                    [rcur, rUE], [rPL])
                for ed in range(2):
                    a0 = 0 if ed == 0 else L - 8
                    ib = ((g * 6 + si) * 2 + ed) * 8
                    S.op("dve", lambda e, cur=cur, off=off, a0=a0, ib=ib: e.tensor_tensor(
                        out=E8[:, 0:8], in0=cur[:, off + a0:off + a0 + 8], in1=INVB[:, ib:ib + 8], op=ALU.mult), [rcur, rConst], [rE8])
                    S.op("dve", lambda e, a0=a0: e.tensor_tensor(out=PL[:, a0:a0 + 8], in0=E8[:, 0:8], in1=UE[:, 8 + a0:16 + a0], op=ALU.subtract),
                         [rE8, rUE], [rPL])
                S.op("pe", lambda e, L=L: e.matmul(PS[PJ][:, 0:L], WP, PL[:, 0:L], start=True, stop=True), [rWU, rPL], [rPS[PJ]])
                S.op("act", lambda e, L=L, g=g: e.activation(out=AO[:, 0:L], in_=PS[PJ][:, 0:L], func=AF.Copy,
                                                             scale=PSCL[:, e_ * 4 + g:e_ * 4 + g + 1]), [rPS[PJ], rConst], [rAO])
                for c in range(8):
                    pbo = sbank.next()
                    S.op("pe", lambda e, c=c, pbo=pbo, L=L: e.matmul(PS[pbo][:, 0:L], WOr[:, c * 128:(c + 1) * 128], AO[:, 0:L], start=True, stop=True),
                         [rWU, rAO], [rPS[pbo]])
                    S.op("dve", lambda e, c=c, pbo=pbo, L=L, t0=t0, v=v: e.scalar_tensor_tensor(
                        out=X[:, c, t0:t0 + L], in0=PS[pbo][:, 0:L], scalar=HG[:, gate_s, c, v:v + 1], in1=X[:, c, t0:t0 + L],
                        op0=ALU.mult, op1=ALU.add), [rPS[pbo], rGS, rX[t][c]], [rX[t][c]])

        if STG <= 2:
            return
        keep = [rHH]
        pend1 = S.fence_ops([r for r in region if r is not rHH])
        del region[:]
        region.append(rHH)
        o = base
        KT = ARB[:, o:o + 3072]; o += 3072
        VT = ARB[:, o:o + 3072].rearrange("p (t f) -> p t f", f=128); o += 3072
        QT = ARB[:, o:o + 512]; o += 512
        MB = ARB[:, o:o + 2560]; o += 2560
        TB = ARB[:, o:o + 3072].rearrange("p (h x) -> p h x", h=2); o += 3072
        PT = ARB[:, o:o + 1536].rearrange("p (b n) -> p b n", b=3); o += 1536
        KTC = ARB[:, o:o + 256]; o += 256
        VTC = ARB[:, o:o + 256].rearrange("p (t f) -> p t f", f=128); o += 256
        WQ = ARB[:, o:o + 1024].rearrange("p (k n) -> p k n", k=8); o += 1024
        WK = ARB[:, o:o + 1024].rearrange("p (k n) -> p k n", k=8); o += 1024
        WV = ARB[:, o:o + 1024].rearrange("p (k n) -> p k n", k=8); o += 1024
        WO2 = ARB[:, o:o + 1024]; o += 1024
        assert o <= 43008, o
        KS = ARF[:, 0:512].rearrange("p (t f) -> p t f", f=128)
        VS = ARF[:, 512:1024].rearrange("p (t f) -> p t f", f=128)
        CS = ARF[:, 1024:1280].rearrange("p (t f) -> p t f", f=128)
        rKT = newres("KT", pend1); rVT = newres("VT", pend1); rQT = newres("QT", pend1); rMB = newres("MB", pend1)
        rQTb = [rQT, newres("QT2", pend1)]
        rMBt = [newres("MB%d" % i, pend1) for i in range(5)]
        rTB = newres("TB", pend1); rPT = [newres("PT%d" % i, pend1) for i in range(3)]
        rKTC = newres("KTC", pend1); rVTC = newres("VTC", pend1); rWA = newres("WA", pend1)
        rKS = newres("KS", pend1); rVS = newres("VS", pend1); rCS = newres("CS", pend1)
        ptrot = Rot([0, 1, 2])
        wia = w_in_ab[e_]
        for hc in range(4):
            S.dma("pool", lambda e, hc=hc: e.dma_start(out=WQ, in_=wia[:, 512 + hc * 128:640 + hc * 128].rearrange("(k p) n -> p k n", p=128)), [], [rWA])
            S.dma("pool", lambda e, hc=hc: e.dma_start(out=WK, in_=wia[:, 1024 + hc * 128:1152 + hc * 128].rearrange("(k p) n -> p k n", p=128)), [], [rWA])
            S.dma("pool", lambda e, hc=hc: e.dma_start(out=WV, in_=wia[:, 1536 + hc * 128:1664 + hc * 128].rearrange("(k p) n -> p k n", p=128)), [], [rWA])
            S.dma("pool", lambda e, hc=hc: e.dma_start(out=WO2, in_=w_out_ab[e_][512 + hc * 128:640 + hc * 128, :]), [], [rWA])
            for hf in range(2):
                h = 2 * hc + hf
                for kr in range(2):
                    if os.environ.get("NOTB"):
                        continue
                    off = ((h * 25 + 1 - kr) * 64) * 128 + 63
                    src = bass.AP(tensor=rp_scr, offset=off, ap=[[127, 64], [8192, 24], [1, 64]])
                    S.dma("pool", lambda e, hf=hf, kr=kr, src=src: e.dma_start(
                        out=TB[kr * 64:(kr + 1) * 64, hf, :].rearrange("p (s c) -> p s c", c=64), in_=src), [rRP], [rTB])
                S.op("pool", lambda e, hf=hf: e.tensor_tensor(
                    out=TB[:, hf, :].rearrange("p (s c) -> p s c", c=64), in0=TB[:, hf, :].rearrange("p (s c) -> p s c", c=64),
                    in1=CMT[:, 0:64].unsqueeze(1).to_broadcast([128, 24, 64]), op=ALU.add), [rTB, rConst], [rTB])
            if STG <= 2.2:
                return
            S.dma("sp", lambda e, hc=hc: e.dma_start(out=CS, in_=cnk[e_][:, hc * 128:(hc + 1) * 128].rearrange("(t p) f -> p t f", p=128)), [], [rCS])
            S.dma("pool", lambda e, hc=hc: e.dma_start(out=VTC, in_=cnv[e_][:, hc * 128:(hc + 1) * 128].rearrange("(t p) f -> p t f", p=128)), [], [rVTC])
            for tt in range(2):
                S.op("pe", lambda e, tt=tt: e.transpose(PS[PJ][:, tt * 128:(tt + 1) * 128], CS[:, tt, :], ident[:]), [rCS, rConst], [rPS[PJ]])
            S.op("act", lambda e: e.activation(out=KTC, in_=PS[PJ][:, 0:256], func=AF.Copy), [rPS[PJ]], [rKTC])
            if STG <= 2.4:
                return
            for t in range(5):
                dst = KT[:, 0:512] if t == 0 else KT[:, 512 + 256 + 512 * (t - 1):512 + 256 + 512 * t]
                if t % 2 == 0:
                    proj_fm(WK, rWA, lambda kc, t=t: (H2[:, kc, t * 512:(t + 1) * 512], [rH[t]]), 512,
                            lambda P, bk, dst=dst: S.op("act", lambda e: e.activation(out=dst, in_=P[:, 0:512], func=AF.Copy), [rPS[bk]], [rKT]), bank=PJ)
                else:
                    proj_fm(WK, rWA, lambda kc, t=t: (H2[:, kc, t * 512:(t + 1) * 512], [rH[t]]), 512,
                            lambda P, bk, dst=dst: S.op("dve", lambda e: e.tensor_copy(out=dst, in_=P[:, 0:512]), [rPS[bk]], [rKT]), bank=2)
            proj_fm(WK, rWA, lambda kc: (HH[:, kc, :], [rHH]), 512,
                    lambda P: (S.op("act", lambda e: e.activation(out=KT[:, 512:768], in_=P[:, 0:256], func=AF.Copy), [rPS[PJ]], [rKT]),
                               S.op("act", lambda e: e.activation(out=KT[:, 512 + 2304:512 + 2560], in_=P[:, 256:512], func=AF.Copy), [rPS[PJ]], [rKT])))
            if STG <= 2.6:
                return
            def vsrc(ti):
                if ti < 4:
                    return lambda kc: H2[:, kc, ti * 128:(ti + 1) * 128], rH[0]
                j = ti - 4
                if j < 2:
                    return lambda kc: HH[:, kc, j * 128:(j + 1) * 128], rHH
                if j >= 18:
                    return lambda kc: HH[:, kc, 256 + (j - 18) * 128:256 + (j - 17) * 128], rHH
                tk = 512 + (j - 2) * 128
                return lambda kc: H2[:, kc, tk:tk + 128], rH[tk // 512]
            for grp in range(6):
                pb2 = sbank.next()
                for q in range(4):
                    ti = grp * 4 + q
                    fn, rr = vsrc(ti)
                    for kc in range(8):
                        S.op("pe", lambda e, kc=kc, q=q, fn=fn, pb2=pb2: e.matmul(PS[pb2][:, q * 128:(q + 1) * 128], fn(kc), WV[:, kc, :],
                                                                              start=(kc == 0), stop=(kc == 7)), [rWA, rr], [rPS[pb2]])
                if grp == 0:
                    S.op("act", lambda e, pb2=pb2: e.activation(out=VS, in_=PS[pb2][:].rearrange("p (t f) -> p t f", f=128), func=AF.Copy), [rPS[pb2]], [rVS])
                    S.op("dve", lambda e: e.tensor_copy(out=VT[:, 0:4, :], in_=VS), [rVS], [rVT])
                else:
                    S.op("dve", lambda e, grp=grp, pb2=pb2: e.tensor_copy(out=VT[:, grp * 4:(grp + 1) * 4, :], in_=PS[pb2][:].rearrange("p (t f) -> p t f", f=128)),
                         [rPS[pb2]], [rVT])
                if grp == 0:
                    for bb in range(2):
                        if os.environ.get("NOOUTDMA"):
                            continue
                        S.dma("sp", lambda e, bb=bb, hc=hc: e.dma_start(
                            out=o_nbv[bb, e_][:, hc * 128:(hc + 1) * 128].rearrange("(t p) f -> p t f", p=128), in_=VS[:, bb * 2:bb * 2 + 2, :]), [rVS], [])
            if STG <= 2.8:
                return
            pb2 = sbank.next()
            for q in range(4):
                for kc in range(8):
                    S.op("pe", lambda e, kc=kc, q=q, pb2=pb2: e.matmul(PS[pb2][:, q * 128:(q + 1) * 128], H2[:, kc, q * 128:(q + 1) * 128], WK[:, kc, :],
                                                                   start=(kc == 0), stop=(kc == 7)), [rWA, rH[0]], [rPS[pb2]])
            S.op("act", lambda e, pb2=pb2: e.activation(out=KS, in_=PS[pb2][:].rearrange("p (t f) -> p t f", f=128), func=AF.Copy), [rPS[pb2]], [rKS])
            for bb in range(2):
                S.dma("sp", lambda e, bb=bb, hc=hc: e.dma_start(
                    out=o_nbk[bb, e_][:, hc * 128:(hc + 1) * 128].rearrange("(t p) f -> p t f", p=128), in_=KS[:, bb * 2:bb * 2 + 2, :]), [rKS], [])
            if STG <= 3:
                return
            QTb = [QT, ARB[:, 50688:51200]]
            sb_e = Rot([0, 1])
            OPE = 2

            def qproj_e(t):
                qd = QTb[t % 2]
                proj_fm(WQ, rWA, lambda kc: (H2[:, kc, t * 512:(t + 1) * 512], [rH[t]]), 512,
                        lambda P: S.op("act", lambda e: e.activation(out=qd, in_=P[:, 0:512], func=AF.Identity, scale=0.125), [rPS[PJ]], [rQTb[t % 2]]))

            def outproj_e(t, c):
                v = vsel(t)
                S.op("pe", lambda e: e.matmul(PS[OPE][:], WO2[:, c * 128:(c + 1) * 128], MB[:, t * 512:(t + 1) * 512], start=True, stop=True),
                     [rWA, rMBt[t]], [rPS[OPE]])
                S.op("dve", lambda e: e.scalar_tensor_tensor(
                    out=X[:, c, t * 512:(t + 1) * 512], in0=PS[OPE][:], scalar=HG[:, gate_s, c, v:v + 1], in1=X[:, c, t * 512:(t + 1) * 512],
                    op0=ALU.mult, op1=ALU.add), [rPS[OPE], rGS, rX[t][c]], [rX[t][c]])

            qproj_e(0)
            for t in range(5):
                if STG <= 4 and t >= 1:
                    return
                nsteps = 8 if t == 0 else 20
                acts = {}
                if t + 1 < 5:
                    acts.setdefault(0, []).append(lambda t=t: qproj_e(t + 1))
                if t >= 1:
                    for c in range(8):
                        acts.setdefault(2 + 2 * c, []).append(lambda t=t, c=c: outproj_e(t - 1, c))
                cnt = [0]

                def hook():
                    k = cnt[0]
                    cnt[0] += 1
                    for fn_ in acts.get(k, []):
                        fn_()

                QTt = QTb[t % 2]
                rQt = rQTb[t % 2]
                for hf in range(2):
                    hs = slice(hf * 64, (hf + 1) * 64)
                    if t == 0:
                        for bb in range(2):
                            kts = [dict(kT=KT[hs, bb * 256 + i * 128:bb * 256 + (i + 1) * 128], v=VT[:, bb * 2 + i, hs], nk=128, reads=[rKT, rVT])
                                   for i in range(2)]
                            attention(QTt[hs, bb * 256:(bb + 1) * 256], hf, 256, kts, MB[hs, bb * 256:(bb + 1) * 256], rQt, rMBt[t], PT, rPT, ptrot,
                                      hook=hook, sb=sb_e, pre_n=1)
                    else:
                        b = t - 1
                        kts = []
                        for kt in range(8):
                            ko = 512 + (8 * b + 2 * kt) * 64
                            bias = [(identb[:], TB[:, hf, (15 - 2 * kt) * 64:(15 - 2 * kt) * 64 + 512], [rTB, rConst]),
                                    (SEL[:, kt * 128:(kt + 1) * 128], RMB[:, b * 512:(b + 1) * 512], [rConst])]
                            kts.append(dict(kT=KT[hs, ko:ko + 128], v=VT[:, 4 + 4 * b + kt, hs], nk=128, reads=[rKT, rVT], bias=bias))
                        for i in range(2):
                            kts.append(dict(kT=KTC[hs, i * 128:(i + 1) * 128], v=VTC[:, i, hs], nk=128, reads=[rKTC, rVTC]))
                        attention(QTt[hs, :], hf, 512, kts, MB[hs, t * 512:(t + 1) * 512], rQt, rMBt[t], PT, rPT, ptrot,
                                  hook=hook, sb=sb_e, pre_n=1)
                assert cnt[0] == nsteps, (cnt[0], nsteps)
            for c in range(8):
                outproj_e(4, c)
        pend2 = S.fence_ops(region)
        del region[:]
        for r in flatG + g_rW:
            r.pend = list(pend2)


    c_ropeA = din("c_ropeA", [128, 64])
    c_ropeB = din("c_ropeB", [128, 128])
    ROPA = sb("ROPA", [128, 64], F32)
    ROPB = sb("ROPB", [128, 128], F32)
    GQK = sb("GQK", [128, 4], F32)
    GQS = sb("GQS", [128, 2], F32)
    GKR = sb("GKR", [128, 2, 64], F32)
    S.dma("sp", lambda e: e.dma_start(out=ROPA[:], in_=c_ropeA), [], [rConst])
    S.dma("sp", lambda e: e.dma_start(out=ROPB[:], in_=c_ropeB), [], [rConst])
    for o_ in range(2):
        for hf_ in range(2):
            S.dma("sp", lambda e, o_=o_, hf_=hf_: e.dma_start(out=GQK[hf_ * 64:(hf_ + 1) * 64, 2 * o_:2 * o_ + 1],
                                                         in_=g_qnorm[o_].rearrange("(d one) -> d one", one=1),
                                                         allow_slow_non_contiguous=True), [], [rConst])
            S.dma("sp", lambda e, o_=o_, hf_=hf_: e.dma_start(out=GQK[hf_ * 64:(hf_ + 1) * 64, 2 * o_ + 1:2 * o_ + 2],
                                                         in_=g_knorm[o_].rearrange("(d one) -> d one", one=1),
                                                         allow_slow_non_contiguous=True), [], [rConst])
        S.dma("sp", lambda e, o_=o_: e.dma_start(out=GKR[:, o_, :], in_=g_knorm[o_:o_ + 1, :].to_broadcast([128, 64])), [], [rConst])
    for o_ in range(2):
        S.op("dve", lambda e, o_=o_: e.tensor_scalar(out=GQS[:, o_:o_ + 1], in0=GQK[:, 2 * o_:2 * o_ + 1], scalar1=0.125, scalar2=None, op0=ALU.mult),
             [rConst], [rConst])

    def odd_mixer(l, nocc):
        o_ = l // 2
        H2 = ARB[:, 0:20480].rearrange("p (c n) -> p c n", c=8)
        rH = g_rH
        for t in range(2):
            norm_mod(1, t, H2[:, :, t * 512:(t + 1) * 512], rH[t])
        pend0 = S.fence_ops(flatG + g_rW)
        del region[:]
        gate_s = 1
        wq = w_qkv_c[o_]
        a = M0
        KTP = ARB[:, a:a + 1024].rearrange("p (c n) -> p c n", c=2); a += 1024
        VTP = ARB[:, a:a + 1024].rearrange("p (t f) -> p t f", f=256); a += 1024
        SQb = ARB[:, a:a + 512]; a += 512
        KNb = ARB[:, a:a + 512]; a += 512
        a_common = a
        WK2 = ARB[:, a:a + 2048].rearrange("p (k n) -> p k n", k=8); a += 2048
        WV2 = ARB[:, a:a + 2048].rearrange("p (k n) -> p k n", k=8); a += 2048
        KTS = ARB[:, a:a + 1024].rearrange("p (b n) -> p b n", b=2); a += 1024
        VST = ARB[:, a:a + 1024].rearrange("p (b t f) -> p b t f", b=2, t=2); a += 1024
        RT1 = ARF[:, 0:512]
        RT2 = ARF[:, 512:1024]
        KS = ARF[:, 1024:1280]
        VS = ARF[:, 1280:1792].rearrange("p (t f) -> p t f", f=256)
        SS4 = ARF[:, 1792:1796]
        CS = ARF[:, 1800:2056].rearrange("p (t f) -> p t f", f=128)
        rKTP = newres("KTP", pend0); rVTP = newres("VTP", pend0); rSQb = newres("SQb", pend0); rKNb = newres("KNb", pend0)
        rWA = newres("WA", pend0); rKTS = [newres("KTS%d" % i, pend0) for i in range(2)]
        rVST = [newres("VST%d" % i, pend0) for i in range(2)]
        rRT1 = newres("RT1", pend0); rRT2 = newres("RT2", pend0); rKS = newres("KS", pend0); rVS = newres("VS", pend0)
        rSS4 = newres("SS4", pend0); rCS = newres("CS", pend0)

        def head_norm_rope(P, gcol, t, dst, rdst):
            S.op("act", lambda e: e.activation(out=SQb, in_=P, func=AF.Square), [rPS[PJ]], [rSQb])
            pss = sbank.next()
            S.op("pe", lambda e, pss=pss: e.matmul(PS[pss][:], bdb[:], SQb, start=True, stop=True), [rSQb, rConst], [rPS[pss]])
            rstd_from_ps(pss, 512, 1.0 / 64, [])
            if t == 0:
                S.op("dve", lambda e: e.scalar_tensor_tensor(out=dst, in0=P, scalar=gcol, in1=RSTD[:], op0=ALU.mult, op1=ALU.mult),
                     [rPS[PJ], rRSTD, rConst], [rdst])
                return
            S.op("dve", lambda e: e.scalar_tensor_tensor(out=KNb, in0=P, scalar=gcol, in1=RSTD[:], op0=ALU.mult, op1=ALU.mult),
                 [rPS[PJ], rRSTD, rConst], [rKNb])
            pr = sbank.next()
            S.op("pe", lambda e, pr=pr: e.matmul(PS[pr][:], rotb[:], KNb, start=True, stop=True), [rKNb, rConst], [rPS[pr]])
            r0 = 8 * (t - 1)
            v3 = lambda ap: ap.rearrange("p (r c) -> p r c", c=64)
            CAb = ROPA[:, r0:r0 + 8].unsqueeze(2).to_broadcast([128, 8, 64])
            SAb = ROPA[:, 32 + r0:32 + r0 + 8].unsqueeze(2).to_broadcast([128, 8, 64])
            CBb = ROPB[:, 0:64].unsqueeze(1).to_broadcast([128, 8, 64])
            SBb = ROPB[:, 64:128].unsqueeze(1).to_broadcast([128, 8, 64])
            S.op("dve", lambda e: e.tensor_tensor(out=v3(RT1), in0=v3(KNb), in1=CAb, op=ALU.mult), [rKNb, rConst], [rRT1])
            S.op("pool", lambda e: e.tensor_tensor(out=v3(RT1), in0=v3(RT1), in1=CBb, op=ALU.mult), [rRT1, rConst], [rRT1])
            S.op("dve", lambda e, pr=pr: e.tensor_tensor(out=v3(RT2), in0=v3(PS[pr][:]), in1=SAb, op=ALU.mult), [rPS[pr], rConst], [rRT2])
            S.op("pool", lambda e: e.tensor_tensor(out=v3(RT2), in0=v3(RT2), in1=SBb, op=ALU.mult), [rRT2, rConst], [rRT2])
            S.op("pool", lambda e: e.tensor_tensor(out=dst, in0=RT1, in1=RT2, op=ALU.add), [rRT1, rRT2], [rdst])

        S.dma("pool", lambda e: e.dma_start(out=WK2, in_=wq[:, 1024:1280].rearrange("(k p) n -> p k n", p=128)), [], [rWA])
        S.dma("pool", lambda e: e.dma_start(out=WV2, in_=wq[:, 1280:1536].rearrange("(k p) n -> p k n", p=128)), [], [rWA])
        gk = GQK[:, 2 * o_ + 1:2 * o_ + 2]
        nk_ = 0
        for t in range(5):
            for kc2 in range(2):
                for kc in range(8):
                    S.op("pe", lambda e, kc=kc, kc2=kc2, t=t: e.matmul(PS[PJ][:], WK2[:, kc, kc2 * 128:(kc2 + 1) * 128], H2[:, kc, t * 512:(t + 1) * 512],
                                                                    start=(kc == 0), stop=(kc == 7)), [rWA, rH[t]], [rPS[PJ]])
                if t == 0:
                    head_norm_rope(PS[PJ][:], gk, 0, KTP[:, kc2, :], rKTP)
                else:
                    bi = nk_ % 2
                    nk_ += 1
                    head_norm_rope(PS[PJ][:], gk, t, KTS[:, bi, :], rKTS[bi])
                    S.dma("sp", lambda e, bi=bi, kc2=kc2, t=t: e.dma_start(
                        out=ccKi.ap()[kc2 * 128:(kc2 + 1) * 128, (t - 1) * 512:t * 512], in_=KTS[:, bi, :]), [rKTS[bi], rCCo], [rCCi])
            if t + 2 < 5:
                norm_mod(1, t + 2, H2[:, :, (t + 2) * 512:(t + 3) * 512], rH[t + 2])
        nv_ = 0
        for pr2 in range(10):
            pb2 = sbank.next()
            for q in range(2):
                ti = pr2 * 2 + q
                for kc in range(8):
                    S.op("pe", lambda e, kc=kc, q=q, ti=ti, pb2=pb2: e.matmul(PS[pb2][:, q * 256:(q + 1) * 256], H2[:, kc, ti * 128:(ti + 1) * 128], WV2[:, kc, :],
                                                                          start=(kc == 0), stop=(kc == 7)), [rWA, rH[ti // 4]], [rPS[pb2]])
            if pr2 < 2:
                S.op("act", lambda e, pb2=pb2: e.activation(out=VS, in_=PS[pb2][:].rearrange("p (t f) -> p t f", f=256), func=AF.Copy), [rPS[pb2]], [rVS])
                S.op("dve", lambda e, pr2=pr2: e.tensor_copy(out=VTP[:, pr2 * 2:pr2 * 2 + 2, :], in_=VS), [rVS], [rVTP])
                S.dma("sp", lambda e, pr2=pr2: e.dma_start(out=o_atv[pr2, o_].rearrange("(t p) f -> p t f", p=128), in_=VS), [rVS], [])
            else:
                bi = nv_ % 2
                nv_ += 1
                S.op("dve", lambda e, pb2=pb2, bi=bi: e.tensor_copy(out=VST[:, bi], in_=PS[pb2][:].rearrange("p (t f) -> p t f", f=256)), [rPS[pb2]], [rVST[bi]])
                S.dma("sp", lambda e, bi=bi, pr2=pr2: e.dma_start(
                    out=ccVi.ap()[(pr2 - 2) * 256:(pr2 - 1) * 256, :].rearrange("(t p) f -> p t f", p=128), in_=VST[:, bi]), [rVST[bi], rCCo], [rCCi])
        for ti in range(4):
            pb2 = sbank.next()
            for kc in range(8):
                S.op("pe", lambda e, kc=kc, ti=ti, pb2=pb2: e.matmul(PS[pb2][:, 0:256], H2[:, kc, ti * 128:(ti + 1) * 128], WK2[:, kc, :],
                                                                 start=(kc == 0), stop=(kc == 7)), [rWA, rH[0]], [rPS[pb2]])
            S.op("act", lambda e, pb2=pb2: e.activation(out=KS, in_=PS[pb2][:, 0:256], func=AF.Square), [rPS[pb2]], [rKS])
            S.op("dve", lambda e: e.tensor_reduce(out=SS4, in_=KS.rearrange("p (h d) -> p h d", d=64), axis=AX.X, op=ALU.add), [rKS], [rSS4])
            S.op("act", lambda e: e.activation(out=SS4, in_=SS4, func=AF.Sqrt, bias=epsT[:], scale=1.0 / 64), [rSS4, rConst], [rSS4])
            S.op("dve", lambda e: e.reciprocal(out=SS4, in_=SS4), [rSS4], [rSS4])
            S.op("dve", lambda e, pb2=pb2: e.tensor_tensor(out=KS.rearrange("p (h d) -> p h d", d=64), in0=PS[pb2][:, 0:256].rearrange("p (h d) -> p h d", d=64),
                                                  in1=SS4.unsqueeze(2).to_broadcast([128, 4, 64]), op=ALU.mult), [rPS[pb2], rSS4, rKS], [rKS])
            S.op("dve", lambda e: e.tensor_tensor(out=KS.rearrange("p (h d) -> p h d", d=64), in0=KS.rearrange("p (h d) -> p h d", d=64),
                                                  in1=GKR[:, o_, :].unsqueeze(1).to_broadcast([128, 4, 64]), op=ALU.mult), [rKS, rConst], [rKS])
            S.dma("sp", lambda e, ti=ti: e.dma_start(out=o_atk[ti // 2, o_][(ti % 2) * 128:(ti % 2 + 1) * 128, :], in_=KS), [rKS], [])
        cc_allgather(ccKi, ccKo, nocc)
        cc_allgather(ccVi, ccVo, nocc)

        keepers = [rKTP, rVTP, rSQb, rKNb, rRT1, rRT2, rCS]
        pend1 = S.fence_ops([r for r in region if r not in keepers] + [rSQ])
        del region[:]
        region.extend(keepers)
        a = a_common
        KTF = ARB[:, a:a + 4352]; a += 4352
        VTF = ARB[:, a:a + 6528].rearrange("p (t f) -> p t f", f=192); a += 6528
        VPA = ARB[:, a:a + 1536].rearrange("p (t k f) -> p t k f", t=4, k=2); a += 1536
        WQj = ARB[:, a:a + 2048].rearrange("p (b k n) -> p b k n", b=2, k=8); a += 2048
        WOj = ARB[:, a:a + 2048].rearrange("p (b n) -> p b n", b=2); a += 2048
        QZ = ARB[:, a:a + 2048].rearrange("p (u h n) -> p u h n", u=2, h=2); a += 2048
        MJ = ARB[:, a:a + 1024].rearrange("p (u n) -> p u n", u=2); a += 1024
        PT = ARB[:, a:a + 1536].rearrange("p (b n) -> p b n", b=3); a += 1536
        SQ2 = SQb
        KN2 = KNb
        QRAW = ARF[:, 2560:3072]
        assert a <= 47104, a
        rKTF = newres("KTF", pend1); rVTF = newres("VTF", pend1); rVPA = newres("VPA", pend1)
        rWj = [newres("Wj%d" % i, pend1) for i in range(2)]
        rQZ = [newres("QZ%d" % i, pend1) for i in range(2)]
        rMJ = [newres("MJ%d" % i, pend1) for i in range(2)]
        rPT = [newres("PT%d" % i, pend1) for i in range(3)]
        rSQ2 = rSQb; rKN2 = rKNb
        rQRAW = newres("QRAW", pend1)
        ptrot = Rot([0, 1, 2])
        gqs = GQS[:, o_:o_ + 1]
        BD_B, ROT_B, OPJ, SWP_B, KW_B = 5, 6, 6, 5, 7
        KEEPWARM = True
        KWN = 128
        MSE = "dve" if os.environ.get("NOPOOLMS") else "pool"
        for u2 in range(2):
            S.op(MSE, lambda e, u2=u2: e.memset(QZ[64:128, u2, 0, :], 0.0), [], [rQZ[u2]])
            S.op(MSE, lambda e, u2=u2: e.memset(QZ[0:64, u2, 1, :], 0.0), [], [rQZ[u2]])
        S.op(MSE, lambda e: e.memset(VTF[:, :, 64:128], 1.0), [], [rVTF])
        S.op(MSE, lambda e: e.memset(VPA[:, :, :, 64:128], 1.0), [], [rVPA])
        for kc2 in range(2):
            S.op("dve", lambda e, kc2=kc2: e.tensor_copy(out=VPA[:, :, kc2, 0:64], in_=VTP[:, :, (2 * kc2) * 64:(2 * kc2 + 1) * 64]), [rVTP], [rVPA])
            S.op("dve", lambda e, kc2=kc2: e.tensor_copy(out=VPA[:, :, kc2, 128:192], in_=VTP[:, :, (2 * kc2 + 1) * 64:(2 * kc2 + 2) * 64]), [rVTP], [rVPA])
        S.op("dve", lambda e: e.memset(TMPF[:, 0, :], 0.0), [], [rTMPF[0]])

        units = [(kc2, j, t) for kc2 in range(2) for j in range(4) for t in range(5)]
        OSTG = float(os.environ.get("ODDSTG", "99"))
        if OSTG <= 0:
            return

        def load_kv(kc2):
            S.dma("sp", lambda e: e.dma_start(out=CS, in_=cak[o_][:, kc2 * 128:(kc2 + 1) * 128].rearrange("(t p) f -> p t f", p=128)), [], [rCS])
            for tt in range(2):
                S.op("pe", lambda e, tt=tt: e.transpose(PS[PJ][:, tt * 128:(tt + 1) * 128], CS[:, tt, :], ident[:]), [rCS, rConst], [rPS[PJ]])
            S.op("act", lambda e: e.activation(out=KTF[:, 0:256], in_=PS[PJ][:, 0:256], func=AF.Copy), [rPS[PJ]], [rKTF])
            for rk in range(2):
                S.dma("sp", lambda e, rk=rk: e.dma_start(out=KTF[:, 256 + rk * 2048:256 + (rk + 1) * 2048],
                                                     in_=ccKo.ap()[rk * 256 + kc2 * 128:rk * 256 + (kc2 + 1) * 128, :]), [rCCo], [rKTF])
            for g2 in range(2):
                c0 = kc2 * 128 + g2 * 64
                S.dma("pool", lambda e, g2=g2, c0=c0: e.dma_start(out=VTF[:, 0:2, g2 * 128:g2 * 128 + 64],
                                                              in_=cav[o_][:, c0:c0 + 64].rearrange("(t p) f -> p t f", p=128)), [], [rVTF])
                S.dma("sp", lambda e, g2=g2, c0=c0: e.dma_start(out=VTF[:, 2:34, g2 * 128:g2 * 128 + 64],
                                                            in_=ccVo.ap()[:, c0:c0 + 64].rearrange("(t p) f -> p t f", p=128)), [rCCo], [rVTF])

        def load_w(kc2, j, bj):
            for two in range(2):
                c0 = kc2 * 512 + two * 256 + j * 64
                S.dma("pool", lambda e, two=two, c0=c0: e.dma_start(
                    out=WQj[:, bj, :, two * 64:(two + 1) * 64], in_=wq[:, c0:c0 + 64].rearrange("(k p) n -> p k n", p=128)), [], [rWj[bj]])
                r0_ = (8 * kc2 + 4 * two + j) * 64
                S.dma("pool", lambda e, two=two, r0_=r0_: e.dma_start(
                    out=WOj[two * 64:(two + 1) * 64, bj, :], in_=w_out_c[o_][r0_:r0_ + 64, :]), [], [rWj[bj]])

        def wbuf(ui):
            kc2, j, t = units[ui]
            return (kc2 * 4 + j) % 2

        def stage0(ui):
            kc2, j, t = units[ui]
            bj = wbuf(ui)
            if t == 0:
                if j == 0:
                    load_kv(kc2)
                load_w(kc2, j, bj)
            for kc in range(8):
                S.op("pe", lambda e, kc=kc: e.matmul(PS[PJ][:], WQj[:, bj, kc, :], H2[:, kc, t * 512:(t + 1) * 512],
                                                 start=(kc == 0), stop=(kc == 7)), [rWj[bj], rH[t]], [rPS[PJ]])
            S.op("dve", lambda e: e.tensor_copy(out=QRAW, in_=PS[PJ][:]), [rPS[PJ]], [rQRAW])
            S.op("pool", lambda e: e.tensor_tensor(out=SQ2, in0=QRAW, in1=QRAW, op=ALU.mult), [rQRAW], [rSQ2])

        def stage1(ui):
            kc2, j, t = units[ui]
            u2 = ui % 2
            S.op("pe", lambda e: e.matmul(PS[BD_B][:], bdb[:], SQ2, start=True, stop=True), [rSQ2, rConst], [rPS[BD_B]])
            S.op("act", lambda e: e.activation(out=RSTD[:], in_=PS[BD_B][:], func=AF.Ln, bias=epsT[:], scale=1.0 / 64),
                 [rPS[BD_B], rConst], [rRSTD])
            S.op("act", lambda e: e.activation(out=RSTD[:], in_=RSTD[:], func=AF.Exp, scale=-0.5), [rRSTD], [rRSTD])
            if t == 0:
                for hf in range(2):
                    hs = slice(hf * 64, (hf + 1) * 64)
                    S.op("dve", lambda e, hs=hs, hf=hf: e.scalar_tensor_tensor(out=QZ[hs, u2, hf, :], in0=QRAW[hs, :], scalar=gqs[hs, :], in1=RSTD[hs, :],
                                                                        op0=ALU.mult, op1=ALU.mult), [rQRAW, rRSTD, rConst], [rQZ[u2]])
            else:
                S.op("dve", lambda e: e.scalar_tensor_tensor(out=KN2, in0=QRAW, scalar=gqs, in1=RSTD[:], op0=ALU.mult, op1=ALU.mult),
                     [rQRAW, rRSTD, rConst], [rKN2])

        def stage2(ui):
            kc2, j, t = units[ui]
            u2 = ui % 2
            if t == 0:
                return
            S.op("pe", lambda e: e.matmul(PS[ROT_B][:], rotb[:], KN2, start=True, stop=True), [rKN2, rConst], [rPS[ROT_B]])
            r0 = 8 * (t - 1)
            v3 = lambda ap: ap.rearrange("p (r c) -> p r c", c=64)
            CAb = ROPA[:, r0:r0 + 8].unsqueeze(2).to_broadcast([128, 8, 64])
            SAb = ROPA[:, 32 + r0:32 + r0 + 8].unsqueeze(2).to_broadcast([128, 8, 64])
            CBb = ROPB[:, 0:64].unsqueeze(1).to_broadcast([128, 8, 64])
            SBb = ROPB[:, 64:128].unsqueeze(1).to_broadcast([128, 8, 64])
            S.op("dve", lambda e: e.tensor_tensor(out=v3(RT1), in0=v3(KN2), in1=CAb, op=ALU.mult), [rKN2, rConst], [rRT1])
            S.op("pool", lambda e: e.tensor_tensor(out=v3(RT1), in0=v3(RT1), in1=CBb, op=ALU.mult), [rRT1, rConst], [rRT1])
            S.op("dve", lambda e: e.tensor_tensor(out=v3(RT2), in0=v3(PS[ROT_B][:]), in1=SAb, op=ALU.mult), [rPS[ROT_B], rConst], [rRT2])
            S.op("pool", lambda e: e.tensor_tensor(out=v3(RT2), in0=v3(RT2), in1=SBb, op=ALU.mult), [rRT2, rConst], [rRT2])
            for hf in range(2):
                hs = slice(hf * 64, (hf + 1) * 64)
                S.op("pool", lambda e, hs=hs, hf=hf: e.tensor_tensor(out=QZ[hs, u2, hf, :], in0=RT1[hs, :], in1=RT2[hs, :], op=ALU.add),
                     [rRT1, rRT2], [rQZ[u2]])

        def outproj_c(ui, c):
            kc2, j, t = units[ui]
            u2 = ui % 2
            bj = wbuf(ui)
            v = vsel(t)
            S.op("pe", lambda e: e.matmul(PS[OPJ][:], WOj[:, bj, c * 128:(c + 1) * 128], MJ[:, u2, :], start=True, stop=True),
                 [rWj[bj], rMJ[u2]], [rPS[OPJ]])
            S.op("dve", lambda e: e.scalar_tensor_tensor(
                out=X[:, c, t * 512:(t + 1) * 512], in0=PS[OPJ][:], scalar=HG[:, gate_s, c, v:v + 1], in1=X[:, c, t * 512:(t + 1) * 512],
                op0=ALU.mult, op1=ALU.add), [rPS[OPJ], rGS, rX[t][c]], [rX[t][c]])

        def attention_aug(qz, hf, nq, ktiles, out_ap, rQ_, rOut, hook):
            po = obank.next()
            hs = slice(hf * 64, (hf + 1) * 64)
            ss = slice((1 - hf) * 64, (2 - hf) * 64)
            n = len(ktiles)
            banks = {}

            def s_mm(i):
                kt = ktiles[i]
                pss = sbank.next()
                banks[i] = pss
                S.op("pe", lambda e, pss=pss, kt=kt: e.matmul(PS[pss][:, 0:nq], kt["kT"], qz, start=True, stop=True),
                     [rQ_] + kt["reads"], [rPS[pss]])

            pre = min(2, n)
            for i in range(pre):
                s_mm(i)
            for i, kt in enumerate(ktiles):
                pss = banks[i]
                pt = ptrot.next()
                S.op("act", lambda e, pss=pss, pt=pt: e.activation(out=PT[:, pt, 0:nq], in_=PS[pss][:, 0:nq], func=AF.Exp), [rPS[pss]], [rPT[pt]])
                if i + pre < n:
                    s_mm(i + pre)
                S.op("pe", lambda e, pt=pt, kt=kt, i=i: e.matmul(PS[po][:, 0:nq], kt["v"], PT[:, pt, 0:nq], start=(i == 0), stop=(i == n - 1)),
                     [rPT[pt]] + kt["reads"], [rPS[po]])
                if nq == 512 and KEEPWARM:
                    S.op("pe", lambda e, pt=pt: e.matmul(PS[KW_B][:, 0:KWN], identb[:], PT[:, pt, 0:KWN], start=True, stop=True),
                         [rPT[pt], rConst], [rPS[KW_B]])
                hook()
            S.op("dve", lambda e: e.reciprocal(out=TMPF[ss, 0, 0:nq], in_=PS[po][ss, 0:nq]), [rPS[po]], [rTMPF[0]])

            def fin():
                psw = SWP_B
                S.op("pe", lambda e: e.matmul(PS[psw][:, 0:nq], swpf[:], TMPF[:, 0, 0:nq], start=True, stop=True), [rTMPF[0], rConst], [rPS[psw]])
                S.op("act", lambda e: e.activation(out=TMPF[hs, 1, 0:nq], in_=PS[psw][hs, 0:nq], func=AF.Copy), [rPS[psw]], [rTMPF[1]])
                S.op("dve", lambda e: e.tensor_tensor(out=out_ap, in0=PS[po][hs, 0:nq], in1=TMPF[hs, 1, 0:nq], op=ALU.mult),
                     [rPS[po], rTMPF[1]], [rOut])
            return fin

        pending_fin = []

        def needs_kv(ui):
            return ui < len(units) and units[ui][1] == 0 and units[ui][2] == 0

        for ui, (kc2, j, t) in enumerate(units):
            u2 = ui % 2
            if needs_kv(ui):
                stage0(ui)
                if OSTG <= 0.3:
                    return
                stage1(ui)
                if OSTG <= 0.6:
                    return
                stage2(ui)
            if OSTG <= 1:
                return
            if OSTG < 50 and ui >= OSTG - 1:
                return
            nsteps = 8 if t == 0 else 68
            acts = {}
            nxt = ui + 1 < len(units) and not needs_kv(ui + 1)
            if nxt:
                acts.setdefault(0, []).append(lambda: stage0(ui + 1))
                acts.setdefault(16 if t > 0 else nsteps // 3, []).append(lambda: stage1(ui + 1))
                acts.setdefault(28 if t > 0 else (2 * nsteps) // 3, []).append(lambda: stage2(ui + 1))
            if ui >= 1:
                for c in range(8):
                    st_ = min(3 + c, 7) if t == 0 else 36 + 2 * c
                    acts.setdefault(st_, []).append(lambda c=c: outproj_c(ui - 1, c))
            cnt = [0]
            since = [99]

            def hook():
                k = cnt[0]
                cnt[0] += 1
                since[0] += 1
                if pending_fin and since[0] >= 9:
                    pending_fin.pop(0)()
                for fn_ in acts.get(k, []):
                    fn_()

            if t == 0:
                for bb in range(2):
                    for hf in range(2):
                        hs = slice(hf * 64, (hf + 1) * 64)
                        kts = [dict(kT=KTP[:, kc2, bb * 256 + i * 128:bb * 256 + (i + 1) * 128],
                                    v=VPA[:, bb * 2 + i, kc2, hf * 64:hf * 64 + 128], nk=128, reads=[rKTP, rVPA]) for i in range(2)]
                        while len(pending_fin) > 1:
                            pending_fin.pop(0)()
                        pending_fin.append(attention_aug(QZ[:, u2, hf, bb * 256:(bb + 1) * 256], hf, 256, kts, MJ[hs, u2, bb * 256:(bb + 1) * 256], rQZ[u2], rMJ[u2], hook))
                        since[0] = 0
            else:
                for hf in range(2):
                    hs = slice(hf * 64, (hf + 1) * 64)
                    kts = [dict(kT=KTF[:, i * 128:(i + 1) * 128], v=VTF[:, i, hf * 64:hf * 64 + 128], nk=128, reads=[rKTF, rVTF]) for i in range(34)]
                    while len(pending_fin) > 1:
                        pending_fin.pop(0)()
                    pending_fin.append(attention_aug(QZ[:, u2, hf, :], hf, 512, kts, MJ[hs, u2, :], rQZ[u2], rMJ[u2], hook))
                    since[0] = 0
            assert cnt[0] == nsteps, (cnt[0], nsteps)
        while pending_fin:
            pending_fin.pop(0)()
        for c in range(8):
            outproj_c(len(units) - 1, c)
        pend2 = S.fence_ops(region)
        del region[:]
        for r in flatG + g_rW + [rSQ]:
            r.pend = list(pend2)

    dbg_n = [0]

    def dumpX(tag):
        if not debug:
            return
        o = dout("dbgX_%s" % tag, [128, NCH * T])
        allr = [r for t in range(5) for r in rX[t]]
        S.dma("sp", lambda e: e.dma_start(out=o, in_=X[:].rearrange("p c n -> p (c n)")), allr, [])

    def dumpS(tag, ap, n, reads):
        if not debug:
            return
        o = dout("dbgS_%s" % tag, [128, n])
        S.dma("sp", lambda e: e.dma_start(out=o, in_=ap), reads, [])

    dumpX("load")
    for l in range(n_layers):
        ada(l)
        if l == 0:
            dumpS("mod", MOD[:].rearrange("p m v -> p (m v)"), 144, [rMOD])
            dumpS("gs", GS[:].rearrange("p s c v -> p (s c v)"), 48, [rGS])
            dumpS("hg", HG[:].rearrange("p s c v -> p (s c v)"), 48, [rGS])
        ffn(l, 0, 0)
        if l == 0:
            dumpX("ffn0")
        if mix:
            if l % 2 == 0:
                even_mixer(l, mix == 2)
                if l == 0:
                    dumpX("mix0")
            else:
                odd_mixer(l, mix == 2)
                if l == 1:
                    dumpX("mix1")
        ffn(l, 2, 1)
    final_out()

    S.emit(stack)
    stack.close()
    return nc


def _consts(core):
    half = core % 2
    ident = np.eye(128, dtype=np.float32)
    rot = np.zeros((128, 128), np.float32)
    for m in range(128):
        if m % 32 < 16:
            rot[m + 16, m] = -1.0
        else:
            rot[m - 16, m] = 1.0
    bd = np.zeros((128, 128), np.float32)
    bd[:64, :64] = 1.0
    bd[64:, 64:] = 1.0
    t = np.arange(TS)
    row = (32 * half + t // 64).astype(np.float32)
    col = (t % 64).astype(np.float32)
    inv = (10000.0 ** (-np.arange(16, dtype=np.float32) / 16)).astype(np.float32)
    cos = np.zeros((128, TS), np.float32)
    sin = np.zeros((128, TS), np.float32)
    for p in range(128):
        d = p % 64
        pos = row if d < 32 else col
        ang = pos * inv[d % 16]
        cos[p] = np.cos(ang)
        sin[p] = np.sin(ang)
    cq = np.arange(64)
    cstart = np.clip(cq - 8, 0, 48)
    ck = np.arange(64)
    valid = (ck[None, :] >= cstart[:, None]) & (ck[None, :] < cstart[:, None] + 16)
    cm = np.where(valid.T, 0.0, NEG).astype(np.float32)
    cmt = np.tile(cm, (2, 8))
    sel = np.zeros((16, 8 * 128), np.float32)
    for kt in range(8):
        for kr in range(2):
            sel[2 * kt + kr, kt * 128 + kr * 64:kt * 128 + (kr + 1) * 64] = 1.0
    rm = np.full((16, 4 * 512), NEG, np.float32)
    for b in range(4):
        for qr in range(8):
            i = 8 * b + qr
            r = 32 * half + i
            rs = min(max(r - 4, 0), 56)
            for k in range(16):
                keyrow = 32 * half - 4 + 8 * b + k
                if rs <= keyrow <= rs + 7:
                    rm[k, b * 512 + qr * 64:b * 512 + (qr + 1) * 64] = 0.0
    invb = np.zeros((4, 6, 2, 8), np.float32)
    for g in range(4):
        w = 2 << g
        for si in range(6):
            for ed in range(2):
                for j in range(8):
                    if si < 2:
                        L = 256
                        tt = j if ed == 0 else 248 + j
                    else:
                        L = 4096
                        tt = half * 2048 + (si - 2) * 512 + (j if ed == 0 else 504 + j)
                    lo = min(max(tt - w // 2, 0), L - 1)
                    hi = min(max(tt + (w - 1 - w // 2), 0), L - 1)
                    invb[g, si, ed, j] = 1.0 / (hi - lo + 1)
    invb = np.tile(invb.reshape(1, -1), (128, 1)).astype(np.float32)
    hv = np.zeros((128, 2), np.float32)
    hv[:, 0] = 1.0 if half == 1 else 0.0
    hv[:, 1] = 1.0 if half == 0 else 0.0
    ropeA = np.ones((128, 64), np.float32)
    ropeB = np.ones((128, 128), np.float32)
    for p in range(128):
        d = p % 64
        f = inv[d % 16]
        if d < 32:
            ang = (32 * half + np.arange(32, dtype=np.float32)) * f
            ropeA[p, 0:32] = np.cos(ang)
            ropeA[p, 32:64] = np.sin(ang)
        else:
            ang = np.arange(64, dtype=np.float32) * f
            ropeB[p, 0:64] = np.cos(ang)
            ropeB[p, 64:128] = np.sin(ang)
    swp = np.zeros((128, 128), np.float32)
    for m in range(128):
        swp[(m + 64) % 128, m] = 1.0
    return dict(c_ident=ident, c_rot=rot, c_bd=bd, c_swp=swp, c_cos=cos, c_sin=sin,
                c_rm=rm, c_sel=sel, c_cmt=cmt, c_invb=invb, c_hv=hv, c_ropeA=ropeA, c_ropeB=ropeB)


_NC_CACHE = {}


def kernel(x_prompt, x_sample, cache_nb_k, cache_nb_v, cache_attn_k, cache_attn_v, c, c_ctx,
           w_mod, b_mod, g_norm, w_ffn_in, w_ffn_out, w_in_ab, w_pool, pool_scale, nb_rpb, w_out_ab,
           w_qkv_c, g_qnorm, g_knorm, w_out_c, g_final):
    f = lambda a: np.ascontiguousarray(np.asarray(a, dtype=np.float32))
    x_prompt, x_sample = f(x_prompt), f(x_sample)
    if "nc" not in _NC_CACHE:
        _NC_CACHE["nc"] = build_program(DEPTH, 1)
    nc = _NC_CACHE["nc"]
    rpbP = np.zeros((2, 8, 25, 128), np.float32)
    rpbP[:, :, 5:20, 48:79] = f(nb_rpb)[:, :, ::-1, ::-1]
    shared = dict(w_mod=f(w_mod), b_mod=f(b_mod), g_norm=f(g_norm), w_ffn_in=f(w_ffn_in), w_ffn_out=f(w_ffn_out),
                  w_in_ab=f(w_in_ab), w_pool=f(w_pool), pool_scale=f(pool_scale), rpbP=rpbP, w_out_ab=f(w_out_ab),
                  w_qkv_c=f(w_qkv_c), g_qnorm=f(g_qnorm), g_knorm=f(g_knorm), w_out_c=f(w_out_c), g_final=f(g_final))
    in_maps = []
    for core in range(8):
        seq, half = core // 2, core % 2
        xin = np.concatenate([x_prompt[2 * core].reshape(256, D), x_prompt[2 * core + 1].reshape(256, D),
                              x_sample[seq, half * TS:(half + 1) * TS]], axis=0)
        m = dict(shared)
        m.update(xin=np.ascontiguousarray(xin),
                 cvec=np.ascontiguousarray(np.stack([f(c_ctx), f(c)[seq]], 0)),
                 cnk=f(cache_nb_k)[seq].reshape(2, 256, 512), cnv=f(cache_nb_v)[seq].reshape(2, 256, 512),
                 cak=f(cache_attn_k)[seq].reshape(2, 256, 256), cav=f(cache_attn_v)[seq].reshape(2, 256, 256))
        m.update(_consts(core))
        in_maps.append(m)
    res = run_bass_kernel_spmd(nc, in_maps, core_ids=list(range(8)))
    R = res.results
    y_prompt = np.zeros((16, 256, D), np.float32)
    y_sample = np.zeros((4, 4096, D), np.float32)
    nbk = np.zeros((16, 2, 256, 8, 64), np.float32)
    nbv = np.zeros((16, 2, 256, 8, 64), np.float32)
    atk = np.zeros((16, 2, 256, 4, 64), np.float32)
    atv = np.zeros((16, 2, 256, 4, 64), np.float32)
    for core in range(8):
        seq, half = core // 2, core % 2
        yy = R[core]["y"]
        y_prompt[2 * core] = yy[0:256]
        y_prompt[2 * core + 1] = yy[256:512]
        y_sample[seq, half * TS:(half + 1) * TS] = yy[512:]
        nbk[2 * core:2 * core + 2] = R[core]["o_nbk"].reshape(2, 2, 256, 8, 64)
        nbv[2 * core:2 * core + 2] = R[core]["o_nbv"].reshape(2, 2, 256, 8, 64)
        atk[2 * core:2 * core + 2] = R[core]["o_atk"].reshape(2, 2, 256, 4, 64)
        atv[2 * core:2 * core + 2] = R[core]["o_atv"].reshape(2, 2, 256, 4, 64)
    return (y_prompt, y_sample, nbk, nbv, atk, atv)
```
